# Optimizing a Trainium2 kernel written in Bass

```python
import jax, jax.numpy as jnp
from jax import lax
import numpy as np

D_MODEL = 1024
BATCH = 2
SEQ = 8192
DEPTH = 2

CHUNK = 64
EPS = 1e-6
CONV_K = 4
A_HEADS = 8
A_HEAD_DIM = 64
A_LEFT_CHUNKS = 8
A_MAX_REL = 256
B_HEADS = 4
B_HEAD_DIM = 128
C_HEADS = 8
C_HEAD_DIM = 64
C_GROUPS = 2
C_STATE = 128
D_HEADS = 8
D_HEAD_DIM = 64
D_QBLOCK = 128
D_FF = -(-8 * D_MODEL // (3 * 256)) * 256

A_W = A_HEADS * A_HEAD_DIM
B_W = B_HEADS * B_HEAD_DIM
C_W = C_HEADS * C_HEAD_DIM
D_W = D_HEADS * D_HEAD_DIM
C_BC = C_GROUPS * C_STATE
C_CONV_CH = C_W + 2 * C_BC
EVEN_SPLIT = (A_W, A_W, A_W, 3 * B_W, B_HEADS, B_HEADS, B_W)
ODD_SPLIT = (C_W, C_CONV_CH, C_HEADS, D_W, D_W, D_W, D_HEADS)
PROJ_EVEN = sum(EVEN_SPLIT)
PROJ_ODD = sum(ODD_SPLIT)
N_EVEN = (DEPTH + 1) // 2
N_ODD = DEPTH // 2

kernel_name = "hybrid_chunk_causal_encoder_trunk"


def rms_norm(x, w):
    xf = x.astype(jnp.float32)
    y = xf * lax.rsqrt(jnp.mean(xf * xf, axis=-1, keepdims=True) + EPS)
    return (y * w.astype(jnp.float32)).astype(x.dtype)


def l2_normalize(x):
    return x * lax.rsqrt(jnp.sum(x * x, axis=-1, keepdims=True) + EPS)


def split_cols(a, sizes):
    return jnp.split(a, np.cumsum(sizes)[:-1].tolist(), axis=-1)


def causal_depthwise_conv(x, w):
    return lax.conv_general_dilated(
        x, w[:, None, :].astype(x.dtype), window_strides=(1,),
        padding=[(CONV_K - 1, 0)], dimension_numbers=("NWC", "WIO", "NWC"),
        feature_group_count=x.shape[-1])


def swiglu(h, w_gate, w_up, w_down):
    return (jax.nn.silu(h @ w_gate) * (h @ w_up)) @ w_down


def chunk_band_attention(q, k, v, rel_bias):
    bsz, t, h, dh = q.shape
    nc = t // CHUNK
    band = A_LEFT_CHUNKS + 1
    qc = q.reshape(bsz, nc, CHUNK, h, dh)
    pad = ((0, 0), (A_LEFT_CHUNKS, 0), (0, 0), (0, 0), (0, 0))
    kp = jnp.pad(k.reshape(bsz, nc, CHUNK, h, dh), pad)
    vp = jnp.pad(v.reshape(bsz, nc, CHUNK, h, dh), pad)
    kb = jnp.concatenate([kp[:, j:j + nc] for j in range(band)], axis=2)
    vb = jnp.concatenate([vp[:, j:j + nc] for j in range(band)], axis=2)
    qi = jnp.arange(CHUNK)
    kj = jnp.arange(band * CHUNK)
    dist = qi[:, None] + A_LEFT_CHUNKS * CHUNK - kj[None, :]
    idx = jnp.clip(dist, -A_MAX_REL, A_MAX_REL) + A_MAX_REL
    bias = rel_bias.astype(jnp.float32)[:, idx]
    valid = (jnp.arange(nc)[:, None] - A_LEFT_CHUNKS + kj[None, :] // CHUNK) >= 0
    s = jnp.einsum("bnqhd,bnkhd->bnhqk", qc, kb).astype(jnp.float32) * (dh ** -0.5) + bias
    s = jnp.where(valid[None, :, None, None, :], s, -jnp.inf)
    p = jax.nn.softmax(s, axis=-1).astype(v.dtype)
    o = jnp.einsum("bnhqk,bnkhd->bnqhd", p, vb)
    return o.reshape(bsz, t, h * dh)


def gated_delta_rule(q, k, v, beta, g):
    f32 = jnp.float32
    bsz, t, h, dk = q.shape
    dv = v.shape[-1]
    nc = t // CHUNK
    q = l2_normalize(q.astype(f32)) * (dk ** -0.5)
    k = l2_normalize(k.astype(f32))
    v = v.astype(f32)

    def to_chunks(a):
        return jnp.moveaxis(a.astype(f32).reshape((bsz, nc, CHUNK) + a.shape[2:]), 3, 1)

    q, k, v, beta, g = (to_chunks(a) for a in (q, k, v, beta, g))
    gc = jnp.cumsum(g, axis=-1)
    causal = jnp.tril(jnp.ones((CHUNK, CHUNK), bool))
    strict = jnp.tril(jnp.ones((CHUNK, CHUNK), bool), -1)
    decay = jnp.exp(jnp.where(causal, gc[..., :, None] - gc[..., None, :], -jnp.inf))
    kk = jnp.einsum("bhnid,bhnjd->bhnij", k, k)
    a_strict = jnp.where(strict, beta[..., :, None] * kk * decay, 0.0)
    m = a_strict + jnp.eye(CHUNK, dtype=f32)
    rhs = jnp.concatenate([v * beta[..., None], k * (beta * jnp.exp(gc))[..., None]], axis=-1)
    sol = lax.linalg.triangular_solve(m, rhs, left_side=True, lower=True, unit_diagonal=True)
    u, w = sol[..., :dv], sol[..., dv:]
    attn = jnp.einsum("bhnid,bhnjd->bhnij", q, k) * decay
    q_dec = q * jnp.exp(gc)[..., None]
    k_st = k * jnp.exp(gc[..., -1:] - gc)[..., None]
    g_last = jnp.exp(gc[..., -1])

    def step(state, inp):
        u_c, w_c, q_c, k_c, a_c, gl = inp
        v_new = u_c - jnp.einsum("bhcd,bhde->bhce", w_c, state)
        o = jnp.einsum("bhcd,bhde->bhce", q_c, state) + jnp.einsum("bhij,bhje->bhie", a_c, v_new)
        state = state * gl[..., None, None] + jnp.einsum("bhcd,bhce->bhde", k_c, v_new)
        return state, o

    xs = tuple(jnp.moveaxis(a, 2, 0) for a in (u, w, q_dec, k_st, attn, g_last))
    s0 = jnp.zeros((bsz, h, dk, dv), f32)
    _, o = lax.scan(step, s0, xs)
    return o.transpose(1, 0, 3, 2, 4).reshape(bsz, t, h, dv)


def ssd_scan(x, dt, a, bm, cm):
    f32 = jnp.float32
    bsz, t, h, p = x.shape
    g, n = bm.shape[2], bm.shape[3]
    hg = h // g
    nc = t // CHUNK
    xc = x.astype(f32).reshape(bsz, nc, CHUNK, g, hg, p)
    dtc = dt.astype(f32).reshape(bsz, nc, CHUNK, g, hg)
    bc = bm.astype(f32).reshape(bsz, nc, CHUNK, g, n)
    cc = cm.astype(f32).reshape(bsz, nc, CHUNK, g, n)
    da_cs = jnp.cumsum(dtc * a.astype(f32).reshape(g, hg), axis=2)
    seg = da_cs[:, :, :, None] - da_cs[:, :, None, :]
    mask = jnp.tril(jnp.ones((CHUNK, CHUNK), bool))[:, :, None, None]
    lmat = jnp.exp(jnp.where(mask, seg, -jnp.inf))
    cb = jnp.einsum("bclgn,bcsgn->bclsg", cc, bc)
    wts = cb[..., None] * lmat * dtc[:, :, None]
    y_diag = jnp.einsum("bclsgh,bcsghp->bclghp", wts, xc)
    decay_states = jnp.exp(da_cs[:, :, -1:] - da_cs)
    states = jnp.einsum("bclgn,bclgh,bclghp->bcghpn", bc, decay_states * dtc, xc)
    chunk_decay = jnp.exp(da_cs[:, :, -1])

    def step(state, inp):
        st, dec = inp
        return state * dec[..., None, None] + st, state

    s0 = jnp.zeros((bsz, g, hg, p, n), f32)
    _, prev = lax.scan(step, s0, (jnp.moveaxis(states, 1, 0), jnp.moveaxis(chunk_decay, 1, 0)))
    prev = jnp.moveaxis(prev, 0, 1)
    y_off = jnp.einsum("bclgn,bcghpn,bclgh->bclghp", cc, prev, jnp.exp(da_cs))
    return (y_diag + y_off).reshape(bsz, t, h, p)


def forgetting_attention(q, k, v, log_f):
    bsz, t, h, dh = q.shape
    f_cum = jnp.cumsum(log_f.astype(jnp.float32), axis=1).transpose(0, 2, 1)
    qh, kh, vh = (a.transpose(0, 2, 1, 3) for a in (q, k, v))
    kpos = jnp.arange(t)
    scale = dh ** -0.5

    def block(i):
        start = i * D_QBLOCK
        qb = lax.dynamic_slice_in_dim(qh, start, D_QBLOCK, axis=2)
        fq = lax.dynamic_slice_in_dim(f_cum, start, D_QBLOCK, axis=2)
        s = jnp.einsum("bhqd,bhkd->bhqk", qb, kh).astype(jnp.float32) * scale
        s = s + fq[..., :, None] - f_cum[:, :, None, :]
        qpos = start + jnp.arange(D_QBLOCK)
        s = jnp.where(kpos[None, :] <= qpos[:, None], s, -jnp.inf)
        pr = jax.nn.softmax(s, axis=-1).astype(vh.dtype)
        return jnp.einsum("bhqk,bhkd->bhqd", pr, vh)

    o = lax.map(block, jnp.arange(t // D_QBLOCK))
    return o.transpose(1, 0, 3, 2, 4).reshape(bsz, t, h * dh)


def even_mixer(h, w_in, rel_bias, conv_w, a_log, dt_bias, norm_w, w_out):
    bsz, t, _ = h.shape
    proj = h @ w_in
    a_q, a_k, a_v, b_qkv, b_beta, b_a, b_z = split_cols(proj, EVEN_SPLIT)
    heads_a = lambda z: z.reshape(bsz, t, A_HEADS, A_HEAD_DIM)
    o_a = chunk_band_attention(heads_a(a_q), heads_a(a_k), heads_a(a_v), rel_bias)
    qkv = jax.nn.silu(causal_depthwise_conv(b_qkv, conv_w))
    b_q, b_k, b_v = (z.reshape(bsz, t, B_HEADS, B_HEAD_DIM) for z in split_cols(qkv, (B_W, B_W, B_W)))
    beta = jax.nn.sigmoid(b_beta.astype(jnp.float32))
    g = -jnp.exp(a_log.astype(jnp.float32)) * jax.nn.softplus(b_a.astype(jnp.float32) + dt_bias.astype(jnp.float32))
    o_b = gated_delta_rule(b_q, b_k, b_v, beta, g).astype(h.dtype)
    o_b = rms_norm(o_b, norm_w) * jax.nn.silu(b_z.reshape(bsz, t, B_HEADS, B_HEAD_DIM))
    return jnp.concatenate([o_a, o_b.reshape(bsz, t, B_W)], axis=-1) @ w_out


def odd_mixer(h, w_in, conv_w, conv_b, dt_bias, a_log, d_skip, norm_w, f_bias, w_out):
    bsz, t, _ = h.shape
    proj = h @ w_in
    c_z, c_xbc, c_dt, d_q, d_k, d_v, d_f = split_cols(proj, ODD_SPLIT)
    xbc = jax.nn.silu(causal_depthwise_conv(c_xbc, conv_w) + conv_b)
    c_x, c_b, c_c = split_cols(xbc, (C_W, C_BC, C_BC))
    c_x = c_x.reshape(bsz, t, C_HEADS, C_HEAD_DIM)
    dt = jax.nn.softplus(c_dt.astype(jnp.float32) + dt_bias.astype(jnp.float32))
    a = -jnp.exp(a_log.astype(jnp.float32))
    y = ssd_scan(c_x, dt, a, c_b.reshape(bsz, t, C_GROUPS, C_STATE), c_c.reshape(bsz, t, C_GROUPS, C_STATE))
    y = (y + d_skip.astype(jnp.float32)[:, None] * c_x.astype(jnp.float32)).astype(h.dtype)
    y = (y.reshape(bsz, t, C_W) * jax.nn.silu(c_z)).reshape(bsz, t, C_GROUPS, C_W // C_GROUPS)
    y = rms_norm(y, norm_w.reshape(C_GROUPS, C_W // C_GROUPS)).reshape(bsz, t, C_W)
    heads_d = lambda z: z.reshape(bsz, t, D_HEADS, D_HEAD_DIM)
    log_f = jax.nn.log_sigmoid(d_f.astype(jnp.float32) + f_bias.astype(jnp.float32))
    o_d = forgetting_attention(heads_d(d_q), heads_d(d_k), heads_d(d_v), log_f)
    return jnp.concatenate([y, o_d], axis=-1) @ w_out


def setup_inputs(seed: int = 0) -> dict:
    key = jax.random.key(seed)
    ks = jax.random.split(key, 24)
    f32 = jnp.float32
    nrm = lambda k, shape, scale: jax.random.normal(k, shape, f32) * scale

    def mamba_dt_bias(k, shape):
        dt = jnp.exp(jax.random.uniform(k, shape, f32, np.log(1e-3), np.log(1e-1)))
        return dt + jnp.log(-jnp.expm1(-dt))

    return {
        "x": nrm(ks[0], (BATCH, SEQ, D_MODEL), 1.0),
        "norm_mix": 1.0 + nrm(ks[1], (DEPTH, D_MODEL), 0.02),
        "norm_ffn": 1.0 + nrm(ks[2], (DEPTH, D_MODEL), 0.02),
        "norm_final": 1.0 + nrm(ks[3], (D_MODEL,), 0.02),
        "ffn_w_gate": nrm(ks[4], (DEPTH, D_MODEL, D_FF), D_MODEL ** -0.5),
        "ffn_w_up": nrm(ks[5], (DEPTH, D_MODEL, D_FF), D_MODEL ** -0.5),
        "ffn_w_down": nrm(ks[6], (DEPTH, D_FF, D_MODEL), D_FF ** -0.5),
        "ab_w_in": nrm(ks[7], (N_EVEN, D_MODEL, PROJ_EVEN), D_MODEL ** -0.5),
        "ab_rel_bias": nrm(ks[8], (N_EVEN, A_HEADS, 2 * A_MAX_REL + 1), 0.2),
        "ab_conv_w": nrm(ks[9], (N_EVEN, CONV_K, 3 * B_W), CONV_K ** -0.5),
        "ab_a_log": jnp.log(jax.random.uniform(ks[10], (N_EVEN, B_HEADS), f32, 1.0, 16.0)),
        "ab_dt_bias": mamba_dt_bias(ks[11], (N_EVEN, B_HEADS)),
        "ab_norm_w": 1.0 + nrm(ks[12], (N_EVEN, B_HEAD_DIM), 0.02),
        "ab_w_out": nrm(ks[13], (N_EVEN, A_W + B_W, D_MODEL), (A_W + B_W) ** -0.5),
        "cd_w_in": nrm(ks[14], (N_ODD, D_MODEL, PROJ_ODD), D_MODEL ** -0.5),
        "cd_conv_w": nrm(ks[15], (N_ODD, CONV_K, C_CONV_CH), CONV_K ** -0.5),
        "cd_conv_b": nrm(ks[16], (N_ODD, C_CONV_CH), 0.02),
        "cd_dt_bias": mamba_dt_bias(ks[17], (N_ODD, C_HEADS)),
        "cd_a_log": jnp.log(jax.random.uniform(ks[18], (N_ODD, C_HEADS), f32, 1.0, 16.0)),
        "cd_d_skip": 1.0 + nrm(ks[19], (N_ODD, C_HEADS), 0.1),
        "cd_norm_w": 1.0 + nrm(ks[20], (N_ODD, C_W), 0.02),
        "cd_f_bias": 2.0 + nrm(ks[21], (N_ODD, D_HEADS), 0.1),
        "cd_w_out": nrm(ks[22], (N_ODD, C_W + D_W, D_MODEL), (C_W + D_W) ** -0.5),
    }


def reference(x, norm_mix, norm_ffn, norm_final, ffn_w_gate, ffn_w_up, ffn_w_down,
              ab_w_in, ab_rel_bias, ab_conv_w, ab_a_log, ab_dt_bias, ab_norm_w, ab_w_out,
              cd_w_in, cd_conv_w, cd_conv_b, cd_dt_bias, cd_a_log, cd_d_skip, cd_norm_w,
              cd_f_bias, cd_w_out):
    for layer in range(DEPTH):
        h = rms_norm(x, norm_mix[layer])
        i = layer // 2
        if layer % 2 == 0:
            x = x + even_mixer(h, ab_w_in[i], ab_rel_bias[i], ab_conv_w[i], ab_a_log[i],
                               ab_dt_bias[i], ab_norm_w[i], ab_w_out[i])
        else:
            x = x + odd_mixer(h, cd_w_in[i], cd_conv_w[i], cd_conv_b[i], cd_dt_bias[i],
                              cd_a_log[i], cd_d_skip[i], cd_norm_w[i], cd_f_bias[i], cd_w_out[i])
        h = rms_norm(x, norm_ffn[layer])
        x = x + swiglu(h, ffn_w_gate[layer], ffn_w_up[layer], ffn_w_down[layer])
    return rms_norm(x, norm_final)
```

```python
import contextlib
import os
import numpy as np
import ml_dtypes
import concourse.bass as bass
import concourse.mybir as mybir
from concourse.bass_utils import run_bass_kernel_spmd

_CTX = {}

F32 = mybir.dt.float32
BF16 = mybir.dt.bfloat16
AF = mybir.ActivationFunctionType
ALU = mybir.AluOpType
AX = mybir.AxisListType

ENGS = ["pe", "dve", "act", "pool", "sp"]
EPOCH = 30000
NDMA = 24


class KB:
    def __init__(self, nc):
        self.nc = nc
        self.ops = {e: [] for e in ENGS}
        self.cnt = {e: 0 for e in ENGS}
        self.lastw = {}
        self.readers = {}
        self.seen = {e: {} for e in ENGS}
        self.dma_i = 0
        self.semnames = set()
        self.dma_tokens = {}

    def _deps(self, eng, reads, writes):
        toks = {}
        def add(t):
            if t is None:
                return
            s, v = t
            if toks.get(s, 0) < v:
                toks[s] = v
        for k in reads:
            add(self.lastw.get(k))
        for k in writes:
            add(self.lastw.get(k))
            for s, v in self.readers.get(k, {}).items():
                add((s, v))
        waits = []
        for s, v in toks.items():
            if eng == "pe" and s.startswith("pe"):
                continue
            if self.seen[eng].get(s, 0) >= v:
                continue
            self.seen[eng][s] = v
            waits.append((s, v))
        return waits

    def _commit(self, tok, reads, writes):
        for k in reads:
            d = self.readers.setdefault(k, {})
            if d.get(tok[0], 0) < tok[1]:
                d[tok[0]] = tok[1]
        for k in writes:
            self.lastw[k] = tok
            self.readers[k] = {}

    def op(self, eng, fn, reads=(), writes=()):
        waits = self._deps(eng, reads, writes)
        n = self.cnt[eng]
        self.cnt[eng] = n + 1
        s = "%s%d" % (eng, n // EPOCH)
        self.semnames.add(s)
        tok = (s, n % EPOCH + 1)
        self.ops[eng].append((waits, fn, (s, 1)))
        self._commit(tok, reads, writes)
        return tok

    def dma(self, out, in_, reads=(), writes=(), eng="sp", **kw):
        i = self.dma_i
        self.dma_i += 1
        s = "dma%d" % (i % NDMA)
        self.semnames.add(s)
        waits = self._deps(eng, reads, writes)
        prev = 16 * (i // NDMA)
        if prev > 0 and self.seen[eng].get(s, 0) < prev:
            self.seen[eng][s] = prev
            waits.append((s, prev))
        tok = (s, prev + 16)
        self.ops[eng].append((waits, lambda e: e.dma_start(out=out, in_=in_, **kw), (s, 16)))
        self._commit(tok, reads, writes)
        self.dma_tokens[s] = tok[1]
        return tok

    def mm(self, out, lhsT, rhs, start=True, stop=True, reads=(), writes=(), **kw):
        return self.op("pe", lambda e: e.matmul(out, lhsT, rhs, start=start, stop=stop, **kw), reads, writes)

    def tr(self, out, in_, ident, reads=(), writes=()):
        return self.op("pe", lambda e: e.transpose(out, in_, ident), reads, writes)

    def act(self, out, in_, func, reads=(), writes=(), **kw):
        return self.op("act", lambda e: e.activation(out, in_, func, **kw), reads, writes)

    def v(self, eng, name, *args, reads=(), writes=(), **kw):
        return self.op(eng, lambda e: getattr(e, name)(*args, **kw), reads, writes)

    def _sem(self, s):
        if s not in self.sems:
            self.sems[s] = self.semstack.enter_context(self.nc.semaphore(s))
        return self.sems[s]

    def flush(self):
        nc = self.nc
        if not hasattr(self, "sems"):
            self.sems = {}
            self.semstack = contextlib.ExitStack()
        finals = []
        for e in ENGS:
            if e == "sp" or self.cnt[e] == 0:
                continue
            n = self.cnt[e] - 1
            finals.append(("%s%d" % (e, n // EPOCH), n % EPOCH + 1))
        for s, v in self.dma_tokens.items():
            finals.append((s, v))
        for s in sorted(self.semnames):
            self._sem(s)
        sems = self.sems
        ops = self.ops
        with nc.Block() as block:
            def replay(eng, lst):
                for waits, fn, inc in lst:
                    for s, v in waits:
                        eng.wait_ge(sems[s], v)
                    fn(eng).then_inc(sems[inc[0]], inc[1])
                for s, v in finals:
                    eng.wait_ge(sems[s], v)

            @block.tensor
            def _(eng):
                replay(eng, ops["pe"])

            @block.vector
            def _(eng):
                replay(eng, ops["dve"])

            @block.scalar
            def _(eng):
                replay(eng, ops["act"])

            @block.gpsimd
            def _(eng):
                replay(eng, ops["pool"])

            @block.sync
            def _(eng):
                replay(eng, ops["sp"])
        self.ops = {e: [] for e in ENGS}
        self.lastw = {}
        self.readers = {}
        for e in ENGS:
            for s, v in finals:
                if self.seen[e].get(s, 0) < v:
                    self.seen[e][s] = v

    def close(self):
        self.semstack.close()


D = 1024
T = 8192
TBM = 512
NBLK = T // TBM
EPS = 1e-6


class Mix:
    def __init__(self, ncols, xdtype_norm=True):
        self.nc = nc = _CTX["nc"]
        self.kb = _CTX["kb"]
        self.tag = _CTX["tag"]
        self.st = contextlib.ExitStack()
        self.ncols = ncols
        self.x_d = self.dram("x", [T, D])
        self.w_d = self.dram("w_in", [D, ncols])
        self.nw_d = self.dram("norm_w", [1, D])
        self.id_d = self.dram("ident", [128, 128])
        kb = self.kb
        self.w = self.sb("w", [128, 8, ncols], BF16)
        self.nw = self.sb("nw", [128, D])
        self.idf = self.sb("idf", [128, 128])
        self.idb = self.sb("idb", [128, 128], BF16)
        self.xt = self.sb("xt", [128, 4, D])
        self.hb = self.sb("hb", [128, 4, D], BF16)
        self.hT = self.sb("hT", [128, 8, TBM], BF16)
        self.ss = self.sb("ss", [128, 4])
        self.rs = self.sb("rs", [128, 4])
        self.ptr = [self.ps("ptr%d" % i, [128, 4, 128], BF16) for i in range(2)]
        kb.dma(self.idf[:], self.id_d[:, :], writes=["idf"])
        kb.v("dve", "tensor_copy", self.idb[:], self.idf[:], reads=["idf"], writes=["idb"])
        kb.dma(self.nw[:], self.nw_d[0:1, :].to_broadcast([128, D]), writes=["nw"])
        stg = self.xt
        i = 0
        for kc in range(8):
            for n0 in range(0, ncols, 1024):
                n1 = min(ncols, n0 + 1024)
                s = i % 4
                i += 1
                kb.dma(stg[:, s, 0:n1 - n0], self.w_d[kc * 128:(kc + 1) * 128, n0:n1], writes=[("xt", s)])
                eng = ["dve", "pool"][s % 2]
                kb.v(eng, "tensor_copy", self.w[:, kc, n0:n1], stg[:, s, 0:n1 - n0], reads=[("xt", s)], writes=["w"])

    def dram(self, name, shape, dt=F32, kind="ExternalInput"):
        if kind == "ExternalOutput":
            return _CTX["out"]
        if name == "x":
            return _CTX["x"]
        return self.nc.dram_tensor("%s_%s" % (name, self.tag), shape, dt, kind=kind).ap()

    def sb(self, name, shape, dt=F32):
        return self.st.enter_context(self.nc.sbuf_tensor("%s_%s" % (self.tag, name), shape, dt))

    def ps(self, name, shape, dt=F32):
        return self.st.enter_context(self.nc.psum_tensor("%s_%s" % (self.tag, name), shape, dt))

    def const(self, name, shape, bf=False):
        d = self.dram(name, shape)
        t = self.sb(name + "_s", shape)
        self.kb.dma(t[:], d, writes=[name])
        if bf:
            tb = self.sb(name + "_b", shape, BF16)
            self.kb.v("dve", "tensor_copy", tb[:], t[:], reads=[name], writes=[name + "_b"])
            return t, tb
        return t

    def frontend(self, blk):
        kb = self.kb
        xt, hb, hT, ss, rs = self.xt, self.hb, self.hT, self.ss, self.rs
        t0 = blk * TBM
        for j in range(4):
            kb.dma(xt[:, j, :], self.x_d[t0 + j * 128:t0 + (j + 1) * 128, :], writes=[("xt", j)])
        for j in range(4):
            kb.act(hb[:, j, :], xt[:, j, :], AF.Square, accum_out=ss[:, j:j + 1], reads=[("xt", j)], writes=[("hb", j), ("ss", j)])
            kb.v("dve", "tensor_scalar", rs[:, j:j + 1], ss[:, j:j + 1], 1.0 / D, EPS, ALU.mult, ALU.add,
                 reads=[("ss", j)], writes=[("rs", j)])
            kb.act(rs[:, j:j + 1], rs[:, j:j + 1], AF.Sqrt, reads=[("rs", j)], writes=[("rs", j)])
            kb.v("dve", "reciprocal", rs[:, j:j + 1], rs[:, j:j + 1], reads=[("rs", j)], writes=[("rs", j)])
            kb.v("dve", "scalar_tensor_tensor", hb[:, j, :], xt[:, j, :], rs[:, j:j + 1], self.nw[:], ALU.mult, ALU.mult,
                 reads=[("xt", j), ("rs", j), "nw"], writes=[("hb", j)])
            for q4 in range(2):
                pt = self.ptr[q4]
                for i in range(4):
                    kc = q4 * 4 + i
                    kb.tr(pt[:, i, :], hb[:, j, kc * 128:(kc + 1) * 128], self.idb[:], reads=[("hb", j), "idb"], writes=[("ptr", q4)])
                if q4 == 0:
                    kb.v("dve", "tensor_copy", hT[:, 0:4, j * 128:(j + 1) * 128], pt[:], reads=[("ptr", q4)], writes=["hT"])
                else:
                    kb.act(hT[:, 4:8, j * 128:(j + 1) * 128], pt[:], AF.Copy, reads=[("ptr", q4)], writes=["hT"])

    def proj_fm(self, psum, pkey, c0, ncol=128):
        for kc in range(8):
            self.kb.mm(psum, self.w[:, kc, c0:c0 + ncol], self.hT[:, kc, :], start=(kc == 0), stop=(kc == 7),
                       reads=["w", "hT"], writes=[pkey])

    def proj_tm(self, psum, pkey, j, c0, ncol):
        for kc in range(8):
            self.kb.mm(psum, self.hT[:, kc, j * 128:(j + 1) * 128], self.w[:, kc, c0:c0 + ncol], start=(kc == 0), stop=(kc == 7),
                       reads=["w", "hT"], writes=[pkey])

    def finish(self):
        self.kb.flush()
        self.st.close()


D = 1024
DFF = 2816
NF = DFF // 128
TOK = 8192
TB = 256
EPS = 1e-6


def build_tok(final):
    nc = _CTX["nc"]
    tag = _CTX["tag"]
    def dr(name, shape, dt=F32, kind="ExternalInput"):
        if name in ("x", "omT"):
            return _CTX[name]
        if kind == "ExternalOutput":
            return _CTX["out"]
        return nc.dram_tensor("%s_%s" % (name, tag), shape, dt, kind=kind).ap()
    x_d = dr("x", [TOK, D])
    om_d = dr("omT", [D, TOK], BF16)
    wo_d = dr("w_out", [D, D])
    wg_d = dr("w_gate", [D, DFF])
    wu_d = dr("w_up", [D, DFF])
    wd_d = dr("w_down", [DFF, D])
    nf_d = dr("norm_ffn", [1, D])
    id_d = dr("ident", [128, 128])
    if final:
        nfin_d = dr("norm_final", [1, D])
        gnw_d = dr("gnw", [128, 4])
    y_d = dr("y", [TOK, D], F32, "ExternalOutput")
    kb = _CTX["kb"]
    with contextlib.ExitStack() as st:
        sb = lambda name, shape, dt=F32: st.enter_context(nc.sbuf_tensor("%s_%s" % (tag, name), shape, dt))
        ps = lambda name, shape, dt=F32: st.enter_context(nc.psum_tensor("%s_%s" % (tag, name), shape, dt))
        wo = sb("wo", [128, 8, D], BF16)
        wg = sb("wg", [128, 8, DFF], BF16)
        wu = sb("wu", [128, 8, DFF], BF16)
        wd = sb("wd", [128, NF, D], BF16)
        stg = [sb("stg%d" % i, [128, 1024]) for i in range(3)]
        nfw = sb("nfw", [128, D])
        idf = sb("idf", [128, 128]); idb = sb("idb", [128, 128], BF16)
        ones = sb("ones", [128, 128], BF16)
        xt = sb("xt", [128, 2, D])
        om = sb("om", [128, 8, TB], BF16)
        hb = sb("hb", [128, 2, D], BF16)
        hT = sb("hT", [128, 8, TB], BF16)
        actT = sb("actT", [128, NF, TB], BF16)
        sg = [sb("sg%d" % i, [128, TB]) for i in range(2)]
        ss = sb("ss", [128, 4]); rs = sb("rs", [128, 4])
        if final:
            nfin = sb("nfin", [128, D]); gnw = sb("gnw_s", [128, 4])
            sq = sb("sq", [128, 4, TB], BF16); rg = sb("rg", [128, TB])
        pmm = [ps("pmm%d" % i, [128, 512]) for i in range(2)]
        pg = [ps("pg%d" % i, [128, 512]) for i in range(2)]
        pu = [ps("pu%d" % i, [128, 512]) for i in range(2)]
        ptr = [ps("ptr%d" % i, [128, 4, 128], BF16) for i in range(2)]

        kb.dma(idf[:], id_d[:, :], writes=["idf"])
        kb.v("dve", "tensor_copy", idb[:], idf[:], reads=["idf"], writes=["idb"])
        kb.v("dve", "memset", ones[:], 1.0, writes=["ones"])
        kb.dma(nfw[:], nf_d[0:1, :].to_broadcast([128, D]), writes=["nfw"])
        if final:
            kb.dma(nfin[:], nfin_d[0:1, :].to_broadcast([128, D]), writes=["nfin"])
            kb.dma(gnw[:], gnw_d[:, :], writes=["gnw"])
        cvt_i = [0]
        def load_w(dst, src, K, N, name):
            for kc in range(K // 128):
                for n0 in range(0, N, 1024):
                    n1 = min(N, n0 + 1024)
                    i = cvt_i[0] % 3
                    cvt_i[0] += 1
                    kb.dma(stg[i][:, 0:n1 - n0], src[kc * 128:(kc + 1) * 128, n0:n1], writes=[("stg", i)])
                    eng = ["dve", "pool", "act"][i]
                    if eng == "act":
                        kb.act(dst[:, kc, n0:n1], stg[i][:, 0:n1 - n0], AF.Copy, reads=[("stg", i)], writes=[name])
                    else:
                        kb.v(eng, "tensor_copy", dst[:, kc, n0:n1], stg[i][:, 0:n1 - n0], reads=[("stg", i)], writes=[name])
        load_w(wo, wo_d, D, D, "wo")
        load_w(wg, wg_d, D, DFF, "wg")
        load_w(wu, wu_d, D, DFF, "wu")
        load_w(wd, wd_d, DFF, D, "wd")

        def rmsnorm_rows(src, j, col, wtile, dst, dst_key):
            kb.act(hb[:, j, :], src, AF.Square, accum_out=ss[:, col:col + 1], reads=["xt"], writes=["hb", ("ss", col)])
            kb.v("dve", "tensor_scalar", rs[:, col:col + 1], ss[:, col:col + 1], 1.0 / D, EPS, ALU.mult, ALU.add,
                 reads=[("ss", col)], writes=[("rs", col)])
            kb.act(rs[:, col:col + 1], rs[:, col:col + 1], AF.Sqrt, reads=[("rs", col)], writes=[("rs", col)])
            kb.v("dve", "reciprocal", rs[:, col:col + 1], rs[:, col:col + 1], reads=[("rs", col)], writes=[("rs", col)])
            kb.v("dve", "scalar_tensor_tensor", dst, src, rs[:, col:col + 1], wtile[:], ALU.mult, ALU.mult,
                 reads=["xt", ("rs", col), "nfw", "nfin"], writes=[dst_key])

        omv = om_d.rearrange("(kc p) t -> p kc t", p=128)
        for blk in range(TOK // TB):
            t0 = blk * TB
            kb.dma(xt[:], x_d[t0:t0 + TB, :].rearrange("(j p) d -> p j d", p=128), writes=["xt"])
            kb.dma(om[:], omv[:, :, t0:t0 + TB], writes=["om"])
            if final:
                kb.v("pool", "tensor_tensor", sq[:], om[:, 0:4, :], om[:, 0:4, :], ALU.mult, reads=["om"], writes=["sq"])
                for G in range(2):
                    p = pmm[G]
                    for i in range(2):
                        kb.mm(p[:, 0:TB], ones[:], sq[:, 2 * G + i, :], start=(i == 0), stop=(i == 1),
                              reads=["ones", "sq"], writes=[("pmm", G)])
                    kb.v("dve", "tensor_scalar", rg[:], p[:, 0:TB], 1.0 / 256, EPS, ALU.mult, ALU.add,
                         reads=[("pmm", G)], writes=["rg"])
                    kb.act(rg[:], rg[:], AF.Sqrt, reads=["rg"], writes=["rg"])
                    kb.v("dve", "reciprocal", rg[:], rg[:], reads=["rg"], writes=["rg"])
                    for i in range(2):
                        kc = 2 * G + i
                        kb.v("dve", "scalar_tensor_tensor", om[:, kc, :], om[:, kc, :], gnw[:, kc:kc + 1], rg[:],
                             ALU.mult, ALU.mult, reads=["om", "rg", "gnw"], writes=["om"])
            for j in range(2):
                for hf in range(2):
                    p = pmm[(2 * j + hf) % 2]
                    key = ("pmm", (2 * j + hf) % 2)
                    for kc in range(8):
                        kb.mm(p[:], om[:, kc, j * 128:(j + 1) * 128], wo[:, kc, hf * 512:(hf + 1) * 512],
                              start=(kc == 0), stop=(kc == 7), reads=["om", "wo"], writes=[key])
                    kb.v("dve", "tensor_tensor", xt[:, j, hf * 512:(hf + 1) * 512], xt[:, j, hf * 512:(hf + 1) * 512], p[:],
                         ALU.add, reads=[key, "xt"], writes=["xt"])
            for j in range(2):
                rmsnorm_rows(xt[:, j, :], j, j, nfw, hb[:, j, :], "hb")
                for q4 in range(2):
                    pt = ptr[q4]
                    for i in range(4):
                        kc = q4 * 4 + i
                        kb.tr(pt[:, i, :], hb[:, j, kc * 128:(kc + 1) * 128], idb[:], reads=["hb", "idb"], writes=[("ptr", q4)])
                    if q4 == 0:
                        kb.v("dve", "tensor_copy", hT[:, 0:4, j * 128:(j + 1) * 128], pt[:], reads=[("ptr", q4)], writes=["hT"])
                    else:
                        kb.act(hT[:, 4:8, j * 128:(j + 1) * 128], pt[:], AF.Copy, reads=[("ptr", q4)], writes=["hT"])
            for f in range(NF):
                b = f % 2
                for kc in range(8):
                    kb.mm(pg[b][:, 0:TB], wg[:, kc, f * 128:(f + 1) * 128], hT[:, kc, :], start=(kc == 0), stop=(kc == 7),
                          reads=["wg", "hT"], writes=[("pg", b)])
                for kc in range(8):
                    kb.mm(pu[b][:, 0:TB], wu[:, kc, f * 128:(f + 1) * 128], hT[:, kc, :], start=(kc == 0), stop=(kc == 7),
                          reads=["wu", "hT"], writes=[("pu", b)])
                kb.act(sg[b][:], pg[b][:, 0:TB], AF.Silu, reads=[("pg", b)], writes=[("sg", b)])
                kb.v("dve", "tensor_tensor", actT[:, f, :], sg[b][:], pu[b][:, 0:TB], ALU.mult,
                     reads=[("sg", b), ("pu", b)], writes=["actT"])
            for j in range(2):
                for hf in range(2):
                    p = pmm[(2 * j + hf) % 2]
                    key = ("pmm", (2 * j + hf) % 2)
                    for f in range(NF):
                        kb.mm(p[:], actT[:, f, j * 128:(j + 1) * 128], wd[:, f, hf * 512:(hf + 1) * 512],
                              start=(f == 0), stop=(f == NF - 1), reads=["actT", "wd"], writes=[key])
                    kb.v("dve", "tensor_tensor", xt[:, j, hf * 512:(hf + 1) * 512], xt[:, j, hf * 512:(hf + 1) * 512], p[:],
                         ALU.add, reads=[key, "xt"], writes=["xt"])
            if final:
                for j in range(2):
                    rmsnorm_rows(xt[:, j, :], j, 2 + j, nfin, xt[:, j, :], "xt")
            kb.dma(y_d[t0:t0 + TB, :].rearrange("(j p) d -> p j d", p=128), xt[:], reads=["xt"])
        kb.flush()


NT = T // 128


def build_band():
    m = Mix(384)
    kb, nc = m.kb, m.nc
    o_d = m.dram("oT", [128, T], BF16, "ExternalOutput")
    bias_d = m.dram("biasT", [128, 10, 128])
    biasf = m.sb("biasf", [128, 10, 128])
    biasb = m.sb("biasb", [128, 10, 128], BF16)
    kb.dma(biasf[:], bias_d, writes=["biasf"])
    kb.v("dve", "tensor_copy", biasb[:], biasf[:], reads=["biasf"], writes=["biasb"])
    qT = m.sb("qT", [128, T], BF16)
    kT = m.sb("kT", [128, T], BF16)
    v = m.sb("v", [128, NT, 2, 65], BF16)
    pT = [m.sb("pT%d" % i, [128, 128], BF16) for i in range(3)]
    osb = m.sb("osb", [128, 4, 128])
    rden = m.sb("rden", [128, 2])
    oT = m.sb("oT_s", [128, 512], BF16)
    pfm = [m.ps("pfm%d" % i, [128, 512]) for i in range(2)]
    ptm = m.ps("ptm", [128, 512])
    pO = [m.ps("pO%d" % i, [128, 2, 128]) for i in range(2)]
    ptf = m.ps("ptf", [128, 4, 128])
    kb.v("dve", "memset", v[:].rearrange("p a b c -> p (a b c)"), 1.0, writes=["v"])
    for blk in range(NBLK):
        m.frontend(blk)
        t0 = blk * TBM
        m.proj_fm(pfm[0][:], ("pfm", 0), 0)
        kb.v("dve", "tensor_scalar", qT[:, t0:t0 + TBM], pfm[0][:], 0.125, None, ALU.mult, reads=[("pfm", 0)], writes=["qT"])
        m.proj_fm(pfm[1][:], ("pfm", 1), 128)
        kb.v("dve", "tensor_copy", kT[:, t0:t0 + TBM], pfm[1][:], reads=[("pfm", 1)], writes=["kT"])
        for j in range(4):
            tile = blk * 4 + j
            m.proj_tm(ptm[:, 0:128], "ptm", j, 256, 128)
            kb.v("dve", "tensor_copy", v[:, tile, :, 0:64], ptm[:, 0:128].rearrange("p (h d) -> p h d", h=2),
                 reads=["ptm"], writes=["v"])
        it = 0
        for j in range(4):
            qt = blk * 4 + j
            po = pO[qt % 2]; pok = ("pO", qt % 2)
            for h in range(2):
                hs = slice(h * 64, (h + 1) * 64)
                deltas = [dl for dl in range(5) if qt - dl >= 0]
                for di, dl in enumerate(deltas):
                    kt = qt - dl
                    ps_ = pfm[it % 2]; psk = ("pfm", it % 2)
                    pt = pT[it % 3]; ptk = ("pT", it % 3)
                    it += 1
                    kb.mm(ps_[:, 0:128], kT[hs, kt * 128:(kt + 1) * 128], qT[hs, qt * 128:(qt + 1) * 128], start=True, stop=False,
                          reads=["kT", "qT"], writes=[psk])
                    kb.mm(ps_[:, 0:128], m.idb[:], biasb[:, h * 5 + dl, :], start=False, stop=True, reads=["idb", "biasb"], writes=[psk])
                    kb.act(pt[:], ps_[:, 0:128], AF.Exp, reads=[psk], writes=[ptk])
                    kb.mm(po[:, h, 0:65], pt[:], v[:, kt, h, :], start=(di == 0), stop=(di == len(deltas) - 1),
                          reads=[ptk, "v"], writes=[pok])
                kb.v("dve", "reciprocal", rden[:, h:h + 1], po[:, h, 64:65], reads=[pok], writes=["rden"])
                kb.v("dve", "tensor_scalar", osb[:, j, hs], po[:, h, 0:64], rden[:, h:h + 1], None, ALU.mult,
                     reads=[pok, "rden"], writes=["osb"])
            kb.op("pe", lambda e, j=j: e.transpose(ptf[:, j, :], osb[:, j, :], m.idf[:]), reads=["osb", "idf"], writes=["ptf"])
        kb.v("dve", "tensor_copy", oT[:], ptf[:].rearrange("p s q -> p (s q)"), reads=["ptf"], writes=["oT"])
        kb.dma(o_d[:, t0:t0 + TBM], oT[:], reads=["oT"])
    return m.finish()


def band_bias(rb2):
    kk = np.arange(128)[:, None]; qq = np.arange(128)[None, :]
    out = np.empty((128, 10, 128), np.float32)
    for h in range(2):
        for dl in range(5):
            dist = 128 * dl + qq - kk
            idx = np.clip(dist, -256, 256) + 256
            cd = 2 * dl + qq // 64 - kk // 64
            valid = (cd >= 0) & (cd <= 8)
            out[:, h * 5 + dl, :] = np.where(valid, rb2[h][idx], np.float32(-30000.0))
    return out


NT = T // 128


def build_fox():
    m = Mix(386)
    kb, nc = m.kb, m.nc
    fb_d = m.dram("f_bias", [128, 2])
    o_d = m.dram("oT", [128, T], BF16, "ExternalOutput")
    trif = m.const("trif", [128, 128])
    umat = m.const("umat", [128, 128])
    sel = m.const("sel127", [128, 128])
    onesf = m.sb("onesf", [128, 128])
    kb.v("dve", "memset", onesf[:], 1.0, writes=["onesf"])
    trib = m.sb("trib", [128, 128], BF16)
    kb.v("dve", "tensor_copy", trib[:], trif[:], reads=["trif"], writes=["trib"])
    nfb = m.sb("nfb", [128, 2])
    kb.dma(nfb[:], fb_d[:, :], writes=["nfb"])
    kb.v("dve", "tensor_scalar", nfb[:], nfb[:], -1.0, None, ALU.mult, reads=["nfb"], writes=["nfb"])

    qT = m.sb("qT", [128, T], BF16)
    kT = m.sb("kT", [128, T], BF16)
    v = m.sb("v", [128, NT, 2, 65], BF16)
    lf = m.sb("lf", [128, 2, NT])
    F = m.sb("F", [128, 2, NT])
    totT = m.sb("totT", [128, 128])
    flast = m.sb("flast", [128, 2, NT])
    nbias = m.sb("nbias", [128, 4, NT])
    pT = [m.sb("pT%d" % i, [128, 512], BF16) for i in range(3)]
    osb = m.sb("osb", [128, 4, 128])
    rden = m.sb("rden", [128, 4])
    oT = m.sb("oT_s", [128, 512], BF16)
    pfm = [m.ps("pfm%d" % i, [128, 512]) for i in range(2)]
    ptm = m.ps("ptm", [128, 512])
    pO = [m.ps("pO%d" % i, [128, 4, 128]) for i in range(2)]
    ptf = m.ps("ptf", [128, 4, 128])

    kb.v("dve", "memset", v[:].rearrange("p a b c -> p (a b c)"), 1.0, writes=["v"])
    for blk in range(NBLK):
        m.frontend(blk)
        t0 = blk * TBM
        m.proj_fm(pfm[0][:], ("pfm", 0), 0)
        kb.v("dve", "tensor_scalar", qT[:, t0:t0 + TBM], pfm[0][:], 0.125, None, ALU.mult, reads=[("pfm", 0)], writes=["qT"])
        m.proj_fm(pfm[1][:], ("pfm", 1), 128)
        kb.v("dve", "tensor_copy", kT[:, t0:t0 + TBM], pfm[1][:], reads=[("pfm", 1)], writes=["kT"])
        for j in range(4):
            tile = blk * 4 + j
            m.proj_tm(ptm[:, 0:130], "ptm", j, 256, 130)
            kb.v("dve", "tensor_copy", v[:, tile, :, 0:64], ptm[:, 0:128].rearrange("p (h d) -> p h d", h=2),
                 reads=["ptm"], writes=["v"])
            kb.v("dve", "tensor_copy", lf[:, :, tile], ptm[:, 128:130], reads=["ptm"], writes=["lf"])
    for h in range(2):
        kb.act(lf[:, h, :], lf[:, h, :], AF.Exp, scale=-1.0, bias=nfb[:, h:h + 1], reads=["lf", "nfb"], writes=["lf"])
    kb.act(lf[:], lf[:], AF.Ln, bias=1.0, reads=["lf"], writes=["lf"])
    kb.v("dve", "tensor_scalar", lf[:], lf[:], -1.0, None, ALU.mult, reads=["lf"], writes=["lf"])
    lf2 = lf[:].rearrange("p h t -> p (h t)")
    kb.mm(ptm[:, 0:128], lf2, onesf[:], reads=["lf", "onesf"], writes=["ptm"])
    kb.v("dve", "tensor_copy", totT[:], ptm[:, 0:128], reads=["ptm"], writes=["totT"])
    kb.mm(ptm[:, 0:128], trif[:], lf2, start=True, stop=False, reads=["trif", "lf"], writes=["ptm"])
    kb.mm(ptm[:, 0:128], totT[:], umat[:], start=False, stop=True, reads=["totT", "umat"], writes=["ptm"])
    kb.v("dve", "tensor_copy", F[:].rearrange("p h t -> p (h t)"), ptm[:, 0:128], reads=["ptm"], writes=["F"])
    kb.mm(ptm[:, 0:128], sel[:], F[:].rearrange("p h t -> p (h t)"), reads=["sel127", "F"], writes=["ptm"])
    kb.v("dve", "tensor_copy", flast[:].rearrange("p h t -> p (h t)"), ptm[:, 0:128], reads=["ptm"], writes=["flast"])

    it = 0
    for qb in range(NBLK):
        q0 = qb * TBM
        for h in range(2):
            hs = slice(h * 64, (h + 1) * 64)
            nkt = 4 * qb + 4
            for s in range(4):
                kb.v("dve", "tensor_scalar", nbias[:, s, 0:nkt], F[:, h, 0:nkt], flast[:, h, 4 * qb + s:4 * qb + s + 1], -1.0,
                     ALU.subtract, ALU.mult, reads=["F", "flast"], writes=["nbias"])
            po = pO[(2 * qb + h) % 2]
            pok = ("pO", (2 * qb + h) % 2)
            for kt in range(nkt):
                j = max(0, kt - 4 * qb)
                n = TBM - 128 * j
                ps_ = pfm[it % 2]; psk = ("pfm", it % 2)
                pt = pT[it % 3]; ptk = ("pT", it % 3)
                it += 1
                kb.mm(ps_[:, 0:n], kT[hs, kt * 128:(kt + 1) * 128], qT[hs, q0 + 128 * j:q0 + TBM],
                      reads=["kT", "qT"], writes=[psk])
                for s in range(j, 4):
                    kb.act(pt[:, (s - j) * 128:(s - j + 1) * 128], ps_[:, (s - j) * 128:(s - j + 1) * 128], AF.Exp,
                           bias=nbias[:, s, kt:kt + 1], reads=[psk, "nbias"], writes=[ptk])
                if kt >= 4 * qb:
                    kb.v("pool", "tensor_tensor", pt[:, 0:128], pt[:, 0:128], trib[:], ALU.mult, reads=[ptk, "trib"], writes=[ptk])
                for s in range(j, 4):
                    kb.mm(po[:, s, 0:65], pt[:, (s - j) * 128:(s - j + 1) * 128], v[:, kt, h, :],
                          start=(kt == 0 and s == 0), stop=(kt == 4 * qb + s), reads=[ptk, "v"], writes=[pok])
            for s in range(4):
                kb.v("dve", "reciprocal", rden[:, s:s + 1], po[:, s, 64:65], reads=[pok], writes=["rden"])
                kb.v("dve", "tensor_scalar", osb[:, s, hs], po[:, s, 0:64], rden[:, s:s + 1], None, ALU.mult,
                     reads=[pok, "rden"], writes=["osb"])
        for s in range(4):
            kb.op("pe", lambda e, s=s: e.transpose(ptf[:, s, :], osb[:, s, :], m.idf[:]), reads=["osb", "idf"], writes=["ptf"])
        kb.v("dve", "tensor_copy", oT[:], ptf[:].rearrange("p s q -> p (s q)"), reads=["ptf"], writes=["oT"])
        kb.dma(o_d[:, q0:q0 + TBM], oT[:], reads=["oT"])
    return m.finish()


def fox_consts():
    j = np.arange(128)
    trif = (j[:, None] <= j[None, :]).astype(np.float32)
    hh = j // 64; tt = j % 64
    umat = ((hh[:, None] == hh[None, :]) & (tt[:, None] < tt[None, :])).astype(np.float32)
    sel = np.zeros((128, 128), np.float32); sel[127, :] = 1.0
    return {"trif": trif, "umat": umat, "sel127": sel, "ident": np.eye(128, dtype=np.float32)}


def build_ssd():
    m = Mix(514)
    kb, nc = m.kb, m.nc
    o_d = m.dram("oT", [128, T], BF16, "ExternalOutput")
    cw = m.const("convw", [128, 12])
    cb = m.const("convb", [128, 3])
    dtb = m.const("dtb", [128, 2])
    alog = m.const("alog", [128, 2])
    dsk = m.const("dskip", [128, 2])
    triblk = m.const("triblk", [128, 128])
    blkones = m.const("blkones", [128, 128])
    onesf = m.sb("onesf", [128, 128])
    kb.v("dve", "memset", onesf[:], 1.0, writes=["onesf"])
    na = m.sb("na", [128, 2])
    kb.act(na[:], alog[:], AF.Exp, reads=["alog"], writes=["na"])
    kb.v("dve", "tensor_scalar", na[:], na[:], -1.0, None, ALU.mult, reads=["na"], writes=["na"])
    raw = m.sb("raw", [128, 3, 515])
    acc = m.sb("acc", [128, 512])
    xbc = m.sb("xbc", [128, 3, 512])
    zs = m.sb("zs", [128, 4, 128])
    dt = m.sb("dt", [128, 4, 2])
    da = m.sb("da", [128, 2])
    dar = m.sb("dar", [128, 128])
    dc = m.sb("dc", [128, 2])
    tmp = m.sb("tmp", [128, 128])
    LT = m.sb("LT", [128, 128])
    wT = m.sb("wT", [128, 128])
    xtok = m.sb("xtok", [128, 128]); btok = m.sb("btok", [128, 128])
    bw = m.sb("bw", [128, 2]); Bw = [m.sb("Bw%d" % h, [128, 128]) for h in range(2)]
    edacs = m.sb("edacs", [128, 2])
    cdec = [m.sb("cdec%d" % h, [128, 128]) for h in range(2)]
    stateT = m.sb("stateT", [128, 128])
    ysb = m.sb("ysb", [128, 128])
    oT = m.sb("oT_s", [128, 512], BF16)
    pfm = [m.ps("pfm%d" % i, [128, 512]) for i in range(2)]
    ptm = pfm[1]
    pg = [m.ps("pg%d" % i, [128, 512]) for i in range(4)]
    kb.v("dve", "memset", raw[:].rearrange("p a b -> p (a b)"), 0.0, writes=["raw"])
    kb.v("dve", "memset", stateT[:], 0.0, writes=["stateT"])
    for blk in range(NBLK):
        m.frontend(blk)
        t0 = blk * TBM
        for i in range(3):
            p = pfm[i % 2]; pk = ("pfm", i % 2)
            m.proj_fm(p[:], pk, i * 128)
            kb.act(raw[:, i, 3:515], p[:], AF.Copy, reads=[pk], writes=["raw"])
            kb.v("dve", "tensor_scalar", acc[:], raw[:, i, 0:512], cw[:, 4 * i:4 * i + 1], cb[:, i:i + 1], ALU.mult, ALU.add,
                 reads=["raw", "convw", "convb"], writes=["acc"])
            for k in range(1, 4):
                kb.v("dve", "scalar_tensor_tensor", acc[:], raw[:, i, k:k + 512], cw[:, 4 * i + k:4 * i + k + 1], acc[:],
                     ALU.mult, ALU.add, reads=["raw", "convw", "acc"], writes=["acc"])
            kb.act(xbc[:, i, :], acc[:], AF.Exp, scale=-1.0, reads=["acc"], writes=["xbc"])
            kb.v("dve", "tensor_scalar", xbc[:, i, :], xbc[:, i, :], 1.0, None, ALU.add, reads=["xbc"], writes=["xbc"])
            kb.v("dve", "reciprocal", xbc[:, i, :], xbc[:, i, :], reads=["xbc"], writes=["xbc"])
            kb.v("dve", "tensor_tensor", xbc[:, i, :], xbc[:, i, :], acc[:], ALU.mult, reads=["xbc", "acc"], writes=["xbc"])
            kb.v("dve", "tensor_copy", raw[:, i, 0:3], raw[:, i, 512:515], reads=["raw"], writes=["raw"])
        for j in range(4):
            m.proj_tm(ptm[:, 0:130], ("pfm", 1), j, 384, 130)
            kb.act(zs[:, j, :], ptm[:, 0:128], AF.Exp, scale=-1.0, reads=[("pfm", 1)], writes=["zs"])
            kb.v("dve", "tensor_scalar", zs[:, j, :], zs[:, j, :], 1.0, None, ALU.add, reads=["zs"], writes=["zs"])
            kb.v("dve", "reciprocal", zs[:, j, :], zs[:, j, :], reads=["zs"], writes=["zs"])
            kb.v("dve", "tensor_tensor", zs[:, j, :], zs[:, j, :], ptm[:, 0:128], ALU.mult, reads=["zs", ("pfm", 1)], writes=["zs"])
            kb.v("dve", "tensor_tensor", dt[:, j, :], ptm[:, 128:130], dtb[:], ALU.add, reads=[("pfm", 1), "dtb"], writes=["dt"])
        kb.act(dt[:], dt[:], AF.Exp, reads=["dt"], writes=["dt"])
        kb.act(dt[:], dt[:], AF.Ln, bias=1.0, reads=["dt"], writes=["dt"])
        for j in range(4):
            sl = slice(j * 128, (j + 1) * 128)
            kb.op("pe", lambda e, sl=sl: e.transpose(pg[0][:, 0:128], xbc[:, 0, sl], m.idf[:]), reads=["xbc", "idf"], writes=[("pg", 0)])
            kb.op("pe", lambda e, sl=sl: e.transpose(pg[0][:, 128:256], xbc[:, 1, sl], m.idf[:]), reads=["xbc", "idf"], writes=[("pg", 0)])
            kb.v("dve", "tensor_copy", xtok[:], pg[0][:, 0:128], reads=[("pg", 0)], writes=["xtok"])
            kb.act(btok[:], pg[0][:, 128:256], AF.Copy, reads=[("pg", 0)], writes=["btok"])
            kb.v("dve", "tensor_tensor", da[:], dt[:, j, :], na[:], ALU.mult, reads=["dt", "na"], writes=["da"])
            kb.mm(pg[2][:, 0:128], xbc[:, 1, sl], xbc[:, 2, sl], reads=["xbc"], writes=[("pg", 2)])
            for h in range(2):
                hh = slice(h * 64, (h + 1) * 64)
                kb.v("dve", "tensor_scalar", dar[:], onesf[:], da[:, h:h + 1], None, ALU.mult, reads=["onesf", "da"], writes=["dar"])
                kb.mm(pg[1][:, 0:128], dar[:], triblk[:], reads=["dar", "triblk"], writes=[("pg", 1)])
                kb.mm(pg[1][:, 128:256], dar[:], blkones[:], reads=["dar", "blkones"], writes=[("pg", 1)])
                kb.mm(pg[1][:, 256:384], triblk[:], dar[:], reads=["dar", "triblk"], writes=[("pg", 1)])
                kb.mm(pg[1][:, 384:512], blkones[:], dar[:], reads=["dar", "blkones"], writes=[("pg", 1)])
                kb.v("dve", "tensor_copy", dc[:, 0:1], pg[1][:, 256:257], reads=[("pg", 1)], writes=["dc"])
                kb.v("dve", "tensor_copy", dc[:, 1:2], pg[1][:, 384:385], reads=[("pg", 1)], writes=["dc"])
                kb.v("dve", "tensor_scalar", tmp[:], pg[1][:, 0:128], dc[:, 0:1], 0.0, ALU.subtract, ALU.min,
                     reads=[("pg", 1), "dc"], writes=["tmp"])
                kb.act(LT[:], tmp[:], AF.Exp, reads=["tmp"], writes=["LT"])
                kb.v("dve", "scalar_tensor_tensor", LT[:], LT[:], dt[:, j, h:h + 1], triblk[:], ALU.mult, ALU.mult,
                     reads=["LT", "dt", "triblk"], writes=["LT"])
                kb.v("dve", "tensor_tensor", wT[:], LT[:], pg[2][:, 0:128], ALU.mult, reads=["LT", ("pg", 2)], writes=["wT"])
                kb.mm(pg[2][:, 128 + h * 64:128 + (h + 1) * 64], wT[:], xtok[:, hh], reads=["wT", "xtok"], writes=[("pg", 2)])
                kb.v("dve", "tensor_tensor", bw[:, h:h + 1], dc[:, 1:2], dc[:, 0:1], ALU.subtract, reads=["dc"], writes=["bw"])
                kb.act(bw[:, h:h + 1], bw[:, h:h + 1], AF.Exp, reads=["bw"], writes=["bw"])
                kb.v("dve", "tensor_tensor", bw[:, h:h + 1], bw[:, h:h + 1], dt[:, j, h:h + 1], ALU.mult, reads=["bw", "dt"], writes=["bw"])
                kb.v("dve", "tensor_scalar", Bw[h][:], btok[:], bw[:, h:h + 1], None, ALU.mult, reads=["btok", "bw"], writes=[("Bw", h)])
                kb.act(edacs[:, h:h + 1], dc[:, 0:1], AF.Exp, reads=["dc"], writes=["edacs"])
                kb.act(cdec[h][:], pg[1][:, 128:256], AF.Exp, reads=[("pg", 1)], writes=[("cdec", h)])
            kb.act(ysb[:], pg[2][:, 128:256], AF.Copy, reads=[("pg", 2)], writes=["ysb"])
            for c in range(2):
                cs = slice(c * 64, (c + 1) * 64)
                kb.mm(pg[3][:, 0:128], xbc[:, 2, sl], stateT[:], reads=["xbc", "stateT"], writes=[("pg", 3)])
                for h in range(2):
                    hh = slice(h * 64, (h + 1) * 64)
                    kb.v("dve", "scalar_tensor_tensor", ysb[cs, hh], pg[3][cs, hh], edacs[cs, h:h + 1], ysb[cs, hh], ALU.mult, ALU.add,
                         reads=[("pg", 3), "edacs", "ysb"], writes=["ysb"])
                for h in range(2):
                    hh = slice(h * 64, (h + 1) * 64)
                    kb.mm(pg[3][:, 128:256], Bw[h][cs, :], xtok[cs, :], reads=[("Bw", h), "xtok"], writes=[("pg", 3)])
                    kb.v("dve", "scalar_tensor_tensor", stateT[:, hh], stateT[:, hh], cdec[h][:, c * 64:c * 64 + 1], pg[3][:, 128 + h * 64:128 + (h + 1) * 64],
                         ALU.mult, ALU.add, reads=["stateT", ("cdec", h), ("pg", 3)], writes=["stateT"])
            for h in range(2):
                hh = slice(h * 64, (h + 1) * 64)
                kb.v("dve", "scalar_tensor_tensor", ysb[:, hh], xtok[:, hh], dsk[:, h:h + 1], ysb[:, hh], ALU.mult, ALU.add,
                     reads=["xtok", "dskip", "ysb"], writes=["ysb"])
            kb.v("dve", "tensor_tensor", ysb[:], ysb[:], zs[:, j, :], ALU.mult, reads=["ysb", "zs"], writes=["ysb"])
            kb.op("pe", lambda e: e.transpose(pg[0][:, 256:384], ysb[:], m.idf[:]), reads=["ysb", "idf"], writes=[("pg", 0)])
            kb.v("dve", "tensor_copy", oT[:, sl], pg[0][:, 256:384], reads=[("pg", 0)], writes=["oT"])
        kb.dma(o_d[:, t0:t0 + TBM], oT[:], reads=["oT"])
    return m.finish()


def chunk_consts_ssd_unused():
    j = np.arange(128)
    same = (j[:, None] // 64) == (j[None, :] // 64)
    return {"triblk": (same & (j[:, None] <= j[None, :])).astype(np.float32), "blkones": same.astype(np.float32)}


def build_gdn():
    m = Mix(514)
    kb, nc = m.kb, m.nc
    o_d = m.dram("oT", [128, T], BF16, "ExternalOutput")
    cw = m.const("convw", [128, 12])
    dtb = m.const("dtb", [128, 1])
    alog = m.const("alog", [128, 1])
    nrmw = m.const("nrmw", [128, 128])
    triblk = m.const("triblk", [128, 128])
    tristr = m.const("tristrict", [128, 128])
    blkones = m.const("blkones", [128, 128])
    idf = m.idf
    onesf = m.sb("onesf", [128, 128])
    kb.v("dve", "memset", onesf[:], 1.0, writes=["onesf"])
    na = m.sb("na", [128, 1])
    kb.act(na[:], alog[:], AF.Exp, reads=["alog"], writes=["na"])
    kb.v("dve", "tensor_scalar", na[:], na[:], -1.0, None, ALU.mult, reads=["na"], writes=["na"])
    T_ = lambda name, shape=[128, 128]: m.sb(name, shape)
    raw = T_("raw", [128, 3, 515]); acc = T_("acc", [128, 512]); qkv = T_("qkv", [128, 3, 512]); sq = T_("sq", [128, 512])
    zs = T_("zs", [128, 4, 128]); beta = T_("beta", [128, 4]); gg = T_("gg", [128, 4])
    ktok = T_("ktok"); vtok = T_("vtok"); gr = T_("gr"); br = T_("br"); dc = T_("dc", [128, 2]); tmp = T_("tmp")
    ET = T_("ET"); egc = T_("egc", [128, 1]); egcbc = T_("egcbc"); eglbc = T_("eglbc"); ksc = T_("ksc", [128, 1]); bcol = T_("bcol", [128, 1])
    NTt = T_("NTt"); AT = T_("AT"); A = T_("A")
    Pa = [T_("Pa0"), T_("Pa1")]; PTa = [T_("PTa0"), T_("PTa1")]; RT = T_("RT")
    vb = T_("vb"); kbt = T_("kbt"); u = T_("u"); wT = T_("wT"); attnT = T_("attnT"); qdT = T_("qdT"); kst = T_("kst")
    vnew = T_("vnew"); S = T_("S"); osb = T_("osb"); junk = T_("junk"); ssq = T_("ssq", [128, 1]); rstd = T_("rstd", [128, 1])
    otk = m.sb("otk", [128, 4, 128], BF16)
    oT = m.sb("oT_s", [128, 4, 128], BF16)
    pfm = [m.ps("pfm%d" % i, [128, 512]) for i in range(2)]
    pg = [m.ps("pg%d" % i, [128, 512]) for i in range(4)]
    ptm = pfm[1]

    def R(b, r):
        return pg[b][:, r * 128:(r + 1) * 128], ("pg", b)

    for t_, k_ in ((raw[:].rearrange("p a b -> p (a b)"), "raw"), (S[:], "S"), (vnew[:], "vnew")):
        kb.v("dve", "memset", t_, 0.0, writes=[k_])

    def silu_from(out, out_key, src, src_key):
        kb.act(out, src, AF.Exp, scale=-1.0, reads=[src_key], writes=[out_key])
        kb.v("dve", "tensor_scalar", out, out, 1.0, None, ALU.add, reads=[out_key], writes=[out_key])
        kb.v("dve", "reciprocal", out, out, reads=[out_key], writes=[out_key])
        kb.v("dve", "tensor_tensor", out, out, src, ALU.mult, reads=[out_key, src_key], writes=[out_key])

    def mm1(dst, lhsT, rhs, reads):
        ap, key = dst
        kb.mm(ap, lhsT, rhs, reads=reads, writes=[key])

    def trp(dst, src, reads):
        ap, key = dst
        kb.op("pe", lambda e: e.transpose(ap, src, idf[:]), reads=reads + ["idf"], writes=[key])
    for blk in range(NBLK):
        m.frontend(blk)
        t0 = blk * TBM
        for i in range(3):
            p = pfm[i % 2]; pk = ("pfm", i % 2)
            m.proj_fm(p[:], pk, i * 128)
            kb.act(raw[:, i, 3:515], p[:], AF.Copy, reads=[pk], writes=["raw"])
            kb.v("dve", "tensor_scalar", acc[:], raw[:, i, 0:512], cw[:, 4 * i:4 * i + 1], None, ALU.mult,
                 reads=["raw", "convw"], writes=["acc"])
            for k in range(1, 4):
                kb.v("dve", "scalar_tensor_tensor", acc[:], raw[:, i, k:k + 512], cw[:, 4 * i + k:4 * i + k + 1], acc[:],
                     ALU.mult, ALU.add, reads=["raw", "convw", "acc"], writes=["acc"])
            silu_from(qkv[:, i, :], ("qkv", i), acc[:], "acc")
            kb.v("dve", "tensor_copy", raw[:, i, 0:3], raw[:, i, 512:515], reads=["raw"], writes=["raw"])
        for i in range(2):
            kb.v("dve", "tensor_tensor", sq[:], qkv[:, i, :], qkv[:, i, :], ALU.mult, reads=[("qkv", i)], writes=["sq"])
            kb.mm(pfm[0][:], onesf[:], sq[:], reads=["onesf", "sq"], writes=[("pfm", 0)])
            kb.v("dve", "tensor_scalar", sq[:], pfm[0][:], EPS, None, ALU.add, reads=[("pfm", 0)], writes=["sq"])
            kb.act(sq[:], sq[:], AF.Sqrt, reads=["sq"], writes=["sq"])
            kb.v("dve", "reciprocal", sq[:], sq[:], reads=["sq"], writes=["sq"])
            kb.v("dve", "scalar_tensor_tensor", qkv[:, i, :], qkv[:, i, :], (128 ** -0.5) if i == 0 else 1.0, sq[:], ALU.mult, ALU.mult,
                 reads=[("qkv", i), "sq"], writes=[("qkv", i)])
        for j in range(4):
            m.proj_tm(ptm[:, 0:130], ("pfm", 1), j, 384, 130)
            silu_from(zs[:, j, :], "zs", ptm[:, 0:128], ("pfm", 1))
            kb.v("dve", "tensor_copy", beta[:, j:j + 1], ptm[:, 128:129], reads=[("pfm", 1)], writes=["beta"])
            kb.v("dve", "tensor_scalar", gg[:, j:j + 1], ptm[:, 129:130], dtb[:, 0:1], None, ALU.add, reads=[("pfm", 1), "dtb"], writes=["gg"])
        kb.act(beta[:], beta[:], AF.Exp, scale=-1.0, reads=["beta"], writes=["beta"])
        kb.v("dve", "tensor_scalar", beta[:], beta[:], 1.0, None, ALU.add, reads=["beta"], writes=["beta"])
        kb.v("dve", "reciprocal", beta[:], beta[:], reads=["beta"], writes=["beta"])
        kb.act(gg[:], gg[:], AF.Exp, reads=["gg"], writes=["gg"])
        kb.act(gg[:], gg[:], AF.Ln, bias=1.0, reads=["gg"], writes=["gg"])
        kb.v("dve", "tensor_scalar", gg[:], gg[:], na[:, 0:1], None, ALU.mult, reads=["gg", "na"], writes=["gg"])
        for j in range(4):
            sl = slice(j * 128, (j + 1) * 128)
            qn, kn, vf = qkv[:, 0, sl], qkv[:, 1, sl], qkv[:, 2, sl]
            trp(R(0, 0), kn, [("qkv", 1)]); trp(R(0, 1), vf, [("qkv", 2)])
            kb.v("dve", "tensor_copy", ktok[:], R(0, 0)[0], reads=[R(0, 0)[1]], writes=["ktok"])
            kb.act(vtok[:], R(0, 1)[0], AF.Copy, reads=[R(0, 1)[1]], writes=["vtok"])
            kb.v("dve", "tensor_scalar", gr[:], onesf[:], gg[:, j:j + 1], None, ALU.mult, reads=["onesf", "gg"], writes=["gr"])
            kb.v("dve", "tensor_scalar", br[:], onesf[:], beta[:, j:j + 1], None, ALU.mult, reads=["onesf", "beta"], writes=["br"])
            mm1(R(1, 0), gr[:], triblk[:], ["gr", "triblk"])
            mm1(R(1, 1), gr[:], blkones[:], ["gr", "blkones"])
            mm1(R(1, 2), triblk[:], gr[:], ["gr", "triblk"])
            mm1(R(1, 3), blkones[:], gr[:], ["gr", "blkones"])
            mm1(R(2, 0), br[:], idf[:], ["br", "idf"])
            kb.v("dve", "tensor_copy", dc[:, 0:1], pg[1][:, 256:257], reads=[("pg", 1)], writes=["dc"])
            kb.v("dve", "tensor_copy", dc[:, 1:2], pg[1][:, 384:385], reads=[("pg", 1)], writes=["dc"])
            kb.v("dve", "tensor_scalar", tmp[:], R(1, 0)[0], dc[:, 0:1], 0.0, ALU.subtract, ALU.min, reads=[("pg", 1), "dc"], writes=["tmp"])
            kb.act(ET[:], tmp[:], AF.Exp, reads=["tmp"], writes=["ET"])
            kb.v("dve", "tensor_tensor", ET[:], ET[:], triblk[:], ALU.mult, reads=["ET", "triblk"], writes=["ET"])
            kb.act(egcbc[:], R(1, 0)[0], AF.Exp, reads=[("pg", 1)], writes=["egcbc"])
            kb.act(eglbc[:], R(1, 1)[0], AF.Exp, reads=[("pg", 1)], writes=["eglbc"])
            kb.act(egc[:], dc[:, 0:1], AF.Exp, reads=["dc"], writes=["egc"])
            kb.v("dve", "tensor_tensor", ksc[:], dc[:, 1:2], dc[:, 0:1], ALU.subtract, reads=["dc"], writes=["ksc"])
            kb.act(ksc[:], ksc[:], AF.Exp, reads=["ksc"], writes=["ksc"])
            kb.v("dve", "tensor_tensor", bcol[:], egc[:], beta[:, j:j + 1], ALU.mult, reads=["egc", "beta"], writes=["bcol"])
            mm1(R(2, 1), kn, kn, [("qkv", 1)])
            kb.v("dve", "tensor_tensor", NTt[:], ET[:], R(2, 1)[0], ALU.mult, reads=["ET", ("pg", 2)], writes=["NTt"])
            kb.v("dve", "tensor_tensor", NTt[:], NTt[:], tristr[:], ALU.mult, reads=["NTt", "tristrict"], writes=["NTt"])
            kb.v("dve", "tensor_tensor", AT[:], NTt[:], R(2, 0)[0], ALU.mult, reads=["NTt", ("pg", 2)], writes=["AT"])
            trp(R(0, 2), AT[:], ["AT"])
            kb.v("dve", "tensor_copy", A[:], R(0, 2)[0], reads=[("pg", 0)], writes=["A"])
            kb.v("dve", "tensor_tensor", RT[:], idf[:], AT[:], ALU.subtract, reads=["idf", "AT"], writes=["RT"])
            P, PT, Pk, PTk = A, AT, "A", "AT"
            for lv in range(1, 6):
                Pn, PTn = Pa[lv % 2], PTa[lv % 2]
                Pnk, PTnk = ("Pa", lv % 2), ("PTa", lv % 2)
                mm1(R(3, 0), PT[:], P[:], [Pk, PTk])
                kb.v("dve", "tensor_copy", Pn[:], R(3, 0)[0], reads=[("pg", 3)], writes=[Pnk])
                if lv < 5:
                    mm1(R(3, 1), P[:], PT[:], [Pk, PTk])
                    kb.act(PTn[:], R(3, 1)[0], AF.Copy, reads=[("pg", 3)], writes=[PTnk])
                mm1(R(3, 2), Pn[:], RT[:], [Pnk, "RT"])
                kb.v("dve", "tensor_tensor", RT[:], RT[:], R(3, 2)[0], ALU.add, reads=["RT", ("pg", 3)], writes=["RT"])
                P, PT, Pk, PTk = Pn, PTn, Pnk, PTnk
            kb.v("dve", "tensor_scalar", vb[:], vtok[:], beta[:, j:j + 1], None, ALU.mult, reads=["vtok", "beta"], writes=["vb"])
            kb.v("dve", "tensor_scalar", kbt[:], ktok[:], bcol[:, 0:1], None, ALU.mult, reads=["ktok", "bcol"], writes=["kbt"])
            mm1(R(2, 3), RT[:], vb[:], ["RT", "vb"])
            kb.v("dve", "tensor_copy", u[:], R(2, 3)[0], reads=[("pg", 2)], writes=["u"])
            mm1(R(3, 3), kbt[:], RT[:], ["kbt", "RT"])
            kb.act(wT[:], R(3, 3)[0], AF.Copy, reads=[("pg", 3)], writes=["wT"])
            mm1(R(2, 2), kn, qn, [("qkv", 0), ("qkv", 1)])
            kb.v("dve", "tensor_tensor", attnT[:], ET[:], R(2, 2)[0], ALU.mult, reads=["ET", ("pg", 2)], writes=["attnT"])
            kb.v("dve", "tensor_tensor", qdT[:], qn, egcbc[:], ALU.mult, reads=[("qkv", 0), "egcbc"], writes=["qdT"])
            kb.v("dve", "tensor_scalar", kst[:], ktok[:], ksc[:, 0:1], None, ALU.mult, reads=["ktok", "ksc"], writes=["kst"])
            for c in range(2):
                cs = slice(c * 64, (c + 1) * 64)
                mm1(R(1, 0), wT[:], S[:], ["wT", "S"])
                kb.v("dve", "tensor_tensor", vnew[cs, :], u[cs, :], pg[1][cs, 0:128], ALU.subtract, reads=["u", ("pg", 1)], writes=["vnew"])
                kb.mm(R(1, 1)[0], qdT[:], S[:], start=True, stop=False, reads=["qdT", "S"], writes=[("pg", 1)])
                kb.mm(R(1, 1)[0], attnT[:], vnew[:], start=False, stop=True, reads=["attnT", "vnew"], writes=[("pg", 1)])
                kb.v("dve", "tensor_copy", osb[cs, :], pg[1][cs, 128:256], reads=[("pg", 1)], writes=["osb"])
                mm1(R(1, 2), kst[cs, :], vnew[cs, :], ["kst", "vnew"])
                kb.v("dve", "scalar_tensor_tensor", S[:], S[:], eglbc[:, c * 64:c * 64 + 1], R(1, 2)[0], ALU.mult, ALU.add,
                     reads=["S", "eglbc", ("pg", 1)], writes=["S"])
            kb.act(junk[:], osb[:], AF.Square, accum_out=ssq[:], reads=["osb"], writes=["junk", "ssq"])
            kb.v("dve", "tensor_scalar", rstd[:], ssq[:], 1.0 / 128, EPS, ALU.mult, ALU.add, reads=["ssq"], writes=["rstd"])
            kb.act(rstd[:], rstd[:], AF.Sqrt, reads=["rstd"], writes=["rstd"])
            kb.v("dve", "reciprocal", rstd[:], rstd[:], reads=["rstd"], writes=["rstd"])
            kb.v("dve", "scalar_tensor_tensor", osb[:], osb[:], rstd[:, 0:1], nrmw[:], ALU.mult, ALU.mult, reads=["osb", "rstd", "nrmw"], writes=["osb"])
            kb.v("dve", "tensor_tensor", osb[:], osb[:], zs[:, j, :], ALU.mult, reads=["osb", "zs"], writes=["osb"])
            kb.v("dve", "tensor_copy", otk[:, j, :], osb[:], reads=["osb"], writes=["otk"])
        for j in range(4):
            kb.tr(m.ptr[0][:, j, :], otk[:, j, :], m.idb[:], reads=["otk", "idb"], writes=[("ptr", 0)])
        kb.v("dve", "tensor_copy", oT[:], m.ptr[0][:], reads=[("ptr", 0)], writes=["oT"])
        kb.dma(o_d[:, t0:t0 + TBM], oT[:].rearrange("p j t -> p (j t)"), reads=["oT"])
    return m.finish()


def chunk_consts():
    j = np.arange(128)
    same = (j[:, None] // 64) == (j[None, :] // 64)
    return {"triblk": (same & (j[:, None] <= j[None, :])).astype(np.float32), "blkones": same.astype(np.float32),
            "tristrict": (same & (j[:, None] < j[None, :])).astype(np.float32)}


def gdn_maps(inputs, xs, ident):
    f32 = np.float32
    W = np.asarray(inputs["ab_w_in"][0], f32)
    cwf = np.asarray(inputs["ab_conv_w"][0], f32)
    maps = []
    for c in range(8):
        b, g = c // 4, c % 4
        ar = np.arange(g * 128, (g + 1) * 128)
        cols = np.concatenate([1536 + ar, 2048 + ar, 2560 + ar, 3080 + ar, [3072 + g], [3076 + g]])
        convw = np.stack([cwf[:, ch].T for ch in (ar, 512 + ar, 1024 + ar)], 1).reshape(128, 12)
        one = lambda v: np.full((128, 1), v, f32)
        mm = {"x": np.ascontiguousarray(xs[b]), "w_in": np.ascontiguousarray(W[:, cols]),
              "norm_w": np.asarray(inputs["norm_mix"][0], f32)[None, :], "ident": ident,
              "convw": np.ascontiguousarray(convw), "dtb": one(inputs["ab_dt_bias"][0][g]), "alog": one(inputs["ab_a_log"][0][g]),
              "nrmw": np.ascontiguousarray(np.broadcast_to(np.asarray(inputs["ab_norm_w"][0], f32)[None, :], (128, 128)))}
        mm.update(chunk_consts())
        maps.append(mm)
    return maps


def build_program():
    nc = bass.Bass("TRN2", target_bir_lowering=False)
    kb = KB(nc)
    _CTX.clear()
    _CTX.update(nc=nc, kb=kb)
    x_d = nc.dram_tensor("x", [T, D], F32, kind="ExternalInput").ap()
    y_d = nc.dram_tensor("y", [T, D], F32, kind="ExternalOutput").ap()
    omT0 = nc.dram_tensor("omT0", [1024, T], BF16).ap()
    omT1 = nc.dram_tensor("omT1", [1024, T], BF16).ap()
    x1 = nc.dram_tensor("x1", [T, D], F32).ap()

    def run(tag, fn, **ctx):
        _CTX.update(tag=tag, **ctx)
        fn()

    for g in range(4):
        run("band%d" % g, build_band, x=x_d, out=omT0[g * 128:(g + 1) * 128, :])
    for g in range(4):
        run("gdn%d" % g, build_gdn, x=x_d, out=omT0[512 + g * 128:512 + (g + 1) * 128, :])
    run("tok0", lambda: build_tok(False), x=x_d, omT=omT0, out=x1)
    for g in range(4):
        run("fox%d" % g, build_fox, x=x1, out=omT1[512 + g * 128:512 + (g + 1) * 128, :])
    for g in range(4):
        run("ssd%d" % g, build_ssd, x=x1, out=omT1[g * 128:(g + 1) * 128, :])
    run("tok1", lambda: build_tok(True), x=x1, omT=omT1, out=y_d)
    kb.close()
    return nc


def host_maps(inputs, b):
    f32 = np.float32
    A = lambda k: np.asarray(inputs[k], f32)
    ident = np.eye(128, dtype=f32)
    bc = lambda v: np.ascontiguousarray(np.broadcast_to(np.asarray(v, f32)[None, :], (128, len(v))))
    m = {"x": np.ascontiguousarray(A("x")[b])}

    def put(tag, d):
        for k, v in d.items():
            m["%s_%s" % (k, tag)] = v

    W0 = A("ab_w_in")[0]; rb = A("ab_rel_bias")[0]
    W1 = A("cd_w_in")[0]
    nm0 = A("norm_mix")[0][None, :]; nm1 = A("norm_mix")[1][None, :]
    gm = gdn_maps(inputs, [None, None], ident)
    cwf = A("cd_conv_w")[0]; cbf = A("cd_conv_b")[0]
    cc = chunk_consts(); fc = fox_consts()
    for g in range(4):
        ar = np.arange(g * 128, (g + 1) * 128)
        put("band%d" % g, {"w_in": np.ascontiguousarray(W0[:, np.concatenate([ar, 512 + ar, 1024 + ar])]), "norm_w": nm0,
                           "ident": ident, "biasT": band_bias(rb[2 * g:2 * g + 2])})
        d = dict(gm[g]); d.pop("x")
        put("gdn%d" % g, d)
        cols = np.concatenate([1544 + ar, 2056 + ar, 2568 + ar, 3080 + np.arange(2 * g, 2 * g + 2)])
        d = {"w_in": np.ascontiguousarray(W1[:, cols]), "norm_w": nm1, "f_bias": bc(A("cd_f_bias")[0][2 * g:2 * g + 2])}
        d.update(fc)
        put("fox%d" % g, d)
        G = g // 2
        Bch = 512 + G * 128 + np.arange(128); Cch = 768 + G * 128 + np.arange(128)
        cols = np.concatenate([512 + ar, 512 + Bch, 512 + Cch, ar, 1536 + np.arange(2 * g, 2 * g + 2)])
        convw = np.stack([cwf[:, ch].T for ch in (ar, Bch, Cch)], 1).reshape(128, 12)
        convb = np.stack([cbf[ch] for ch in (ar, Bch, Cch)], 1)
        put("ssd%d" % g, {"w_in": np.ascontiguousarray(W1[:, cols]), "norm_w": nm1, "ident": ident,
                          "convw": np.ascontiguousarray(convw), "convb": np.ascontiguousarray(convb),
                          "dtb": bc(A("cd_dt_bias")[0][2 * g:2 * g + 2]), "alog": bc(A("cd_a_log")[0][2 * g:2 * g + 2]),
                          "dskip": bc(A("cd_d_skip")[0][2 * g:2 * g + 2]), "triblk": cc["triblk"], "blkones": cc["blkones"]})
    for L, wout in ((0, A("ab_w_out")[0]), (1, A("cd_w_out")[0])):
        d = {"w_out": wout, "w_gate": A("ffn_w_gate")[L], "w_up": A("ffn_w_up")[L], "w_down": A("ffn_w_down")[L],
             "norm_ffn": A("norm_ffn")[L][None, :], "ident": ident}
        if L == 1:
            d["norm_final"] = A("norm_final")[None, :]
            d["gnw"] = np.ascontiguousarray(A("cd_norm_w")[0].reshape(4, 128).T)
        put("tok%d" % L, d)
    return m


def kernel(**inputs):
    nc = build_program()
    maps = [host_maps(inputs, b) for b in range(2)]
    res = run_bass_kernel_spmd(nc, maps, core_ids=[0, 1])
    out = np.stack([np.asarray(res.results[b]["y"]) for b in range(2)], 0)
    return out.astype(np.float32)
```

```python
import contextlib
import os
import numpy as np
import ml_dtypes
import concourse.bass as bass
import concourse.mybir as mybir
from concourse.bass_utils import run_bass_kernel_spmd

_CTX = {}

F32 = mybir.dt.float32
BF16 = mybir.dt.bfloat16
AF = mybir.ActivationFunctionType
ALU = mybir.AluOpType
AX = mybir.AxisListType

ENGS = ["pe", "dve", "act", "pool", "sp"]
EPOCH = 30000
NDMA = 24


class KB:
    def __init__(self, nc):
        self.nc = nc
        self.ops = {e: [] for e in ENGS}
        self.cnt = {e: 0 for e in ENGS}
        self.lastw = {}
        self.readers = {}
        self.seen = {e: {} for e in ENGS}
        self.dma_i = 0
        self.semnames = set()
        self.dma_tokens = {}

    def _deps(self, eng, reads, writes):
        toks = {}
        def add(t):
            if t is None:
                return
            s, v = t
            if toks.get(s, 0) < v:
                toks[s] = v
        for k in reads:
            add(self.lastw.get(k))
        for k in writes:
            add(self.lastw.get(k))
            for s, v in self.readers.get(k, {}).items():
                add((s, v))
        waits = []
        for s, v in toks.items():
            if eng == "pe" and s.startswith("pe"):
                continue
            if self.seen[eng].get(s, 0) >= v:
                continue
            self.seen[eng][s] = v
            waits.append((s, v))
        return waits

    def _commit(self, tok, reads, writes):
        for k in reads:
            d = self.readers.setdefault(k, {})
            if d.get(tok[0], 0) < tok[1]:
                d[tok[0]] = tok[1]
        for k in writes:
            self.lastw[k] = tok
            self.readers[k] = {}

    def op(self, eng, fn, reads=(), writes=()):
        waits = self._deps(eng, reads, writes)
        n = self.cnt[eng]
        self.cnt[eng] = n + 1
        s = "%s%d" % (eng, n // EPOCH)
        self.semnames.add(s)
        tok = (s, n % EPOCH + 1)
        self.ops[eng].append((waits, fn, (s, 1)))
        self._commit(tok, reads, writes)
        return tok

    def dma(self, out, in_, reads=(), writes=(), eng="sp", **kw):
        i = self.dma_i
        self.dma_i += 1
        s = "dma%d" % (i % NDMA)
        self.semnames.add(s)
        waits = self._deps(eng, reads, writes)
        prev = 16 * (i // NDMA)
        if prev > 0 and self.seen[eng].get(s, 0) < prev:
            self.seen[eng][s] = prev
            waits.append((s, prev))
        tok = (s, prev + 16)
        self.ops[eng].append((waits, lambda e: e.dma_start(out=out, in_=in_, **kw), (s, 16)))
        self._commit(tok, reads, writes)
        self.dma_tokens[s] = tok[1]
        return tok

    def mm(self, out, lhsT, rhs, start=True, stop=True, reads=(), writes=(), **kw):
        return self.op("pe", lambda e: e.matmul(out, lhsT, rhs, start=start, stop=stop, **kw), reads, writes)

    def tr(self, out, in_, ident, reads=(), writes=()):
        return self.op("pe", lambda e: e.transpose(out, in_, ident), reads, writes)

    def act(self, out, in_, func, reads=(), writes=(), **kw):
        return self.op("act", lambda e: e.activation(out, in_, func, **kw), reads, writes)

    def v(self, eng, name, *args, reads=(), writes=(), **kw):
        return self.op(eng, lambda e: getattr(e, name)(*args, **kw), reads, writes)

    def _sem(self, s):
        if s not in self.sems:
            self.sems[s] = self.semstack.enter_context(self.nc.semaphore(s))
        return self.sems[s]

    def flush(self):
        nc = self.nc
        if not hasattr(self, "sems"):
            self.sems = {}
            self.semstack = contextlib.ExitStack()
        finals = []
        for e in ENGS:
            if e == "sp" or self.cnt[e] == 0:
                continue
            n = self.cnt[e] - 1
            finals.append(("%s%d" % (e, n // EPOCH), n % EPOCH + 1))
        for s, v in self.dma_tokens.items():
            finals.append((s, v))
        for s in sorted(self.semnames):
            self._sem(s)
        sems = self.sems
        ops = self.ops
        with nc.Block() as block:
            def replay(eng, lst):
                for waits, fn, inc in lst:
                    for s, v in waits:
                        eng.wait_ge(sems[s], v)
                    fn(eng).then_inc(sems[inc[0]], inc[1])
                for s, v in finals:
                    eng.wait_ge(sems[s], v)

            @block.tensor
            def _(eng):
                replay(eng, ops["pe"])

            @block.vector
            def _(eng):
                replay(eng, ops["dve"])

            @block.scalar
            def _(eng):
                replay(eng, ops["act"])

            @block.gpsimd
            def _(eng):
                replay(eng, ops["pool"])

            @block.sync
            def _(eng):
                replay(eng, ops["sp"])
        self.ops = {e: [] for e in ENGS}
        self.lastw = {}
        self.readers = {}
        for e in ENGS:
            for s, v in finals:
                if self.seen[e].get(s, 0) < v:
                    self.seen[e][s] = v

    def close(self):
        self.semstack.close()


D = 1024
T = 8192
TBM = 512
NBLK = T // TBM
EPS = 1e-6


class Mix:
    def __init__(self, ncols, xdtype_norm=True):
        self.nc = nc = _CTX["nc"]
        self.kb = _CTX["kb"]
        self.tag = _CTX["tag"]
        self.st = contextlib.ExitStack()
        self.ncols = ncols
        self.x_d = self.dram("x", [T, D])
        self.w_d = self.dram("w_in", [D, ncols])
        self.nw_d = self.dram("norm_w", [1, D])
        self.id_d = self.dram("ident", [128, 128])
        kb = self.kb
        self.w = self.sb("w", [128, 8, ncols], BF16)
        self.nw = self.sb("nw", [128, D])
        self.idf = self.sb("idf", [128, 128])
        self.idb = self.sb("idb", [128, 128], BF16)
        self.xt = self.sb("xt", [128, 4, D])
        self.hb = self.sb("hb", [128, 4, D], BF16)
        self.hT = self.sb("hT", [128, 8, TBM], BF16)
        self.ss = self.sb("ss", [128, 4])
        self.rs = self.sb("rs", [128, 4])
        self.ptr = [self.ps("ptr%d" % i, [128, 4, 128], BF16) for i in range(2)]
        kb.dma(self.idf[:], self.id_d[:, :], writes=["idf"])
        kb.v("dve", "tensor_copy", self.idb[:], self.idf[:], reads=["idf"], writes=["idb"])
        kb.dma(self.nw[:], self.nw_d[0:1, :].to_broadcast([128, D]), writes=["nw"])
        stg = self.xt
        i = 0
        for kc in range(8):
            for n0 in range(0, ncols, 1024):
                n1 = min(ncols, n0 + 1024)
                s = i % 4
                i += 1
                kb.dma(stg[:, s, 0:n1 - n0], self.w_d[kc * 128:(kc + 1) * 128, n0:n1], writes=[("xt", s)])
                eng = ["dve", "pool"][s % 2]
                kb.v(eng, "tensor_copy", self.w[:, kc, n0:n1], stg[:, s, 0:n1 - n0], reads=[("xt", s)], writes=["w"])

    def dram(self, name, shape, dt=F32, kind="ExternalInput"):
        if kind == "ExternalOutput":
            return _CTX["out"]
        if name == "x":
            return _CTX["x"]
        return self.nc.dram_tensor("%s_%s" % (name, self.tag), shape, dt, kind=kind).ap()

    def sb(self, name, shape, dt=F32):
        return self.st.enter_context(self.nc.sbuf_tensor("%s_%s" % (self.tag, name), shape, dt))

    def ps(self, name, shape, dt=F32):
        return self.st.enter_context(self.nc.psum_tensor("%s_%s" % (self.tag, name), shape, dt))

    def const(self, name, shape, bf=False):
        d = self.dram(name, shape)
        t = self.sb(name + "_s", shape)
        self.kb.dma(t[:], d, writes=[name])
        if bf:
            tb = self.sb(name + "_b", shape, BF16)
            self.kb.v("dve", "tensor_copy", tb[:], t[:], reads=[name], writes=[name + "_b"])
            return t, tb
        return t

    def frontend(self, blk):
        kb = self.kb
        xt, hb, hT, ss, rs = self.xt, self.hb, self.hT, self.ss, self.rs
        t0 = blk * TBM
        for j in range(4):
            kb.dma(xt[:, j, :], self.x_d[t0 + j * 128:t0 + (j + 1) * 128, :], writes=[("xt", j)])
        for j in range(4):
            kb.act(hb[:, j, :], xt[:, j, :], AF.Square, accum_out=ss[:, j:j + 1], reads=[("xt", j)], writes=[("hb", j), ("ss", j)])
            kb.v("dve", "tensor_scalar", rs[:, j:j + 1], ss[:, j:j + 1], 1.0 / D, EPS, ALU.mult, ALU.add,
                 reads=[("ss", j)], writes=[("rs", j)])
            kb.act(rs[:, j:j + 1], rs[:, j:j + 1], AF.Sqrt, reads=[("rs", j)], writes=[("rs", j)])
            kb.v("dve", "reciprocal", rs[:, j:j + 1], rs[:, j:j + 1], reads=[("rs", j)], writes=[("rs", j)])
            kb.v("dve", "scalar_tensor_tensor", hb[:, j, :], xt[:, j, :], rs[:, j:j + 1], self.nw[:], ALU.mult, ALU.mult,
                 reads=[("xt", j), ("rs", j), "nw"], writes=[("hb", j)])
            for q4 in range(2):
                pt = self.ptr[q4]
                for i in range(4):
                    kc = q4 * 4 + i
                    kb.tr(pt[:, i, :], hb[:, j, kc * 128:(kc + 1) * 128], self.idb[:], reads=[("hb", j), "idb"], writes=[("ptr", q4)])
                if q4 == 0:
                    kb.v("dve", "tensor_copy", hT[:, 0:4, j * 128:(j + 1) * 128], pt[:], reads=[("ptr", q4)], writes=["hT"])
                else:
                    kb.act(hT[:, 4:8, j * 128:(j + 1) * 128], pt[:], AF.Copy, reads=[("ptr", q4)], writes=["hT"])

    def proj_fm(self, psum, pkey, c0, ncol=128):
        for kc in range(8):
            self.kb.mm(psum, self.w[:, kc, c0:c0 + ncol], self.hT[:, kc, :], start=(kc == 0), stop=(kc == 7),
                       reads=["w", "hT"], writes=[pkey])

    def proj_tm(self, psum, pkey, j, c0, ncol):
        for kc in range(8):
            self.kb.mm(psum, self.hT[:, kc, j * 128:(j + 1) * 128], self.w[:, kc, c0:c0 + ncol], start=(kc == 0), stop=(kc == 7),
                       reads=["w", "hT"], writes=[pkey])

    def finish(self):
        self.kb.flush()
        self.st.close()


D = 1024
DFF = 2816
NF = DFF // 128
TOK = 8192
TB = 256
EPS = 1e-6


def build_tok(final):
    nc = _CTX["nc"]
    tag = _CTX["tag"]
    def dr(name, shape, dt=F32, kind="ExternalInput"):
        if name in ("x", "omT"):
            return _CTX[name]
        if kind == "ExternalOutput":
            return _CTX["out"]
        return nc.dram_tensor("%s_%s" % (name, tag), shape, dt, kind=kind).ap()
    x_d = dr("x", [TOK, D])
    om_d = dr("omT", [D, TOK], BF16)
    wo_d = dr("w_out", [D, D])
    wg_d = dr("w_gate", [D, DFF])
    wu_d = dr("w_up", [D, DFF])
    wd_d = dr("w_down", [DFF, D])
    nf_d = dr("norm_ffn", [1, D])
    id_d = dr("ident", [128, 128])
    if final:
        nfin_d = dr("norm_final", [1, D])
        gnw_d = dr("gnw", [128, 4])
    y_d = dr("y", [TOK, D], F32, "ExternalOutput")
    kb = _CTX["kb"]
    with contextlib.ExitStack() as st:
        sb = lambda name, shape, dt=F32: st.enter_context(nc.sbuf_tensor("%s_%s" % (tag, name), shape, dt))
        ps = lambda name, shape, dt=F32: st.enter_context(nc.psum_tensor("%s_%s" % (tag, name), shape, dt))
        wo = sb("wo", [128, 8, D], BF16)
        wg = sb("wg", [128, 8, DFF], BF16)
        wu = sb("wu", [128, 8, DFF], BF16)
        wd = sb("wd", [128, NF, D], BF16)
        stg = [sb("stg%d" % i, [128, 1024]) for i in range(3)]
        nfw = sb("nfw", [128, D])
        idf = sb("idf", [128, 128]); idb = sb("idb", [128, 128], BF16)
        ones = sb("ones", [128, 128], BF16)
        xt = sb("xt", [128, 2, D])
        om = sb("om", [128, 8, TB], BF16)
        hb = sb("hb", [128, 2, D], BF16)
        hT = sb("hT", [128, 8, TB], BF16)
        actT = sb("actT", [128, NF, TB], BF16)
        sg = [sb("sg%d" % i, [128, TB]) for i in range(2)]
        ss = sb("ss", [128, 4]); rs = sb("rs", [128, 4])
        if final:
            nfin = sb("nfin", [128, D]); gnw = sb("gnw_s", [128, 4])
            sq = sb("sq", [128, 4, TB], BF16); rg = sb("rg", [128, TB])
        pmm = [ps("pmm%d" % i, [128, 512]) for i in range(2)]
        pg = [ps("pg%d" % i, [128, 512]) for i in range(2)]
        pu = [ps("pu%d" % i, [128, 512]) for i in range(2)]
        ptr = [ps("ptr%d" % i, [128, 4, 128], BF16) for i in range(2)]

        kb.dma(idf[:], id_d[:, :], writes=["idf"])
        kb.v("dve", "tensor_copy", idb[:], idf[:], reads=["idf"], writes=["idb"])
        kb.v("dve", "memset", ones[:], 1.0, writes=["ones"])
        kb.dma(nfw[:], nf_d[0:1, :].to_broadcast([128, D]), writes=["nfw"])
        if final:
            kb.dma(nfin[:], nfin_d[0:1, :].to_broadcast([128, D]), writes=["nfin"])
            kb.dma(gnw[:], gnw_d[:, :], writes=["gnw"])
        cvt_i = [0]
        def load_w(dst, src, K, N, name):
            for kc in range(K // 128):
                for n0 in range(0, N, 1024):
                    n1 = min(N, n0 + 1024)
                    i = cvt_i[0] % 3
                    cvt_i[0] += 1
                    kb.dma(stg[i][:, 0:n1 - n0], src[kc * 128:(kc + 1) * 128, n0:n1], writes=[("stg", i)])
                    eng = ["dve", "pool", "act"][i]
                    if eng == "act":
                        kb.act(dst[:, kc, n0:n1], stg[i][:, 0:n1 - n0], AF.Copy, reads=[("stg", i)], writes=[name])
                    else:
                        kb.v(eng, "tensor_copy", dst[:, kc, n0:n1], stg[i][:, 0:n1 - n0], reads=[("stg", i)], writes=[name])
        load_w(wo, wo_d, D, D, "wo")
        load_w(wg, wg_d, D, DFF, "wg")
        load_w(wu, wu_d, D, DFF, "wu")
        load_w(wd, wd_d, DFF, D, "wd")

        def rmsnorm_rows(src, j, col, wtile, dst, dst_key):
            kb.act(hb[:, j, :], src, AF.Square, accum_out=ss[:, col:col + 1], reads=["xt"], writes=["hb", ("ss", col)])
            kb.v("dve", "tensor_scalar", rs[:, col:col + 1], ss[:, col:col + 1], 1.0 / D, EPS, ALU.mult, ALU.add,
                 reads=[("ss", col)], writes=[("rs", col)])
            kb.act(rs[:, col:col + 1], rs[:, col:col + 1], AF.Sqrt, reads=[("rs", col)], writes=[("rs", col)])
            kb.v("dve", "reciprocal", rs[:, col:col + 1], rs[:, col:col + 1], reads=[("rs", col)], writes=[("rs", col)])
            kb.v("dve", "scalar_tensor_tensor", dst, src, rs[:, col:col + 1], wtile[:], ALU.mult, ALU.mult,
                 reads=["xt", ("rs", col), "nfw", "nfin"], writes=[dst_key])

        omv = om_d.rearrange("(kc p) t -> p kc t", p=128)
        for blk in range(TOK // TB):
            t0 = blk * TB
            kb.dma(xt[:], x_d[t0:t0 + TB, :].rearrange("(j p) d -> p j d", p=128), writes=["xt"])
            kb.dma(om[:], omv[:, :, t0:t0 + TB], writes=["om"])
            if final:
                kb.v("pool", "tensor_tensor", sq[:], om[:, 0:4, :], om[:, 0:4, :], ALU.mult, reads=["om"], writes=["sq"])
                for G in range(2):
                    p = pmm[G]
                    for i in range(2):
                        kb.mm(p[:, 0:TB], ones[:], sq[:, 2 * G + i, :], start=(i == 0), stop=(i == 1),
                              reads=["ones", "sq"], writes=[("pmm", G)])
                    kb.v("dve", "tensor_scalar", rg[:], p[:, 0:TB], 1.0 / 256, EPS, ALU.mult, ALU.add,
                         reads=[("pmm", G)], writes=["rg"])
                    kb.act(rg[:], rg[:], AF.Sqrt, reads=["rg"], writes=["rg"])
                    kb.v("dve", "reciprocal", rg[:], rg[:], reads=["rg"], writes=["rg"])
                    for i in range(2):
                        kc = 2 * G + i
                        kb.v("dve", "scalar_tensor_tensor", om[:, kc, :], om[:, kc, :], gnw[:, kc:kc + 1], rg[:],
                             ALU.mult, ALU.mult, reads=["om", "rg", "gnw"], writes=["om"])
            for j in range(2):
                for hf in range(2):
                    p = pmm[(2 * j + hf) % 2]
                    key = ("pmm", (2 * j + hf) % 2)
                    for kc in range(8):
                        kb.mm(p[:], om[:, kc, j * 128:(j + 1) * 128], wo[:, kc, hf * 512:(hf + 1) * 512],
                              start=(kc == 0), stop=(kc == 7), reads=["om", "wo"], writes=[key])
                    kb.v("dve", "tensor_tensor", xt[:, j, hf * 512:(hf + 1) * 512], xt[:, j, hf * 512:(hf + 1) * 512], p[:],
                         ALU.add, reads=[key, "xt"], writes=["xt"])
            for j in range(2):
                rmsnorm_rows(xt[:, j, :], j, j, nfw, hb[:, j, :], "hb")
                for q4 in range(2):
                    pt = ptr[q4]
                    for i in range(4):
                        kc = q4 * 4 + i
                        kb.tr(pt[:, i, :], hb[:, j, kc * 128:(kc + 1) * 128], idb[:], reads=["hb", "idb"], writes=[("ptr", q4)])
                    if q4 == 0:
                        kb.v("dve", "tensor_copy", hT[:, 0:4, j * 128:(j + 1) * 128], pt[:], reads=[("ptr", q4)], writes=["hT"])
                    else:
                        kb.act(hT[:, 4:8, j * 128:(j + 1) * 128], pt[:], AF.Copy, reads=[("ptr", q4)], writes=["hT"])
            for f in range(NF):
                b = f % 2
                for kc in range(8):
                    kb.mm(pg[b][:, 0:TB], wg[:, kc, f * 128:(f + 1) * 128], hT[:, kc, :], start=(kc == 0), stop=(kc == 7),
                          reads=["wg", "hT"], writes=[("pg", b)])
                for kc in range(8):
                    kb.mm(pu[b][:, 0:TB], wu[:, kc, f * 128:(f + 1) * 128], hT[:, kc, :], start=(kc == 0), stop=(kc == 7),
                          reads=["wu", "hT"], writes=[("pu", b)])
                kb.act(sg[b][:], pg[b][:, 0:TB], AF.Silu, reads=[("pg", b)], writes=[("sg", b)])
                kb.v("dve", "tensor_tensor", actT[:, f, :], sg[b][:], pu[b][:, 0:TB], ALU.mult,
                     reads=[("sg", b), ("pu", b)], writes=["actT"])
            for j in range(2):
                for hf in range(2):
                    p = pmm[(2 * j + hf) % 2]
                    key = ("pmm", (2 * j + hf) % 2)
                    for f in range(NF):
                        kb.mm(p[:], actT[:, f, j * 128:(j + 1) * 128], wd[:, f, hf * 512:(hf + 1) * 512],
                              start=(f == 0), stop=(f == NF - 1), reads=["actT", "wd"], writes=[key])
                    kb.v("dve", "tensor_tensor", xt[:, j, hf * 512:(hf + 1) * 512], xt[:, j, hf * 512:(hf + 1) * 512], p[:],
                         ALU.add, reads=[key, "xt"], writes=["xt"])
            if final:
                for j in range(2):
                    rmsnorm_rows(xt[:, j, :], j, 2 + j, nfin, xt[:, j, :], "xt")
            kb.dma(y_d[t0:t0 + TB, :].rearrange("(j p) d -> p j d", p=128), xt[:], reads=["xt"])
        kb.flush()


NT = T // 128


def build_band():
    m = Mix(384)
    kb, nc = m.kb, m.nc
    o_d = m.dram("oT", [128, T], BF16, "ExternalOutput")
    bias_d = m.dram("biasT", [128, 10, 128])
    biasf = m.sb("biasf", [128, 10, 128])
    biasb = m.sb("biasb", [128, 10, 128], BF16)
    kb.dma(biasf[:], bias_d, writes=["biasf"])
    kb.v("dve", "tensor_copy", biasb[:], biasf[:], reads=["biasf"], writes=["biasb"])
    qT = m.sb("qT", [128, T], BF16)
    kT = m.sb("kT", [128, T], BF16)
    v = m.sb("v", [128, NT, 2, 65], BF16)
    pT = [m.sb("pT%d" % i, [128, 128], BF16) for i in range(3)]
    osb = m.sb("osb", [128, 4, 128])
    rden = m.sb("rden", [128, 2])
    oT = m.sb("oT_s", [128, 512], BF16)
    pfm = [m.ps("pfm%d" % i, [128, 512]) for i in range(2)]
    ptm = m.ps("ptm", [128, 512])
    pO = [m.ps("pO%d" % i, [128, 2, 128]) for i in range(2)]
    ptf = m.ps("ptf", [128, 4, 128])
    kb.v("dve", "memset", v[:].rearrange("p a b c -> p (a b c)"), 1.0, writes=["v"])
    for blk in range(NBLK):
        m.frontend(blk)
        t0 = blk * TBM
        m.proj_fm(pfm[0][:], ("pfm", 0), 0)
        kb.v("dve", "tensor_scalar", qT[:, t0:t0 + TBM], pfm[0][:], 0.125, None, ALU.mult, reads=[("pfm", 0)], writes=["qT"])
        m.proj_fm(pfm[1][:], ("pfm", 1), 128)
        kb.v("dve", "tensor_copy", kT[:, t0:t0 + TBM], pfm[1][:], reads=[("pfm", 1)], writes=["kT"])
        for j in range(4):
            tile = blk * 4 + j
            m.proj_tm(ptm[:, 0:128], "ptm", j, 256, 128)
            kb.v("dve", "tensor_copy", v[:, tile, :, 0:64], ptm[:, 0:128].rearrange("p (h d) -> p h d", h=2),
                 reads=["ptm"], writes=["v"])
        it = 0
        for j in range(4):
            qt = blk * 4 + j
            po = pO[qt % 2]; pok = ("pO", qt % 2)
            for h in range(2):
                hs = slice(h * 64, (h + 1) * 64)
                deltas = [dl for dl in range(5) if qt - dl >= 0]
                for di, dl in enumerate(deltas):
                    kt = qt - dl
                    ps_ = pfm[it % 2]; psk = ("pfm", it % 2)
                    pt = pT[it % 3]; ptk = ("pT", it % 3)
                    it += 1
                    kb.mm(ps_[:, 0:128], kT[hs, kt * 128:(kt + 1) * 128], qT[hs, qt * 128:(qt + 1) * 128], start=True, stop=False,
                          reads=["kT", "qT"], writes=[psk])
                    kb.mm(ps_[:, 0:128], m.idb[:], biasb[:, h * 5 + dl, :], start=False, stop=True, reads=["idb", "biasb"], writes=[psk])
                    kb.act(pt[:], ps_[:, 0:128], AF.Exp, reads=[psk], writes=[ptk])
                    kb.mm(po[:, h, 0:65], pt[:], v[:, kt, h, :], start=(di == 0), stop=(di == len(deltas) - 1),
                          reads=[ptk, "v"], writes=[pok])
                kb.v("dve", "reciprocal", rden[:, h:h + 1], po[:, h, 64:65], reads=[pok], writes=["rden"])
                kb.v("dve", "tensor_scalar", osb[:, j, hs], po[:, h, 0:64], rden[:, h:h + 1], None, ALU.mult,
                     reads=[pok, "rden"], writes=["osb"])
            kb.op("pe", lambda e, j=j: e.transpose(ptf[:, j, :], osb[:, j, :], m.idf[:]), reads=["osb", "idf"], writes=["ptf"])
        kb.v("dve", "tensor_copy", oT[:], ptf[:].rearrange("p s q -> p (s q)"), reads=["ptf"], writes=["oT"])
        kb.dma(o_d[:, t0:t0 + TBM], oT[:], reads=["oT"])
    return m.finish()


def band_bias(rb2):
    kk = np.arange(128)[:, None]; qq = np.arange(128)[None, :]
    out = np.empty((128, 10, 128), np.float32)
    for h in range(2):
        for dl in range(5):
            dist = 128 * dl + qq - kk
            idx = np.clip(dist, -256, 256) + 256
            cd = 2 * dl + qq // 64 - kk // 64
            valid = (cd >= 0) & (cd <= 8)
            out[:, h * 5 + dl, :] = np.where(valid, rb2[h][idx], np.float32(-30000.0))
    return out


NT = T // 128


def build_fox():
    m = Mix(386)
    kb, nc = m.kb, m.nc
    fb_d = m.dram("f_bias", [128, 2])
    o_d = m.dram("oT", [128, T], BF16, "ExternalOutput")
    trif = m.const("trif", [128, 128])
    umat = m.const("umat", [128, 128])
    sel = m.const("sel127", [128, 128])
    onesf = m.sb("onesf", [128, 128])
    kb.v("dve", "memset", onesf[:], 1.0, writes=["onesf"])
    trib = m.sb("trib", [128, 128], BF16)
    kb.v("dve", "tensor_copy", trib[:], trif[:], reads=["trif"], writes=["trib"])
    nfb = m.sb("nfb", [128, 2])
    kb.dma(nfb[:], fb_d[:, :], writes=["nfb"])
    kb.v("dve", "tensor_scalar", nfb[:], nfb[:], -1.0, None, ALU.mult, reads=["nfb"], writes=["nfb"])

    qT = m.sb("qT", [128, T], BF16)
    kT = m.sb("kT", [128, T], BF16)
    v = m.sb("v", [128, NT, 2, 128], BF16)
    lf = m.sb("lf", [128, 2, NT])
    F = m.sb("F", [128, 2, NT])
    totT = m.sb("totT", [128, 128])
    flast = m.sb("flast", [128, 2, NT])
    nbias = m.sb("nbias", [128, NT])
    dd = m.sb("dd", [128, 2, NT]); dhi = m.sb("dhi", [128, 2, NT], BF16); dhf = m.sb("dhf", [128, 2, NT]); dlo = m.sb("dlo", [128, 2, NT], BF16)
    frow = m.sb("frow", [128, 2, T], BF16)
    ones2 = m.sb("ones2", [128, 128], BF16); onesr = m.sb("onesr", [128, 128], BF16)
    pT = [m.sb("pT%d" % i, [128, 512], BF16) for i in range(3)]
    osb = m.sb("osb", [128, 512])
    rden = m.sb("rden", [64, 512])
    oT = [m.sb("oT_s%d" % i, [64, 512], BF16) for i in range(2)]
    sel65 = m.sb("sel65", [128, 64])
    kb.v("dve", "memset", sel65[:], 0.0, writes=["sel65"])
    kb.v("dve", "memset", sel65[64:65, :], 1.0, writes=["sel65"])
    pfm = [m.ps("pfm%d" % i, [128, 512]) for i in range(2)]
    ptm = m.ps("ptm", [128, 512])
    pO = [m.ps("pO%d" % i, [128, 512]) for i in range(2)]
    ptf = m.ps("ptf", [128, 512])

    kb.v("dve", "memset", v[:].rearrange("p a b c -> p (a b c)"), 0.0, writes=["v"])
    kb.v("dve", "memset", v[:, :, :, 64:65], 1.0, writes=["v"])
    kb.v("dve", "memset", lf[:].rearrange("p h t -> p (h t)"), 0.0, writes=["lf"])
    for blk in range(NBLK):
        m.frontend(blk)
        t0 = blk * TBM
        m.proj_fm(pfm[0][:], ("pfm", 0), 0)
        kb.v("dve", "tensor_scalar", qT[:, t0:t0 + TBM], pfm[0][:], 0.125, None, ALU.mult, reads=[("pfm", 0)], writes=["qT"])
        m.proj_fm(pfm[1][:], ("pfm", 1), 128)
        kb.v("dve", "tensor_copy", kT[:, t0:t0 + TBM], pfm[1][:], reads=[("pfm", 1)], writes=["kT"])
        for j in range(4):
            tile = blk * 4 + j
            m.proj_tm(ptm[:, 0:130], "ptm", j, 256, 130)
            kb.v("dve", "tensor_copy", v[:, tile, :, 0:64], ptm[:, 0:128].rearrange("p (h d) -> p h d", h=2),
                 reads=["ptm"], writes=["v"])
            kb.v("dve", "tensor_copy", lf[:, :, tile], ptm[:, 128:130], reads=["ptm"], writes=["lf"])
    for h in range(2):
        kb.act(lf[:, h, :], lf[:, h, :], AF.Exp, scale=-1.0, bias=nfb[:, h:h + 1], reads=["lf", "nfb"], writes=["lf"])
    kb.act(lf[:], lf[:], AF.Ln, bias=1.0, reads=["lf"], writes=["lf"])
    kb.v("dve", "tensor_scalar", lf[:], lf[:], -1.0, None, ALU.mult, reads=["lf"], writes=["lf"])
    lf2 = lf[:].rearrange("p h t -> p (h t)")
    kb.mm(ptm[:, 0:128], lf2, onesf[:], reads=["lf", "onesf"], writes=["ptm"])
    kb.v("dve", "tensor_copy", totT[:], ptm[:, 0:128], reads=["ptm"], writes=["totT"])
    kb.mm(ptm[:, 0:128], trif[:], lf2, start=True, stop=False, reads=["trif", "lf"], writes=["ptm"])
    kb.mm(ptm[:, 0:128], totT[:], umat[:], start=False, stop=True, reads=["totT", "umat"], writes=["ptm"])
    kb.v("dve", "tensor_copy", F[:].rearrange("p h t -> p (h t)"), ptm[:, 0:128], reads=["ptm"], writes=["F"])
    kb.mm(ptm[:, 0:128], sel[:], F[:].rearrange("p h t -> p (h t)"), reads=["sel127", "F"], writes=["ptm"])
    kb.v("dve", "tensor_copy", flast[:].rearrange("p h t -> p (h t)"), ptm[:, 0:128], reads=["ptm"], writes=["flast"])

    fl4 = flast[:].rearrange("p h (b s) -> p h b s", s=4)
    dd4 = dd[:].rearrange("p h (b s) -> p h b s", s=4)
    for s_ in range(4):
        kb.v("dve", "tensor_tensor", dd4[:, :, :, s_], fl4[:, :, :, s_], fl4[:, :, :, 3], ALU.subtract, reads=["flast"], writes=["dd"])
    kb.v("dve", "tensor_copy", dhi[:], dd[:], reads=["dd"], writes=["dhi"])
    kb.v("dve", "tensor_copy", dhf[:], dhi[:], reads=["dhi"], writes=["dhf"])
    kb.v("dve", "tensor_tensor", dd[:], dd[:], dhf[:], ALU.subtract, reads=["dd", "dhf"], writes=["dd"])
    kb.v("dve", "tensor_copy", dlo[:], dd[:], reads=["dd"], writes=["dlo"])
    kb.v("dve", "memset", frow[:].rearrange("p h t -> p (h t)"), 0.0, writes=["frow"])
    kb.v("dve", "memset", ones2[:], 0.0, writes=["ones2"])
    kb.v("dve", "memset", ones2[0:1, :], 1.0, writes=["ones2"])
    kb.v("dve", "memset", ones2[32:33, :], 1.0, writes=["ones2"])
    kb.v("dve", "memset", onesr[:], 1.0, writes=["onesr"])
    for h in range(2):
        for t_ in range(NT):
            kb.v("dve", "tensor_scalar", frow[0:1, h, t_ * 128:(t_ + 1) * 128], onesr[0:1, :], dhi[0:1, h, t_:t_ + 1], None, ALU.mult,
                 reads=["onesr", "dhi"], writes=["frow"])
            kb.v("pool", "tensor_scalar", frow[32:33, h, t_ * 128:(t_ + 1) * 128], onesr[32:33, :], dlo[32:33, h, t_:t_ + 1], None, ALU.mult,
                 reads=["onesr", "dlo"], writes=["frow"])
    pairs = []
    for qb in range(NBLK):
        for h in range(2):
            nkt = 4 * qb + 4
            for kt in range(nkt):
                pairs.append((qb, h, kt, nkt, len(pairs)))

    def emit_scores(p):
        qb, h, kt, nkt, it = p
        hs = slice(h * 64, (h + 1) * 64)
        q0 = qb * TBM
        j = max(0, kt - 4 * qb)
        n = TBM - 128 * j
        ps_ = pfm[it % 2]; psk = ("pfm", it % 2)
        kb.mm(ps_[:, 0:n], kT[hs, kt * 128:(kt + 1) * 128], qT[hs, q0 + 128 * j:q0 + TBM], start=True, stop=False,
              reads=["kT", "qT"], writes=[psk])
        kb.mm(ps_[:, 0:n], ones2[:], frow[:, h, q0 + 128 * j:q0 + TBM], start=False, stop=True,
              reads=["ones2", "frow"], writes=[psk])

    def emit_rest(p):
        qb, h, kt, nkt, it = p
        q0 = qb * TBM
        j = max(0, kt - 4 * qb)
        n = TBM - 128 * j
        ps_ = pfm[it % 2]; psk = ("pfm", it % 2)
        pt = pT[it % 3]; ptk = ("pT", it % 3)
        po = pO[(2 * qb + h) % 2]
        pok = ("pO", (2 * qb + h) % 2)
        if kt == 0:
            kb.v("dve", "tensor_scalar", nbias[:, 0:nkt], F[:, h, 0:nkt], flast[:, h, 4 * qb + 3:4 * qb + 4], -1.0,
                 ALU.subtract, ALU.mult, reads=["F", "flast"], writes=["nbias"])
        kb.act(pt[:, 0:n], ps_[:, 0:n], AF.Exp, bias=nbias[:, kt:kt + 1], reads=[psk, "nbias"], writes=[ptk])
        if kt >= 4 * qb:
            kb.v("pool", "tensor_tensor", pt[:, 0:128], pt[:, 0:128], trib[:], ALU.mult, reads=[ptk, "trib"], writes=[ptk])
        kb.mm(po[:, 128 * j:TBM], v[:, kt, h, :], pt[:, 0:n], start=(kt == 0), stop=(kt == nkt - 1),
              reads=[ptk, "v"], writes=[pok])
        if kt == nkt - 1:
            kb.act(osb[:], po[:, :], AF.Copy, reads=[pok], writes=["osb"])
            kb.mm(ptf[0:64, :], sel65[:], osb[:], reads=["sel65", "osb"], writes=["ptf"])
            kb.v("dve", "reciprocal", rden[:], ptf[0:64, :], reads=["ptf"], writes=["rden"])
            kb.v("dve", "tensor_tensor", oT[h][:], osb[0:64, :], rden[:], ALU.mult, reads=["osb", "rden"], writes=[("oT", h)])
            kb.dma(o_d[h * 64:(h + 1) * 64, q0:q0 + TBM], oT[h][:], reads=[("oT", h)])

    if pairs:
        emit_scores(pairs[0])
    for i, p in enumerate(pairs):
        if i + 1 < len(pairs):
            emit_scores(pairs[i + 1])
        emit_rest(p)
    return m.finish()


def fox_consts():
    j = np.arange(128)
    trif = (j[:, None] <= j[None, :]).astype(np.float32)
    hh = j // 64; tt = j % 64
    umat = ((hh[:, None] == hh[None, :]) & (tt[:, None] < tt[None, :])).astype(np.float32)
    sel = np.zeros((128, 128), np.float32); sel[127, :] = 1.0
    return {"trif": trif, "umat": umat, "sel127": sel, "ident": np.eye(128, dtype=np.float32)}


def build_ssd():
    m = Mix(514)
    kb, nc = m.kb, m.nc
    o_d = m.dram("oT", [128, T], BF16, "ExternalOutput")
    cw = m.const("convw", [128, 12])
    cb = m.const("convb", [128, 3])
    dtb = m.const("dtb", [128, 2])
    alog = m.const("alog", [128, 2])
    dsk = m.const("dskip", [128, 2])
    triblk = m.const("triblk", [128, 128])
    blkones = m.const("blkones", [128, 128])
    onesf = m.sb("onesf", [128, 128])
    kb.v("dve", "memset", onesf[:], 1.0, writes=["onesf"])
    na = m.sb("na", [128, 2])
    kb.act(na[:], alog[:], AF.Exp, reads=["alog"], writes=["na"])
    kb.v("dve", "tensor_scalar", na[:], na[:], -1.0, None, ALU.mult, reads=["na"], writes=["na"])
    raw = m.sb("raw", [128, 3, 515])
    acc = m.sb("acc", [128, 512])
    xbc = m.sb("xbc", [128, 3, 512])
    zs = m.sb("zs", [128, 4, 128])
    dt = m.sb("dt", [128, 4, 2])
    da = m.sb("da", [128, 2])
    dar = m.sb("dar", [128, 128])
    dc = m.sb("dc", [128, 2])
    tmp = m.sb("tmp", [128, 128])
    LT = m.sb("LT", [128, 128])
    wT = m.sb("wT", [128, 128])
    xtok = m.sb("xtok", [128, 128]); btok = m.sb("btok", [128, 128])
    bw = m.sb("bw", [128, 2]); Bw = [m.sb("Bw%d" % h, [128, 128]) for h in range(2)]
    edacs = m.sb("edacs", [128, 2])
    cdec = [m.sb("cdec%d" % h, [128, 128]) for h in range(2)]
    stateT = m.sb("stateT", [128, 128])
    ysb = m.sb("ysb", [128, 128])
    oT = m.sb("oT_s", [128, 512], BF16)
    pfm = [m.ps("pfm%d" % i, [128, 512]) for i in range(2)]
    ptm = pfm[1]
    pg = [m.ps("pg%d" % i, [128, 512]) for i in range(4)]
    kb.v("dve", "memset", raw[:].rearrange("p a b -> p (a b)"), 0.0, writes=["raw"])
    kb.v("dve", "memset", stateT[:], 0.0, writes=["stateT"])
    for blk in range(NBLK):
        m.frontend(blk)
        t0 = blk * TBM
        for i in range(3):
            p = pfm[i % 2]; pk = ("pfm", i % 2)
            m.proj_fm(p[:], pk, i * 128)
            kb.act(raw[:, i, 3:515], p[:], AF.Copy, reads=[pk], writes=["raw"])
            kb.v("dve", "tensor_scalar", acc[:], raw[:, i, 0:512], cw[:, 4 * i:4 * i + 1], cb[:, i:i + 1], ALU.mult, ALU.add,
                 reads=["raw", "convw", "convb"], writes=["acc"])
            for k in range(1, 4):
                kb.v("dve", "scalar_tensor_tensor", acc[:], raw[:, i, k:k + 512], cw[:, 4 * i + k:4 * i + k + 1], acc[:],
                     ALU.mult, ALU.add, reads=["raw", "convw", "acc"], writes=["acc"])
            kb.act(xbc[:, i, :], acc[:], AF.Exp, scale=-1.0, reads=["acc"], writes=["xbc"])
            kb.v("dve", "tensor_scalar", xbc[:, i, :], xbc[:, i, :], 1.0, None, ALU.add, reads=["xbc"], writes=["xbc"])
            kb.v("dve", "reciprocal", xbc[:, i, :], xbc[:, i, :], reads=["xbc"], writes=["xbc"])
            kb.v("dve", "tensor_tensor", xbc[:, i, :], xbc[:, i, :], acc[:], ALU.mult, reads=["xbc", "acc"], writes=["xbc"])
            kb.v("dve", "tensor_copy", raw[:, i, 0:3], raw[:, i, 512:515], reads=["raw"], writes=["raw"])
        for j in range(4):
            m.proj_tm(ptm[:, 0:130], ("pfm", 1), j, 384, 130)
            kb.act(zs[:, j, :], ptm[:, 0:128], AF.Exp, scale=-1.0, reads=[("pfm", 1)], writes=["zs"])
            kb.v("dve", "tensor_scalar", zs[:, j, :], zs[:, j, :], 1.0, None, ALU.add, reads=["zs"], writes=["zs"])
            kb.v("dve", "reciprocal", zs[:, j, :], zs[:, j, :], reads=["zs"], writes=["zs"])
            kb.v("dve", "tensor_tensor", zs[:, j, :], zs[:, j, :], ptm[:, 0:128], ALU.mult, reads=["zs", ("pfm", 1)], writes=["zs"])
            kb.v("dve", "tensor_tensor", dt[:, j, :], ptm[:, 128:130], dtb[:], ALU.add, reads=[("pfm", 1), "dtb"], writes=["dt"])
        kb.act(dt[:], dt[:], AF.Exp, reads=["dt"], writes=["dt"])
        kb.act(dt[:], dt[:], AF.Ln, bias=1.0, reads=["dt"], writes=["dt"])
        for j in range(4):
            sl = slice(j * 128, (j + 1) * 128)
            kb.op("pe", lambda e, sl=sl: e.transpose(pg[0][:, 0:128], xbc[:, 0, sl], m.idf[:]), reads=["xbc", "idf"], writes=[("pg", 0)])
            kb.op("pe", lambda e, sl=sl: e.transpose(pg[0][:, 128:256], xbc[:, 1, sl], m.idf[:]), reads=["xbc", "idf"], writes=[("pg", 0)])
            kb.v("dve", "tensor_copy", xtok[:], pg[0][:, 0:128], reads=[("pg", 0)], writes=["xtok"])
            kb.act(btok[:], pg[0][:, 128:256], AF.Copy, reads=[("pg", 0)], writes=["btok"])
            kb.v("dve", "tensor_tensor", da[:], dt[:, j, :], na[:], ALU.mult, reads=["dt", "na"], writes=["da"])
            kb.mm(pg[2][:, 0:128], xbc[:, 1, sl], xbc[:, 2, sl], reads=["xbc"], writes=[("pg", 2)])
            for h in range(2):
                hh = slice(h * 64, (h + 1) * 64)
                kb.v("dve", "tensor_scalar", dar[:], onesf[:], da[:, h:h + 1], None, ALU.mult, reads=["onesf", "da"], writes=["dar"])
                kb.mm(pg[1][:, 0:128], dar[:], triblk[:], reads=["dar", "triblk"], writes=[("pg", 1)])
                kb.mm(pg[1][:, 128:256], dar[:], blkones[:], reads=["dar", "blkones"], writes=[("pg", 1)])
                kb.mm(pg[1][:, 256:384], triblk[:], dar[:], reads=["dar", "triblk"], writes=[("pg", 1)])
                kb.mm(pg[1][:, 384:512], blkones[:], dar[:], reads=["dar", "blkones"], writes=[("pg", 1)])
                kb.v("dve", "tensor_copy", dc[:, 0:1], pg[1][:, 256:257], reads=[("pg", 1)], writes=["dc"])
                kb.v("dve", "tensor_copy", dc[:, 1:2], pg[1][:, 384:385], reads=[("pg", 1)], writes=["dc"])
                kb.v("dve", "tensor_scalar", tmp[:], pg[1][:, 0:128], dc[:, 0:1], 0.0, ALU.subtract, ALU.min,
                     reads=[("pg", 1), "dc"], writes=["tmp"])
                kb.act(LT[:], tmp[:], AF.Exp, reads=["tmp"], writes=["LT"])
                kb.v("dve", "scalar_tensor_tensor", LT[:], LT[:], dt[:, j, h:h + 1], triblk[:], ALU.mult, ALU.mult,
                     reads=["LT", "dt", "triblk"], writes=["LT"])
                kb.v("dve", "tensor_tensor", wT[:], LT[:], pg[2][:, 0:128], ALU.mult, reads=["LT", ("pg", 2)], writes=["wT"])
                kb.mm(pg[2][:, 128 + h * 64:128 + (h + 1) * 64], wT[:], xtok[:, hh], reads=["wT", "xtok"], writes=[("pg", 2)])
                kb.v("dve", "tensor_tensor", bw[:, h:h + 1], dc[:, 1:2], dc[:, 0:1], ALU.subtract, reads=["dc"], writes=["bw"])
                kb.act(bw[:, h:h + 1], bw[:, h:h + 1], AF.Exp, reads=["bw"], writes=["bw"])
                kb.v("dve", "tensor_tensor", bw[:, h:h + 1], bw[:, h:h + 1], dt[:, j, h:h + 1], ALU.mult, reads=["bw", "dt"], writes=["bw"])
                kb.v("dve", "tensor_scalar", Bw[h][:], btok[:], bw[:, h:h + 1], None, ALU.mult, reads=["btok", "bw"], writes=[("Bw", h)])
                kb.act(edacs[:, h:h + 1], dc[:, 0:1], AF.Exp, reads=["dc"], writes=["edacs"])
                kb.act(cdec[h][:], pg[1][:, 128:256], AF.Exp, reads=[("pg", 1)], writes=[("cdec", h)])
            kb.act(ysb[:], pg[2][:, 128:256], AF.Copy, reads=[("pg", 2)], writes=["ysb"])
            for c in range(2):
                cs = slice(c * 64, (c + 1) * 64)
                kb.mm(pg[3][:, 0:128], xbc[:, 2, sl], stateT[:], reads=["xbc", "stateT"], writes=[("pg", 3)])
                for h in range(2):
                    hh = slice(h * 64, (h + 1) * 64)
                    kb.v("dve", "scalar_tensor_tensor", ysb[cs, hh], pg[3][cs, hh], edacs[cs, h:h + 1], ysb[cs, hh], ALU.mult, ALU.add,
                         reads=[("pg", 3), "edacs", "ysb"], writes=["ysb"])
                for h in range(2):
                    hh = slice(h * 64, (h + 1) * 64)
                    kb.mm(pg[3][:, 128:256], Bw[h][cs, :], xtok[cs, :], reads=[("Bw", h), "xtok"], writes=[("pg", 3)])
                    kb.v("dve", "scalar_tensor_tensor", stateT[:, hh], stateT[:, hh], cdec[h][:, c * 64:c * 64 + 1], pg[3][:, 128 + h * 64:128 + (h + 1) * 64],
                         ALU.mult, ALU.add, reads=["stateT", ("cdec", h), ("pg", 3)], writes=["stateT"])
            for h in range(2):
                hh = slice(h * 64, (h + 1) * 64)
                kb.v("dve", "scalar_tensor_tensor", ysb[:, hh], xtok[:, hh], dsk[:, h:h + 1], ysb[:, hh], ALU.mult, ALU.add,
                     reads=["xtok", "dskip", "ysb"], writes=["ysb"])
            kb.v("dve", "tensor_tensor", ysb[:], ysb[:], zs[:, j, :], ALU.mult, reads=["ysb", "zs"], writes=["ysb"])
            kb.op("pe", lambda e: e.transpose(pg[0][:, 256:384], ysb[:], m.idf[:]), reads=["ysb", "idf"], writes=[("pg", 0)])
            kb.v("dve", "tensor_copy", oT[:, sl], pg[0][:, 256:384], reads=[("pg", 0)], writes=["oT"])
        kb.dma(o_d[:, t0:t0 + TBM], oT[:], reads=["oT"])
    return m.finish()


def chunk_consts_ssd_unused():
    j = np.arange(128)
    same = (j[:, None] // 64) == (j[None, :] // 64)
    return {"triblk": (same & (j[:, None] <= j[None, :])).astype(np.float32), "blkones": same.astype(np.float32)}


def build_gdn():
    m = Mix(514)
    kb, nc = m.kb, m.nc
    o_d = m.dram("oT", [128, T], BF16, "ExternalOutput")
    cw = m.const("convw", [128, 12])
    dtb = m.const("dtb", [128, 1])
    alog = m.const("alog", [128, 1])
    nrmw = m.const("nrmw", [128, 128])
    triblk = m.const("triblk", [128, 128])
    tristr = m.const("tristrict", [128, 128])
    blkones = m.const("blkones", [128, 128])
    idf = m.idf
    onesf = m.sb("onesf", [128, 128])
    kb.v("dve", "memset", onesf[:], 1.0, writes=["onesf"])
    na = m.sb("na", [128, 1])
    kb.act(na[:], alog[:], AF.Exp, reads=["alog"], writes=["na"])
    kb.v("dve", "tensor_scalar", na[:], na[:], -1.0, None, ALU.mult, reads=["na"], writes=["na"])
    T_ = lambda name, shape=[128, 128]: m.sb(name, shape)
    raw = T_("raw", [128, 3, 515]); acc = T_("acc", [128, 512]); qkv = T_("qkv", [128, 3, 512]); sq = T_("sq", [128, 512])
    zs = T_("zs", [128, 4, 128]); beta = T_("beta", [128, 4]); gg = T_("gg", [128, 4])
    ktok = T_("ktok"); vtok = T_("vtok"); gr = T_("gr"); br = T_("br"); dc = T_("dc", [128, 2]); tmp = T_("tmp")
    ET = T_("ET"); egc = T_("egc", [128, 1]); egcbc = T_("egcbc"); eglbc = T_("eglbc"); ksc = T_("ksc", [128, 1]); bcol = T_("bcol", [128, 1])
    NTt = T_("NTt"); AT = T_("AT"); A = T_("A")
    Pa = [T_("Pa0"), T_("Pa1")]; PTa = [T_("PTa0"), T_("PTa1")]; RT = T_("RT")
    vb = T_("vb"); kbt = T_("kbt"); u = T_("u"); wT = T_("wT"); attnT = T_("attnT"); qdT = T_("qdT"); kst = T_("kst")
    vnew = T_("vnew"); S = T_("S"); osb = T_("osb"); junk = T_("junk"); ssq = T_("ssq", [128, 1]); rstd = T_("rstd", [128, 1])
    otk = m.sb("otk", [128, 4, 128], BF16)
    oT = m.sb("oT_s", [128, 4, 128], BF16)
    pfm = [m.ps("pfm%d" % i, [128, 512]) for i in range(2)]
    pg = [m.ps("pg%d" % i, [128, 512]) for i in range(4)]
    ptm = pfm[1]

    def R(b, r):
        return pg[b][:, r * 128:(r + 1) * 128], ("pg", b)

    for t_, k_ in ((raw[:].rearrange("p a b -> p (a b)"), "raw"), (S[:], "S"), (vnew[:], "vnew")):
        kb.v("dve", "memset", t_, 0.0, writes=[k_])

    def silu_from(out, out_key, src, src_key):
        kb.act(out, src, AF.Exp, scale=-1.0, reads=[src_key], writes=[out_key])
        kb.v("dve", "tensor_scalar", out, out, 1.0, None, ALU.add, reads=[out_key], writes=[out_key])
        kb.v("dve", "reciprocal", out, out, reads=[out_key], writes=[out_key])
        kb.v("dve", "tensor_tensor", out, out, src, ALU.mult, reads=[out_key, src_key], writes=[out_key])

    def mm1(dst, lhsT, rhs, reads):
        ap, key = dst
        kb.mm(ap, lhsT, rhs, reads=reads, writes=[key])

    def trp(dst, src, reads):
        ap, key = dst
        kb.op("pe", lambda e: e.transpose(ap, src, idf[:]), reads=reads + ["idf"], writes=[key])
    for blk in range(NBLK):
        m.frontend(blk)
        t0 = blk * TBM
        for i in range(3):
            p = pfm[i % 2]; pk = ("pfm", i % 2)
            m.proj_fm(p[:], pk, i * 128)
            kb.act(raw[:, i, 3:515], p[:], AF.Copy, reads=[pk], writes=["raw"])
            kb.v("dve", "tensor_scalar", acc[:], raw[:, i, 0:512], cw[:, 4 * i:4 * i + 1], None, ALU.mult,
                 reads=["raw", "convw"], writes=["acc"])
            for k in range(1, 4):
                kb.v("dve", "scalar_tensor_tensor", acc[:], raw[:, i, k:k + 512], cw[:, 4 * i + k:4 * i + k + 1], acc[:],
                     ALU.mult, ALU.add, reads=["raw", "convw", "acc"], writes=["acc"])
            silu_from(qkv[:, i, :], ("qkv", i), acc[:], "acc")
            kb.v("dve", "tensor_copy", raw[:, i, 0:3], raw[:, i, 512:515], reads=["raw"], writes=["raw"])
        for i in range(2):
            kb.v("dve", "tensor_tensor", sq[:], qkv[:, i, :], qkv[:, i, :], ALU.mult, reads=[("qkv", i)], writes=["sq"])
            kb.mm(pfm[0][:], onesf[:], sq[:], reads=["onesf", "sq"], writes=[("pfm", 0)])
            kb.v("dve", "tensor_scalar", sq[:], pfm[0][:], EPS, None, ALU.add, reads=[("pfm", 0)], writes=["sq"])
            kb.act(sq[:], sq[:], AF.Sqrt, reads=["sq"], writes=["sq"])
            kb.v("dve", "reciprocal", sq[:], sq[:], reads=["sq"], writes=["sq"])
            kb.v("dve", "scalar_tensor_tensor", qkv[:, i, :], qkv[:, i, :], (128 ** -0.5) if i == 0 else 1.0, sq[:], ALU.mult, ALU.mult,
                 reads=[("qkv", i), "sq"], writes=[("qkv", i)])
        for j in range(4):
            m.proj_tm(ptm[:, 0:130], ("pfm", 1), j, 384, 130)
            silu_from(zs[:, j, :], "zs", ptm[:, 0:128], ("pfm", 1))
            kb.v("dve", "tensor_copy", beta[:, j:j + 1], ptm[:, 128:129], reads=[("pfm", 1)], writes=["beta"])
            kb.v("dve", "tensor_scalar", gg[:, j:j + 1], ptm[:, 129:130], dtb[:, 0:1], None, ALU.add, reads=[("pfm", 1), "dtb"], writes=["gg"])
        kb.act(beta[:], beta[:], AF.Exp, scale=-1.0, reads=["beta"], writes=["beta"])
        kb.v("dve", "tensor_scalar", beta[:], beta[:], 1.0, None, ALU.add, reads=["beta"], writes=["beta"])
        kb.v("dve", "reciprocal", beta[:], beta[:], reads=["beta"], writes=["beta"])
        kb.act(gg[:], gg[:], AF.Exp, reads=["gg"], writes=["gg"])
        kb.act(gg[:], gg[:], AF.Ln, bias=1.0, reads=["gg"], writes=["gg"])
        kb.v("dve", "tensor_scalar", gg[:], gg[:], na[:, 0:1], None, ALU.mult, reads=["gg", "na"], writes=["gg"])
        for j in range(4):
            sl = slice(j * 128, (j + 1) * 128)
            qn, kn, vf = qkv[:, 0, sl], qkv[:, 1, sl], qkv[:, 2, sl]
            trp(R(0, 0), kn, [("qkv", 1)]); trp(R(0, 1), vf, [("qkv", 2)])
            kb.v("dve", "tensor_copy", ktok[:], R(0, 0)[0], reads=[R(0, 0)[1]], writes=["ktok"])
            kb.act(vtok[:], R(0, 1)[0], AF.Copy, reads=[R(0, 1)[1]], writes=["vtok"])
            kb.v("dve", "tensor_scalar", gr[:], onesf[:], gg[:, j:j + 1], None, ALU.mult, reads=["onesf", "gg"], writes=["gr"])
            kb.v("dve", "tensor_scalar", br[:], onesf[:], beta[:, j:j + 1], None, ALU.mult, reads=["onesf", "beta"], writes=["br"])
            mm1(R(1, 0), gr[:], triblk[:], ["gr", "triblk"])
            mm1(R(1, 1), gr[:], blkones[:], ["gr", "blkones"])
            mm1(R(1, 2), triblk[:], gr[:], ["gr", "triblk"])
            mm1(R(1, 3), blkones[:], gr[:], ["gr", "blkones"])
            mm1(R(2, 0), br[:], idf[:], ["br", "idf"])
            kb.v("dve", "tensor_copy", dc[:, 0:1], pg[1][:, 256:257], reads=[("pg", 1)], writes=["dc"])
            kb.v("dve", "tensor_copy", dc[:, 1:2], pg[1][:, 384:385], reads=[("pg", 1)], writes=["dc"])
            kb.v("dve", "tensor_scalar", tmp[:], R(1, 0)[0], dc[:, 0:1], 0.0, ALU.subtract, ALU.min, reads=[("pg", 1), "dc"], writes=["tmp"])
            kb.act(ET[:], tmp[:], AF.Exp, reads=["tmp"], writes=["ET"])
            kb.v("dve", "tensor_tensor", ET[:], ET[:], triblk[:], ALU.mult, reads=["ET", "triblk"], writes=["ET"])
            kb.act(egcbc[:], R(1, 0)[0], AF.Exp, reads=[("pg", 1)], writes=["egcbc"])
            kb.act(eglbc[:], R(1, 1)[0], AF.Exp, reads=[("pg", 1)], writes=["eglbc"])
            kb.act(egc[:], dc[:, 0:1], AF.Exp, reads=["dc"], writes=["egc"])
            kb.v("dve", "tensor_tensor", ksc[:], dc[:, 1:2], dc[:, 0:1], ALU.subtract, reads=["dc"], writes=["ksc"])
            kb.act(ksc[:], ksc[:], AF.Exp, reads=["ksc"], writes=["ksc"])
            kb.v("dve", "tensor_tensor", bcol[:], egc[:], beta[:, j:j + 1], ALU.mult, reads=["egc", "beta"], writes=["bcol"])
            mm1(R(2, 1), kn, kn, [("qkv", 1)])
            kb.v("dve", "tensor_tensor", NTt[:], ET[:], R(2, 1)[0], ALU.mult, reads=["ET", ("pg", 2)], writes=["NTt"])
            kb.v("dve", "tensor_tensor", NTt[:], NTt[:], tristr[:], ALU.mult, reads=["NTt", "tristrict"], writes=["NTt"])
            kb.v("dve", "tensor_tensor", AT[:], NTt[:], R(2, 0)[0], ALU.mult, reads=["NTt", ("pg", 2)], writes=["AT"])
            trp(R(0, 2), AT[:], ["AT"])
            kb.v("dve", "tensor_copy", A[:], R(0, 2)[0], reads=[("pg", 0)], writes=["A"])
            kb.v("dve", "tensor_tensor", RT[:], idf[:], AT[:], ALU.subtract, reads=["idf", "AT"], writes=["RT"])
            P, PT, Pk, PTk = A, AT, "A", "AT"
            for lv in range(1, 6):
                Pn, PTn = Pa[lv % 2], PTa[lv % 2]
                Pnk, PTnk = ("Pa", lv % 2), ("PTa", lv % 2)
                mm1(R(3, 0), PT[:], P[:], [Pk, PTk])
                kb.v("dve", "tensor_copy", Pn[:], R(3, 0)[0], reads=[("pg", 3)], writes=[Pnk])
                if lv < 5:
                    mm1(R(3, 1), P[:], PT[:], [Pk, PTk])
                    kb.act(PTn[:], R(3, 1)[0], AF.Copy, reads=[("pg", 3)], writes=[PTnk])
                mm1(R(3, 2), Pn[:], RT[:], [Pnk, "RT"])
                kb.v("dve", "tensor_tensor", RT[:], RT[:], R(3, 2)[0], ALU.add, reads=["RT", ("pg", 3)], writes=["RT"])
                P, PT, Pk, PTk = Pn, PTn, Pnk, PTnk
            kb.v("dve", "tensor_scalar", vb[:], vtok[:], beta[:, j:j + 1], None, ALU.mult, reads=["vtok", "beta"], writes=["vb"])
            kb.v("dve", "tensor_scalar", kbt[:], ktok[:], bcol[:, 0:1], None, ALU.mult, reads=["ktok", "bcol"], writes=["kbt"])
            mm1(R(2, 3), RT[:], vb[:], ["RT", "vb"])
            kb.v("dve", "tensor_copy", u[:], R(2, 3)[0], reads=[("pg", 2)], writes=["u"])
            mm1(R(3, 3), kbt[:], RT[:], ["kbt", "RT"])
            kb.act(wT[:], R(3, 3)[0], AF.Copy, reads=[("pg", 3)], writes=["wT"])
            mm1(R(2, 2), kn, qn, [("qkv", 0), ("qkv", 1)])
            kb.v("dve", "tensor_tensor", attnT[:], ET[:], R(2, 2)[0], ALU.mult, reads=["ET", ("pg", 2)], writes=["attnT"])
            kb.v("dve", "tensor_tensor", qdT[:], qn, egcbc[:], ALU.mult, reads=[("qkv", 0), "egcbc"], writes=["qdT"])
            kb.v("dve", "tensor_scalar", kst[:], ktok[:], ksc[:, 0:1], None, ALU.mult, reads=["ktok", "ksc"], writes=["kst"])
            for c in range(2):
                cs = slice(c * 64, (c + 1) * 64)
                mm1(R(1, 0), wT[:], S[:], ["wT", "S"])
                kb.v("dve", "tensor_tensor", vnew[cs, :], u[cs, :], pg[1][cs, 0:128], ALU.subtract, reads=["u", ("pg", 1)], writes=["vnew"])
                kb.mm(R(1, 1)[0], qdT[:], S[:], start=True, stop=False, reads=["qdT", "S"], writes=[("pg", 1)])
                kb.mm(R(1, 1)[0], attnT[:], vnew[:], start=False, stop=True, reads=["attnT", "vnew"], writes=[("pg", 1)])
                kb.v("dve", "tensor_copy", osb[cs, :], pg[1][cs, 128:256], reads=[("pg", 1)], writes=["osb"])
                mm1(R(1, 2), kst[cs, :], vnew[cs, :], ["kst", "vnew"])
                kb.v("dve", "scalar_tensor_tensor", S[:], S[:], eglbc[:, c * 64:c * 64 + 1], R(1, 2)[0], ALU.mult, ALU.add,
                     reads=["S", "eglbc", ("pg", 1)], writes=["S"])
            kb.act(junk[:], osb[:], AF.Square, accum_out=ssq[:], reads=["osb"], writes=["junk", "ssq"])
            kb.v("dve", "tensor_scalar", rstd[:], ssq[:], 1.0 / 128, EPS, ALU.mult, ALU.add, reads=["ssq"], writes=["rstd"])
            kb.act(rstd[:], rstd[:], AF.Sqrt, reads=["rstd"], writes=["rstd"])
            kb.v("dve", "reciprocal", rstd[:], rstd[:], reads=["rstd"], writes=["rstd"])
            kb.v("dve", "scalar_tensor_tensor", osb[:], osb[:], rstd[:, 0:1], nrmw[:], ALU.mult, ALU.mult, reads=["osb", "rstd", "nrmw"], writes=["osb"])
            kb.v("dve", "tensor_tensor", osb[:], osb[:], zs[:, j, :], ALU.mult, reads=["osb", "zs"], writes=["osb"])
            kb.v("dve", "tensor_copy", otk[:, j, :], osb[:], reads=["osb"], writes=["otk"])
        for j in range(4):
            kb.tr(m.ptr[0][:, j, :], otk[:, j, :], m.idb[:], reads=["otk", "idb"], writes=[("ptr", 0)])
        kb.v("dve", "tensor_copy", oT[:], m.ptr[0][:], reads=[("ptr", 0)], writes=["oT"])
        kb.dma(o_d[:, t0:t0 + TBM], oT[:].rearrange("p j t -> p (j t)"), reads=["oT"])
    return m.finish()


def chunk_consts():
    j = np.arange(128)
    same = (j[:, None] // 64) == (j[None, :] // 64)
    return {"triblk": (same & (j[:, None] <= j[None, :])).astype(np.float32), "blkones": same.astype(np.float32),
            "tristrict": (same & (j[:, None] < j[None, :])).astype(np.float32)}


def gdn_maps(inputs, xs, ident):
    f32 = np.float32
    W = np.asarray(inputs["ab_w_in"][0], f32)
    cwf = np.asarray(inputs["ab_conv_w"][0], f32)
    maps = []
    for c in range(8):
        b, g = c // 4, c % 4
        ar = np.arange(g * 128, (g + 1) * 128)
        cols = np.concatenate([1536 + ar, 2048 + ar, 2560 + ar, 3080 + ar, [3072 + g], [3076 + g]])
        convw = np.stack([cwf[:, ch].T for ch in (ar, 512 + ar, 1024 + ar)], 1).reshape(128, 12)
        one = lambda v: np.full((128, 1), v, f32)
        mm = {"x": np.ascontiguousarray(xs[b]), "w_in": np.ascontiguousarray(W[:, cols]),
              "norm_w": np.asarray(inputs["norm_mix"][0], f32)[None, :], "ident": ident,
              "convw": np.ascontiguousarray(convw), "dtb": one(inputs["ab_dt_bias"][0][g]), "alog": one(inputs["ab_a_log"][0][g]),
              "nrmw": np.ascontiguousarray(np.broadcast_to(np.asarray(inputs["ab_norm_w"][0], f32)[None, :], (128, 128)))}
        mm.update(chunk_consts())
        maps.append(mm)
    return maps


def build_program():
    nc = bass.Bass("TRN2", target_bir_lowering=False)
    kb = KB(nc)
    _CTX.clear()
    _CTX.update(nc=nc, kb=kb)
    x_d = nc.dram_tensor("x", [T, D], F32, kind="ExternalInput").ap()
    y_d = nc.dram_tensor("y", [T, D], F32, kind="ExternalOutput").ap()
    omT0 = nc.dram_tensor("omT0", [1024, T], BF16).ap()
    omT1 = nc.dram_tensor("omT1", [1024, T], BF16).ap()
    x1 = nc.dram_tensor("x1", [T, D], F32).ap()

    def run(tag, fn, **ctx):
        _CTX.update(tag=tag, **ctx)
        fn()

    for g in range(4):
        run("band%d" % g, build_band, x=x_d, out=omT0[g * 128:(g + 1) * 128, :])
    for g in range(4):
        run("gdn%d" % g, build_gdn, x=x_d, out=omT0[512 + g * 128:512 + (g + 1) * 128, :])
    run("tok0", lambda: build_tok(False), x=x_d, omT=omT0, out=x1)
    for g in range(4):
        run("fox%d" % g, build_fox, x=x1, out=omT1[512 + g * 128:512 + (g + 1) * 128, :])
    for g in range(4):
        run("ssd%d" % g, build_ssd, x=x1, out=omT1[g * 128:(g + 1) * 128, :])
    run("tok1", lambda: build_tok(True), x=x1, omT=omT1, out=y_d)
    kb.close()
    return nc


def host_maps(inputs, b):
    f32 = np.float32
    A = lambda k: np.asarray(inputs[k], f32)
    ident = np.eye(128, dtype=f32)
    bc = lambda v: np.ascontiguousarray(np.broadcast_to(np.asarray(v, f32)[None, :], (128, len(v))))
    m = {"x": np.ascontiguousarray(A("x")[b])}

    def put(tag, d):
        for k, v in d.items():
            m["%s_%s" % (k, tag)] = v

    W0 = A("ab_w_in")[0]; rb = A("ab_rel_bias")[0]
    W1 = A("cd_w_in")[0]
    nm0 = A("norm_mix")[0][None, :]; nm1 = A("norm_mix")[1][None, :]
    gm = gdn_maps(inputs, [None, None], ident)
    cwf = A("cd_conv_w")[0]; cbf = A("cd_conv_b")[0]
    cc = chunk_consts(); fc = fox_consts()
    for g in range(4):
        ar = np.arange(g * 128, (g + 1) * 128)
        put("band%d" % g, {"w_in": np.ascontiguousarray(W0[:, np.concatenate([ar, 512 + ar, 1024 + ar])]), "norm_w": nm0,
                           "ident": ident, "biasT": band_bias(rb[2 * g:2 * g + 2])})
        d = dict(gm[g]); d.pop("x")
        put("gdn%d" % g, d)
        cols = np.concatenate([1544 + ar, 2056 + ar, 2568 + ar, 3080 + np.arange(2 * g, 2 * g + 2)])
        d = {"w_in": np.ascontiguousarray(W1[:, cols]), "norm_w": nm1, "f_bias": bc(A("cd_f_bias")[0][2 * g:2 * g + 2])}
        d.update(fc)
        put("fox%d" % g, d)
        G = g // 2
        Bch = 512 + G * 128 + np.arange(128); Cch = 768 + G * 128 + np.arange(128)
        cols = np.concatenate([512 + ar, 512 + Bch, 512 + Cch, ar, 1536 + np.arange(2 * g, 2 * g + 2)])
        convw = np.stack([cwf[:, ch].T for ch in (ar, Bch, Cch)], 1).reshape(128, 12)
        convb = np.stack([cbf[ch] for ch in (ar, Bch, Cch)], 1)
        put("ssd%d" % g, {"w_in": np.ascontiguousarray(W1[:, cols]), "norm_w": nm1, "ident": ident,
                          "convw": np.ascontiguousarray(convw), "convb": np.ascontiguousarray(convb),
                          "dtb": bc(A("cd_dt_bias")[0][2 * g:2 * g + 2]), "alog": bc(A("cd_a_log")[0][2 * g:2 * g + 2]),
                          "dskip": bc(A("cd_d_skip")[0][2 * g:2 * g + 2]), "triblk": cc["triblk"], "blkones": cc["blkones"]})
    for L, wout in ((0, A("ab_w_out")[0]), (1, A("cd_w_out")[0])):
        d = {"w_out": wout, "w_gate": A("ffn_w_gate")[L], "w_up": A("ffn_w_up")[L], "w_down": A("ffn_w_down")[L],
             "norm_ffn": A("norm_ffn")[L][None, :], "ident": ident}
        if L == 1:
            d["norm_final"] = A("norm_final")[None, :]
            d["gnw"] = np.ascontiguousarray(A("cd_norm_w")[0].reshape(4, 128).T)
        put("tok%d" % L, d)
    return m


def kernel(**inputs):
    nc = build_program()
    maps = [host_maps(inputs, b) for b in range(2)]
    res = run_bass_kernel_spmd(nc, maps, core_ids=[0, 1])
    out = np.stack([np.asarray(res.results[b]["y"]) for b in range(2)], 0)
    return out.astype(np.float32)
```

```python
import contextlib
import os
import numpy as np
import ml_dtypes
import concourse.bass as bass
import concourse.mybir as mybir
from concourse.bass_utils import run_bass_kernel_spmd

_CTX = {}

F32 = mybir.dt.float32
BF16 = mybir.dt.bfloat16
AF = mybir.ActivationFunctionType
ALU = mybir.AluOpType
AX = mybir.AxisListType

ENGS = ["pe", "dve", "act", "pool", "sp"]
EPOCH = 30000
NDMA = 24


class KB:
    def __init__(self, nc):
        self.nc = nc
        self.ops = {e: [] for e in ENGS}
        self.cnt = {e: 0 for e in ENGS}
        self.lastw = {}
        self.readers = {}
        self.seen = {e: {} for e in ENGS}
        self.dma_i = 0
        self.semnames = set()
        self.dma_tokens = {}

    def _deps(self, eng, reads, writes):
        toks = {}
        def add(t):
            if t is None:
                return
            s, v = t
            if toks.get(s, 0) < v:
                toks[s] = v
        for k in reads:
            add(self.lastw.get(k))
        for k in writes:
            add(self.lastw.get(k))
            for s, v in self.readers.get(k, {}).items():
                add((s, v))
        waits = []
        for s, v in toks.items():
            if eng == "pe" and s.startswith("pe"):
                continue
            if self.seen[eng].get(s, 0) >= v:
                continue
            self.seen[eng][s] = v
            waits.append((s, v))
        return waits

    def _commit(self, tok, reads, writes):
        for k in reads:
            d = self.readers.setdefault(k, {})
            if d.get(tok[0], 0) < tok[1]:
                d[tok[0]] = tok[1]
        for k in writes:
            self.lastw[k] = tok
            self.readers[k] = {}

    def op(self, eng, fn, reads=(), writes=()):
        waits = self._deps(eng, reads, writes)
        n = self.cnt[eng]
        self.cnt[eng] = n + 1
        s = "%s%d" % (eng, n // EPOCH)
        self.semnames.add(s)
        tok = (s, n % EPOCH + 1)
        self.ops[eng].append((waits, fn, (s, 1)))
        self._commit(tok, reads, writes)
        return tok

    def dma(self, out, in_, reads=(), writes=(), eng="sp", **kw):
        i = self.dma_i
        self.dma_i += 1
        s = "dma%d" % (i % NDMA)
        self.semnames.add(s)
        waits = self._deps(eng, reads, writes)
        prev = 16 * (i // NDMA)
        if prev > 0 and self.seen[eng].get(s, 0) < prev:
            self.seen[eng][s] = prev
            waits.append((s, prev))
        tok = (s, prev + 16)
        self.ops[eng].append((waits, lambda e: e.dma_start(out=out, in_=in_, **kw), (s, 16)))
        self._commit(tok, reads, writes)
        self.dma_tokens[s] = tok[1]
        return tok

    def mm(self, out, lhsT, rhs, start=True, stop=True, reads=(), writes=(), **kw):
        return self.op("pe", lambda e: e.matmul(out, lhsT, rhs, start=start, stop=stop, **kw), reads, writes)

    def tr(self, out, in_, ident, reads=(), writes=()):
        return self.op("pe", lambda e: e.transpose(out, in_, ident), reads, writes)

    def act(self, out, in_, func, reads=(), writes=(), **kw):
        return self.op("act", lambda e: e.activation(out, in_, func, **kw), reads, writes)

    def v(self, eng, name, *args, reads=(), writes=(), **kw):
        return self.op(eng, lambda e: getattr(e, name)(*args, **kw), reads, writes)

    def _sem(self, s):
        if s not in self.sems:
            self.sems[s] = self.semstack.enter_context(self.nc.semaphore(s))
        return self.sems[s]

    def flush(self):
        nc = self.nc
        if not hasattr(self, "sems"):
            self.sems = {}
            self.semstack = contextlib.ExitStack()
        finals = []
        for e in ENGS:
            if e == "sp" or self.cnt[e] == 0:
                continue
            n = self.cnt[e] - 1
            finals.append(("%s%d" % (e, n // EPOCH), n % EPOCH + 1))
        for s, v in self.dma_tokens.items():
            finals.append((s, v))
        for s in sorted(self.semnames):
            self._sem(s)
        sems = self.sems
        ops = self.ops
        with nc.Block() as block:
            def replay(eng, lst):
                for waits, fn, inc in lst:
                    for s, v in waits:
                        eng.wait_ge(sems[s], v)
                    fn(eng).then_inc(sems[inc[0]], inc[1])
                for s, v in finals:
                    eng.wait_ge(sems[s], v)

            @block.tensor
            def _(eng):
                replay(eng, ops["pe"])

            @block.vector
            def _(eng):
                replay(eng, ops["dve"])

            @block.scalar
            def _(eng):
                replay(eng, ops["act"])

            @block.gpsimd
            def _(eng):
                replay(eng, ops["pool"])

            @block.sync
            def _(eng):
                replay(eng, ops["sp"])
        self.ops = {e: [] for e in ENGS}
        self.lastw = {}
        self.readers = {}
        for e in ENGS:
            for s, v in finals:
                if self.seen[e].get(s, 0) < v:
                    self.seen[e][s] = v

    def close(self):
        self.semstack.close()


D = 1024
T = 8192
TBM = 512
NBLK = T // TBM
EPS = 1e-6


class Mix:
    def __init__(self, ncols, xdtype_norm=True):
        self.nc = nc = _CTX["nc"]
        self.kb = _CTX["kb"]
        self.tag = _CTX["tag"]
        self.st = contextlib.ExitStack()
        self.ncols = ncols
        self.x_d = self.dram("x", [T, D])
        self.w_d = self.dram("w_in", [D, ncols])
        self.nw_d = self.dram("norm_w", [1, D])
        self.id_d = self.dram("ident", [128, 128])
        kb = self.kb
        self.w = self.sb("w", [128, 8, ncols], BF16)
        self.nw = self.sb("nw", [128, D])
        self.idf = self.sb("idf", [128, 128])
        self.idb = self.sb("idb", [128, 128], BF16)
        self.xt = self.sb("xt", [128, 4, D])
        self.hb = self.sb("hb", [128, 4, D], BF16)
        self.hT = self.sb("hT", [128, 8, TBM], BF16)
        self.ss = self.sb("ss", [128, 4])
        self.rs = self.sb("rs", [128, 4])
        self.ptr = [self.ps("ptr%d" % i, [128, 4, 128], BF16) for i in range(2)]
        kb.dma(self.idf[:], self.id_d[:, :], writes=["idf"])
        kb.v("dve", "tensor_copy", self.idb[:], self.idf[:], reads=["idf"], writes=["idb"])
        kb.dma(self.nw[:], self.nw_d[0:1, :].to_broadcast([128, D]), writes=["nw"])
        stg = self.xt
        i = 0
        for kc in range(8):
            for n0 in range(0, ncols, 1024):
                n1 = min(ncols, n0 + 1024)
                s = i % 4
                i += 1
                kb.dma(stg[:, s, 0:n1 - n0], self.w_d[kc * 128:(kc + 1) * 128, n0:n1], writes=[("xt", s)])
                eng = ["dve", "pool"][s % 2]
                kb.v(eng, "tensor_copy", self.w[:, kc, n0:n1], stg[:, s, 0:n1 - n0], reads=[("xt", s)], writes=["w"])

    def dram(self, name, shape, dt=F32, kind="ExternalInput"):
        if kind == "ExternalOutput":
            return _CTX["out"]
        if name == "x":
            return _CTX["x"]
        return self.nc.dram_tensor("%s_%s" % (name, self.tag), shape, dt, kind=kind).ap()

    def sb(self, name, shape, dt=F32):
        return self.st.enter_context(self.nc.sbuf_tensor("%s_%s" % (self.tag, name), shape, dt))

    def ps(self, name, shape, dt=F32):
        return self.st.enter_context(self.nc.psum_tensor("%s_%s" % (self.tag, name), shape, dt))

    def const(self, name, shape, bf=False):
        d = self.dram(name, shape)
        t = self.sb(name + "_s", shape)
        self.kb.dma(t[:], d, writes=[name])
        if bf:
            tb = self.sb(name + "_b", shape, BF16)
            self.kb.v("dve", "tensor_copy", tb[:], t[:], reads=[name], writes=[name + "_b"])
            return t, tb
        return t

    def frontend(self, blk):
        kb = self.kb
        xt, hb, hT, ss, rs = self.xt, self.hb, self.hT, self.ss, self.rs
        t0 = blk * TBM
        for j in range(4):
            kb.dma(xt[:, j, :], self.x_d[t0 + j * 128:t0 + (j + 1) * 128, :], writes=[("xt", j)])
        for j in range(4):
            kb.act(hb[:, j, :], xt[:, j, :], AF.Square, accum_out=ss[:, j:j + 1], reads=[("xt", j)], writes=[("hb", j), ("ss", j)])
            kb.v("dve", "tensor_scalar", rs[:, j:j + 1], ss[:, j:j + 1], 1.0 / D, EPS, ALU.mult, ALU.add,
                 reads=[("ss", j)], writes=[("rs", j)])
            kb.act(rs[:, j:j + 1], rs[:, j:j + 1], AF.Sqrt, reads=[("rs", j)], writes=[("rs", j)])
            kb.v("dve", "reciprocal", rs[:, j:j + 1], rs[:, j:j + 1], reads=[("rs", j)], writes=[("rs", j)])
            kb.v("dve", "scalar_tensor_tensor", hb[:, j, :], xt[:, j, :], rs[:, j:j + 1], self.nw[:], ALU.mult, ALU.mult,
                 reads=[("xt", j), ("rs", j), "nw"], writes=[("hb", j)])
            for q4 in range(2):
                pt = self.ptr[q4]
                for i in range(4):
                    kc = q4 * 4 + i
                    kb.tr(pt[:, i, :], hb[:, j, kc * 128:(kc + 1) * 128], self.idb[:], reads=[("hb", j), "idb"], writes=[("ptr", q4)])
                if q4 == 0:
                    kb.v("dve", "tensor_copy", hT[:, 0:4, j * 128:(j + 1) * 128], pt[:], reads=[("ptr", q4)], writes=["hT"])
                else:
                    kb.act(hT[:, 4:8, j * 128:(j + 1) * 128], pt[:], AF.Copy, reads=[("ptr", q4)], writes=["hT"])

    def proj_fm(self, psum, pkey, c0, ncol=128):
        for kc in range(8):
            self.kb.mm(psum, self.w[:, kc, c0:c0 + ncol], self.hT[:, kc, :], start=(kc == 0), stop=(kc == 7),
                       reads=["w", "hT"], writes=[pkey])

    def proj_tm(self, psum, pkey, j, c0, ncol):
        for kc in range(8):
            self.kb.mm(psum, self.hT[:, kc, j * 128:(j + 1) * 128], self.w[:, kc, c0:c0 + ncol], start=(kc == 0), stop=(kc == 7),
                       reads=["w", "hT"], writes=[pkey])

    def finish(self):
        self.kb.flush()
        self.st.close()


D = 1024
DFF = 2816
NF = DFF // 128
TOK = 8192
TB = 256
EPS = 1e-6


def build_tok(final):
    nc = _CTX["nc"]
    tag = _CTX["tag"]
    def dr(name, shape, dt=F32, kind="ExternalInput"):
        if name in ("x", "omT"):
            return _CTX[name]
        if kind == "ExternalOutput":
            return _CTX["out"]
        return nc.dram_tensor("%s_%s" % (name, tag), shape, dt, kind=kind).ap()
    x_d = dr("x", [TOK, D])
    om_d = dr("omT", [D, TOK], BF16)
    wo_d = dr("w_out", [D, D])
    wg_d = dr("w_gate", [D, DFF])
    wu_d = dr("w_up", [D, DFF])
    wd_d = dr("w_down", [DFF, D])
    nf_d = dr("norm_ffn", [1, D])
    id_d = dr("ident", [128, 128])
    if final:
        nfin_d = dr("norm_final", [1, D])
        gnw_d = dr("gnw", [128, 4])
    y_d = dr("y", [TOK, D], F32, "ExternalOutput")
    kb = _CTX["kb"]
    with contextlib.ExitStack() as st:
        sb = lambda name, shape, dt=F32: st.enter_context(nc.sbuf_tensor("%s_%s" % (tag, name), shape, dt))
        ps = lambda name, shape, dt=F32: st.enter_context(nc.psum_tensor("%s_%s" % (tag, name), shape, dt))
        wo = sb("wo", [128, 8, D], BF16)
        wg = sb("wg", [128, 8, DFF], BF16)
        wu = sb("wu", [128, 8, DFF], BF16)
        wd = sb("wd", [128, NF, D], BF16)
        stg = [sb("stg%d" % i, [128, 1024]) for i in range(3)]
        nfw = sb("nfw", [128, D])
        idf = sb("idf", [128, 128]); idb = sb("idb", [128, 128], BF16)
        ones = sb("ones", [128, 128], BF16)
        xt = sb("xt", [128, 2, D])
        om = sb("om", [128, 8, TB], BF16)
        hb = sb("hb", [128, 2, D], BF16)
        hT = sb("hT", [128, 8, TB], BF16)
        actT = sb("actT", [128, NF, TB], BF16)
        sg = [sb("sg%d" % i, [128, TB]) for i in range(2)]
        ss = sb("ss", [128, 4]); rs = sb("rs", [128, 4])
        if final:
            nfin = sb("nfin", [128, D]); gnw = sb("gnw_s", [128, 4])
            sq = sb("sq", [128, 4, TB], BF16); rg = sb("rg", [128, TB])
        pmm = [ps("pmm%d" % i, [128, 512]) for i in range(2)]
        pg = [ps("pg%d" % i, [128, 512]) for i in range(2)]
        pu = [ps("pu%d" % i, [128, 512]) for i in range(2)]
        ptr = [ps("ptr%d" % i, [128, 4, 128], BF16) for i in range(2)]

        kb.dma(idf[:], id_d[:, :], writes=["idf"])
        kb.v("dve", "tensor_copy", idb[:], idf[:], reads=["idf"], writes=["idb"])
        kb.v("dve", "memset", ones[:], 1.0, writes=["ones"])
        kb.dma(nfw[:], nf_d[0:1, :].to_broadcast([128, D]), writes=["nfw"])
        if final:
            kb.dma(nfin[:], nfin_d[0:1, :].to_broadcast([128, D]), writes=["nfin"])
            kb.dma(gnw[:], gnw_d[:, :], writes=["gnw"])
        cvt_i = [0]
        def load_w(dst, src, K, N, name):
            for kc in range(K // 128):
                for n0 in range(0, N, 1024):
                    n1 = min(N, n0 + 1024)
                    i = cvt_i[0] % 3
                    cvt_i[0] += 1
                    kb.dma(stg[i][:, 0:n1 - n0], src[kc * 128:(kc + 1) * 128, n0:n1], writes=[("stg", i)])
                    eng = ["dve", "pool", "act"][i]
                    if eng == "act":
                        kb.act(dst[:, kc, n0:n1], stg[i][:, 0:n1 - n0], AF.Copy, reads=[("stg", i)], writes=[name])
                    else:
                        kb.v(eng, "tensor_copy", dst[:, kc, n0:n1], stg[i][:, 0:n1 - n0], reads=[("stg", i)], writes=[name])
        load_w(wo, wo_d, D, D, "wo")
        load_w(wg, wg_d, D, DFF, "wg")
        load_w(wu, wu_d, D, DFF, "wu")
        load_w(wd, wd_d, DFF, D, "wd")

        def rmsnorm_rows(src, j, col, wtile, dst, dst_key):
            kb.act(hb[:, j, :], src, AF.Square, accum_out=ss[:, col:col + 1], reads=["xt"], writes=["hb", ("ss", col)])
            kb.v("dve", "tensor_scalar", rs[:, col:col + 1], ss[:, col:col + 1], 1.0 / D, EPS, ALU.mult, ALU.add,
                 reads=[("ss", col)], writes=[("rs", col)])
            kb.act(rs[:, col:col + 1], rs[:, col:col + 1], AF.Sqrt, reads=[("rs", col)], writes=[("rs", col)])
            kb.v("dve", "reciprocal", rs[:, col:col + 1], rs[:, col:col + 1], reads=[("rs", col)], writes=[("rs", col)])
            kb.v("dve", "scalar_tensor_tensor", dst, src, rs[:, col:col + 1], wtile[:], ALU.mult, ALU.mult,
                 reads=["xt", ("rs", col), "nfw", "nfin"], writes=[dst_key])

        omv = om_d.rearrange("(kc p) t -> p kc t", p=128)
        for blk in range(TOK // TB):
            t0 = blk * TB
            kb.dma(xt[:], x_d[t0:t0 + TB, :].rearrange("(j p) d -> p j d", p=128), writes=["xt"])
            kb.dma(om[:], omv[:, :, t0:t0 + TB], writes=["om"])
            if final:
                kb.v("pool", "tensor_tensor", sq[:], om[:, 0:4, :], om[:, 0:4, :], ALU.mult, reads=["om"], writes=["sq"])
                for G in range(2):
                    p = pmm[G]
                    for i in range(2):
                        kb.mm(p[:, 0:TB], ones[:], sq[:, 2 * G + i, :], start=(i == 0), stop=(i == 1),
                              reads=["ones", "sq"], writes=[("pmm", G)])
                    kb.v("dve", "tensor_scalar", rg[:], p[:, 0:TB], 1.0 / 256, EPS, ALU.mult, ALU.add,
                         reads=[("pmm", G)], writes=["rg"])
                    kb.act(rg[:], rg[:], AF.Sqrt, reads=["rg"], writes=["rg"])
                    kb.v("dve", "reciprocal", rg[:], rg[:], reads=["rg"], writes=["rg"])
                    for i in range(2):
                        kc = 2 * G + i
                        kb.v("dve", "scalar_tensor_tensor", om[:, kc, :], om[:, kc, :], gnw[:, kc:kc + 1], rg[:],
                             ALU.mult, ALU.mult, reads=["om", "rg", "gnw"], writes=["om"])
            for j in range(2):
                for hf in range(2):
                    p = pmm[(2 * j + hf) % 2]
                    key = ("pmm", (2 * j + hf) % 2)
                    for kc in range(8):
                        kb.mm(p[:], om[:, kc, j * 128:(j + 1) * 128], wo[:, kc, hf * 512:(hf + 1) * 512],
                              start=(kc == 0), stop=(kc == 7), reads=["om", "wo"], writes=[key])
                    kb.v("dve", "tensor_tensor", xt[:, j, hf * 512:(hf + 1) * 512], xt[:, j, hf * 512:(hf + 1) * 512], p[:],
                         ALU.add, reads=[key, "xt"], writes=["xt"])
            for j in range(2):
                rmsnorm_rows(xt[:, j, :], j, j, nfw, hb[:, j, :], "hb")
                for q4 in range(2):
                    pt = ptr[q4]
                    for i in range(4):
                        kc = q4 * 4 + i
                        kb.tr(pt[:, i, :], hb[:, j, kc * 128:(kc + 1) * 128], idb[:], reads=["hb", "idb"], writes=[("ptr", q4)])
                    if q4 == 0:
                        kb.v("dve", "tensor_copy", hT[:, 0:4, j * 128:(j + 1) * 128], pt[:], reads=[("ptr", q4)], writes=["hT"])
                    else:
                        kb.act(hT[:, 4:8, j * 128:(j + 1) * 128], pt[:], AF.Copy, reads=[("ptr", q4)], writes=["hT"])
            for f in range(NF):
                b = f % 2
                for kc in range(8):
                    kb.mm(pg[b][:, 0:TB], wg[:, kc, f * 128:(f + 1) * 128], hT[:, kc, :], start=(kc == 0), stop=(kc == 7),
                          reads=["wg", "hT"], writes=[("pg", b)])
                for kc in range(8):
                    kb.mm(pu[b][:, 0:TB], wu[:, kc, f * 128:(f + 1) * 128], hT[:, kc, :], start=(kc == 0), stop=(kc == 7),
                          reads=["wu", "hT"], writes=[("pu", b)])
                kb.act(sg[b][:], pg[b][:, 0:TB], AF.Silu, reads=[("pg", b)], writes=[("sg", b)])
                kb.v("dve", "tensor_tensor", actT[:, f, :], sg[b][:], pu[b][:, 0:TB], ALU.mult,
                     reads=[("sg", b), ("pu", b)], writes=["actT"])
            for j in range(2):
                for hf in range(2):
                    p = pmm[(2 * j + hf) % 2]
                    key = ("pmm", (2 * j + hf) % 2)
                    for f in range(NF):
                        kb.mm(p[:], actT[:, f, j * 128:(j + 1) * 128], wd[:, f, hf * 512:(hf + 1) * 512],
                              start=(f == 0), stop=(f == NF - 1), reads=["actT", "wd"], writes=[key])
                    kb.v("dve", "tensor_tensor", xt[:, j, hf * 512:(hf + 1) * 512], xt[:, j, hf * 512:(hf + 1) * 512], p[:],
                         ALU.add, reads=[key, "xt"], writes=["xt"])
            if final:
                for j in range(2):
                    rmsnorm_rows(xt[:, j, :], j, 2 + j, nfin, xt[:, j, :], "xt")
            kb.dma(y_d[t0:t0 + TB, :].rearrange("(j p) d -> p j d", p=128), xt[:], reads=["xt"])
        kb.flush()


NT = T // 128


def build_band():
    m = Mix(384)
    kb, nc = m.kb, m.nc
    o_d = m.dram("oT", [128, T], BF16, "ExternalOutput")
    bias_d = m.dram("biasT", [128, 10, 128])
    biasf = m.sb("biasf", [128, 10, 128])
    biasb = m.sb("biasb", [128, 10, 128], BF16)
    kb.dma(biasf[:], bias_d, writes=["biasf"])
    kb.v("dve", "tensor_copy", biasb[:], biasf[:], reads=["biasf"], writes=["biasb"])
    qT = m.sb("qT", [128, T], BF16)
    kT = m.sb("kT", [128, T], BF16)
    v = m.sb("v", [128, NT, 2, 65], BF16)
    pT = [m.sb("pT%d" % i, [128, 128], BF16) for i in range(3)]
    osb = m.sb("osb", [128, 4, 128])
    rden = m.sb("rden", [128, 2])
    oT = m.sb("oT_s", [128, 512], BF16)
    pfm = [m.ps("pfm%d" % i, [128, 512]) for i in range(2)]
    ptm = m.ps("ptm", [128, 512])
    pO = [m.ps("pO%d" % i, [128, 2, 128]) for i in range(2)]
    ptf = m.ps("ptf", [128, 4, 128])
    kb.v("dve", "memset", v[:].rearrange("p a b c -> p (a b c)"), 1.0, writes=["v"])
    for blk in range(NBLK):
        m.frontend(blk)
        t0 = blk * TBM
        m.proj_fm(pfm[0][:], ("pfm", 0), 0)
        kb.v("dve", "tensor_scalar", qT[:, t0:t0 + TBM], pfm[0][:], 0.125, None, ALU.mult, reads=[("pfm", 0)], writes=["qT"])
        m.proj_fm(pfm[1][:], ("pfm", 1), 128)
        kb.v("dve", "tensor_copy", kT[:, t0:t0 + TBM], pfm[1][:], reads=[("pfm", 1)], writes=["kT"])
        for j in range(4):
            tile = blk * 4 + j
            m.proj_tm(ptm[:, 0:128], "ptm", j, 256, 128)
            kb.v("dve", "tensor_copy", v[:, tile, :, 0:64], ptm[:, 0:128].rearrange("p (h d) -> p h d", h=2),
                 reads=["ptm"], writes=["v"])
        items = []
        for j in range(4):
            qt = blk * 4 + j
            for h in range(2):
                deltas = [dl for dl in range(5) if qt - dl >= 0]
                for di, dl in enumerate(deltas):
                    items.append((j, qt, h, di, dl, len(deltas), len(items)))

        def emit_scores(itm):
            j, qt, h, di, dl, nd, it = itm
            hs = slice(h * 64, (h + 1) * 64)
            kt = qt - dl
            ps_ = pfm[it % 2]; psk = ("pfm", it % 2)
            kb.mm(ps_[:, 0:128], kT[hs, kt * 128:(kt + 1) * 128], qT[hs, qt * 128:(qt + 1) * 128], start=True, stop=False,
                  reads=["kT", "qT"], writes=[psk])
            kb.mm(ps_[:, 0:128], m.idb[:], biasb[:, h * 5 + dl, :], start=False, stop=True, reads=["idb", "biasb"], writes=[psk])

        def emit_rest(itm):
            j, qt, h, di, dl, nd, it = itm
            hs = slice(h * 64, (h + 1) * 64)
            kt = qt - dl
            ps_ = pfm[it % 2]; psk = ("pfm", it % 2)
            pt = pT[it % 3]; ptk = ("pT", it % 3)
            po = pO[qt % 2]; pok = ("pO", qt % 2)
            kb.act(pt[:], ps_[:, 0:128], AF.Exp, reads=[psk], writes=[ptk])
            kb.mm(po[:, h, 0:65], pt[:], v[:, kt, h, :], start=(di == 0), stop=(di == nd - 1), reads=[ptk, "v"], writes=[pok])
            if di == nd - 1:
                kb.v("dve", "reciprocal", rden[:, h:h + 1], po[:, h, 64:65], reads=[pok], writes=["rden"])
                kb.v("dve", "tensor_scalar", osb[:, j, hs], po[:, h, 0:64], rden[:, h:h + 1], None, ALU.mult,
                     reads=[pok, "rden"], writes=["osb"])
                if h == 1:
                    kb.op("pe", lambda e, j=j: e.transpose(ptf[:, j, :], osb[:, j, :], m.idf[:]), reads=["osb", "idf"], writes=["ptf"])

        emit_scores(items[0])
        for i, itm in enumerate(items):
            if i + 1 < len(items):
                emit_scores(items[i + 1])
            emit_rest(itm)
        kb.v("dve", "tensor_copy", oT[:], ptf[:].rearrange("p s q -> p (s q)"), reads=["ptf"], writes=["oT"])
        kb.dma(o_d[:, t0:t0 + TBM], oT[:], reads=["oT"])
    return m.finish()


def band_bias(rb2):
    kk = np.arange(128)[:, None]; qq = np.arange(128)[None, :]
    out = np.empty((128, 10, 128), np.float32)
    for h in range(2):
        for dl in range(5):
            dist = 128 * dl + qq - kk
            idx = np.clip(dist, -256, 256) + 256
            cd = 2 * dl + qq // 64 - kk // 64
            valid = (cd >= 0) & (cd <= 8)
            out[:, h * 5 + dl, :] = np.where(valid, rb2[h][idx], np.float32(-30000.0))
    return out


NT = T // 128


def build_fox():
    m = Mix(386)
    kb, nc = m.kb, m.nc
    fb_d = m.dram("f_bias", [128, 2])
    o_d = m.dram("oT", [128, T], BF16, "ExternalOutput")
    trif = m.const("trif", [128, 128])
    umat = m.const("umat", [128, 128])
    sel = m.const("sel127", [128, 128])
    onesf = m.sb("onesf", [128, 128])
    kb.v("dve", "memset", onesf[:], 1.0, writes=["onesf"])
    trib = m.sb("trib", [128, 128], BF16)
    kb.v("dve", "tensor_copy", trib[:], trif[:], reads=["trif"], writes=["trib"])
    nfb = m.sb("nfb", [128, 2])
    kb.dma(nfb[:], fb_d[:, :], writes=["nfb"])
    kb.v("dve", "tensor_scalar", nfb[:], nfb[:], -1.0, None, ALU.mult, reads=["nfb"], writes=["nfb"])

    qT = m.sb("qT", [128, T], BF16)
    kT = m.sb("kT", [128, T], BF16)
    v = m.sb("v", [128, NT, 2, 128], BF16)
    lf = m.sb("lf", [128, 2, NT])
    F = m.sb("F", [128, 2, NT])
    totT = m.sb("totT", [128, 128])
    flast = m.sb("flast", [128, 2, NT])
    nbias = m.sb("nbias", [128, NT])
    dd = m.sb("dd", [128, 2, NT]); dhi = m.sb("dhi", [128, 2, NT], BF16); dhf = m.sb("dhf", [128, 2, NT]); dlo = m.sb("dlo", [128, 2, NT], BF16)
    frow = m.sb("frow", [128, 2, T], BF16)
    ones2 = m.sb("ones2", [128, 128], BF16); onesr = m.sb("onesr", [128, 128], BF16)
    pT = [m.sb("pT%d" % i, [128, 512], BF16) for i in range(3)]
    osb = m.sb("osb", [128, 512])
    rden = m.sb("rden", [64, 512])
    oT = [m.sb("oT_s%d" % i, [64, 512], BF16) for i in range(2)]
    sel65 = m.sb("sel65", [128, 64])
    kb.v("dve", "memset", sel65[:], 0.0, writes=["sel65"])
    kb.v("dve", "memset", sel65[64:65, :], 1.0, writes=["sel65"])
    pfm = [m.ps("pfm%d" % i, [128, 512]) for i in range(2)]
    ptm = m.ps("ptm", [128, 512])
    pO = [m.ps("pO%d" % i, [128, 512]) for i in range(2)]
    ptf = m.ps("ptf", [128, 512])

    kb.v("dve", "memset", v[:].rearrange("p a b c -> p (a b c)"), 0.0, writes=["v"])
    kb.v("dve", "memset", v[:, :, :, 64:65], 1.0, writes=["v"])
    kb.v("dve", "memset", lf[:].rearrange("p h t -> p (h t)"), 0.0, writes=["lf"])
    for blk in range(NBLK):
        m.frontend(blk)
        t0 = blk * TBM
        m.proj_fm(pfm[0][:], ("pfm", 0), 0)
        kb.v("dve", "tensor_scalar", qT[:, t0:t0 + TBM], pfm[0][:], 0.125, None, ALU.mult, reads=[("pfm", 0)], writes=["qT"])
        m.proj_fm(pfm[1][:], ("pfm", 1), 128)
        kb.v("dve", "tensor_copy", kT[:, t0:t0 + TBM], pfm[1][:], reads=[("pfm", 1)], writes=["kT"])
        for j in range(4):
            tile = blk * 4 + j
            m.proj_tm(ptm[:, 0:130], "ptm", j, 256, 130)
            kb.v("dve", "tensor_copy", v[:, tile, :, 0:64], ptm[:, 0:128].rearrange("p (h d) -> p h d", h=2),
                 reads=["ptm"], writes=["v"])
            kb.v("dve", "tensor_copy", lf[:, :, tile], ptm[:, 128:130], reads=["ptm"], writes=["lf"])
    for h in range(2):
        kb.act(lf[:, h, :], lf[:, h, :], AF.Exp, scale=-1.0, bias=nfb[:, h:h + 1], reads=["lf", "nfb"], writes=["lf"])
    kb.act(lf[:], lf[:], AF.Ln, bias=1.0, reads=["lf"], writes=["lf"])
    kb.v("dve", "tensor_scalar", lf[:], lf[:], -1.0, None, ALU.mult, reads=["lf"], writes=["lf"])
    lf2 = lf[:].rearrange("p h t -> p (h t)")
    kb.mm(ptm[:, 0:128], lf2, onesf[:], reads=["lf", "onesf"], writes=["ptm"])
    kb.v("dve", "tensor_copy", totT[:], ptm[:, 0:128], reads=["ptm"], writes=["totT"])
    kb.mm(ptm[:, 0:128], trif[:], lf2, start=True, stop=False, reads=["trif", "lf"], writes=["ptm"])
    kb.mm(ptm[:, 0:128], totT[:], umat[:], start=False, stop=True, reads=["totT", "umat"], writes=["ptm"])
    kb.v("dve", "tensor_copy", F[:].rearrange("p h t -> p (h t)"), ptm[:, 0:128], reads=["ptm"], writes=["F"])
    kb.mm(ptm[:, 0:128], sel[:], F[:].rearrange("p h t -> p (h t)"), reads=["sel127", "F"], writes=["ptm"])
    kb.v("dve", "tensor_copy", flast[:].rearrange("p h t -> p (h t)"), ptm[:, 0:128], reads=["ptm"], writes=["flast"])

    fl4 = flast[:].rearrange("p h (b s) -> p h b s", s=4)
    dd4 = dd[:].rearrange("p h (b s) -> p h b s", s=4)
    for s_ in range(4):
        kb.v("dve", "tensor_tensor", dd4[:, :, :, s_], fl4[:, :, :, s_], fl4[:, :, :, 3], ALU.subtract, reads=["flast"], writes=["dd"])
    kb.v("dve", "tensor_copy", dhi[:], dd[:], reads=["dd"], writes=["dhi"])
    kb.v("dve", "tensor_copy", dhf[:], dhi[:], reads=["dhi"], writes=["dhf"])
    kb.v("dve", "tensor_tensor", dd[:], dd[:], dhf[:], ALU.subtract, reads=["dd", "dhf"], writes=["dd"])
    kb.v("dve", "tensor_copy", dlo[:], dd[:], reads=["dd"], writes=["dlo"])
    kb.v("dve", "memset", frow[:].rearrange("p h t -> p (h t)"), 0.0, writes=["frow"])
    kb.v("dve", "memset", ones2[:], 0.0, writes=["ones2"])
    kb.v("dve", "memset", ones2[0:1, :], 1.0, writes=["ones2"])
    kb.v("dve", "memset", ones2[32:33, :], 1.0, writes=["ones2"])
    kb.v("dve", "memset", onesr[:], 1.0, writes=["onesr"])
    for h in range(2):
        for t_ in range(NT):
            kb.v("dve", "tensor_scalar", frow[0:1, h, t_ * 128:(t_ + 1) * 128], onesr[0:1, :], dhi[0:1, h, t_:t_ + 1], None, ALU.mult,
                 reads=["onesr", "dhi"], writes=["frow"])
            kb.v("pool", "tensor_scalar", frow[32:33, h, t_ * 128:(t_ + 1) * 128], onesr[32:33, :], dlo[32:33, h, t_:t_ + 1], None, ALU.mult,
                 reads=["onesr", "dlo"], writes=["frow"])
    pairs = []
    for qb in range(NBLK):
        for h in range(2):
            nkt = 4 * qb + 4
            for kt in range(nkt):
                pairs.append((qb, h, kt, nkt, len(pairs)))

    def emit_scores(p):
        qb, h, kt, nkt, it = p
        hs = slice(h * 64, (h + 1) * 64)
        q0 = qb * TBM
        j = max(0, kt - 4 * qb)
        n = TBM - 128 * j
        ps_ = pfm[it % 2]; psk = ("pfm", it % 2)
        kb.mm(ps_[:, 0:n], kT[hs, kt * 128:(kt + 1) * 128], qT[hs, q0 + 128 * j:q0 + TBM], start=True, stop=False,
              reads=["kT", "qT"], writes=[psk])
        kb.mm(ps_[:, 0:n], ones2[:], frow[:, h, q0 + 128 * j:q0 + TBM], start=False, stop=True,
              reads=["ones2", "frow"], writes=[psk])

    def emit_rest(p):
        qb, h, kt, nkt, it = p
        q0 = qb * TBM
        j = max(0, kt - 4 * qb)
        n = TBM - 128 * j
        ps_ = pfm[it % 2]; psk = ("pfm", it % 2)
        pt = pT[it % 3]; ptk = ("pT", it % 3)
        po = pO[(2 * qb + h) % 2]
        pok = ("pO", (2 * qb + h) % 2)
        if kt == 0:
            kb.v("dve", "tensor_scalar", nbias[:, 0:nkt], F[:, h, 0:nkt], flast[:, h, 4 * qb + 3:4 * qb + 4], -1.0,
                 ALU.subtract, ALU.mult, reads=["F", "flast"], writes=["nbias"])
        kb.act(pt[:, 0:n], ps_[:, 0:n], AF.Exp, bias=nbias[:, kt:kt + 1], reads=[psk, "nbias"], writes=[ptk])
        if kt >= 4 * qb:
            kb.v("pool", "tensor_tensor", pt[:, 0:128], pt[:, 0:128], trib[:], ALU.mult, reads=[ptk, "trib"], writes=[ptk])
        kb.mm(po[:, 128 * j:TBM], v[:, kt, h, :], pt[:, 0:n], start=(kt == 0), stop=(kt == nkt - 1),
              reads=[ptk, "v"], writes=[pok])
        if kt == nkt - 1:
            kb.act(osb[:], po[:, :], AF.Copy, reads=[pok], writes=["osb"])
            kb.mm(ptf[0:64, :], sel65[:], osb[:], reads=["sel65", "osb"], writes=["ptf"])
            kb.v("dve", "reciprocal", rden[:], ptf[0:64, :], reads=["ptf"], writes=["rden"])
            kb.v("dve", "tensor_tensor", oT[h][:], osb[0:64, :], rden[:], ALU.mult, reads=["osb", "rden"], writes=[("oT", h)])
            kb.dma(o_d[h * 64:(h + 1) * 64, q0:q0 + TBM], oT[h][:], reads=[("oT", h)])

    if pairs:
        emit_scores(pairs[0])
    for i, p in enumerate(pairs):
        if i + 1 < len(pairs):
            emit_scores(pairs[i + 1])
        emit_rest(p)
    return m.finish()


def fox_consts():
    j = np.arange(128)
    trif = (j[:, None] <= j[None, :]).astype(np.float32)
    hh = j // 64; tt = j % 64
    umat = ((hh[:, None] == hh[None, :]) & (tt[:, None] < tt[None, :])).astype(np.float32)
    sel = np.zeros((128, 128), np.float32); sel[127, :] = 1.0
    return {"trif": trif, "umat": umat, "sel127": sel, "ident": np.eye(128, dtype=np.float32)}


def build_ssd():
    m = Mix(514)
    kb, nc = m.kb, m.nc
    o_d = m.dram("oT", [128, T], BF16, "ExternalOutput")
    cw = m.const("convw", [128, 12])
    cb = m.const("convb", [128, 3])
    dtb = m.const("dtb", [128, 2])
    alog = m.const("alog", [128, 2])
    dsk = m.const("dskip", [128, 2])
    triblk = m.const("triblk", [128, 128])
    blkones = m.const("blkones", [128, 128])
    onesf = m.sb("onesf", [128, 128])
    kb.v("dve", "memset", onesf[:], 1.0, writes=["onesf"])
    na = m.sb("na", [128, 2])
    kb.act(na[:], alog[:], AF.Exp, reads=["alog"], writes=["na"])
    kb.v("dve", "tensor_scalar", na[:], na[:], -1.0, None, ALU.mult, reads=["na"], writes=["na"])
    raw = m.sb("raw", [128, 3, 515])
    acc = m.sb("acc", [128, 512])
    xbc = m.sb("xbc", [128, 3, 512])
    zs = m.sb("zs", [128, 4, 128])
    dt = m.sb("dt", [128, 4, 2])
    da = m.sb("da", [128, 2])
    dar = m.sb("dar", [128, 128])
    dc = m.sb("dc", [128, 2])
    tmp = m.sb("tmp", [128, 128])
    LT = m.sb("LT", [128, 128])
    wT = m.sb("wT", [128, 128])
    xtok = m.sb("xtok", [128, 128]); btok = m.sb("btok", [128, 128])
    bw = m.sb("bw", [128, 2]); Bw = [m.sb("Bw%d" % h, [128, 128]) for h in range(2)]
    edacs = m.sb("edacs", [128, 2])
    cdec = [m.sb("cdec%d" % h, [128, 128]) for h in range(2)]
    stateT = m.sb("stateT", [128, 128])
    ysb = m.sb("ysb", [128, 128])
    oT = m.sb("oT_s", [128, 512], BF16)
    pfm = [m.ps("pfm%d" % i, [128, 512]) for i in range(2)]
    ptm = pfm[1]
    pg = [m.ps("pg%d" % i, [128, 512]) for i in range(4)]
    kb.v("dve", "memset", raw[:].rearrange("p a b -> p (a b)"), 0.0, writes=["raw"])
    kb.v("dve", "memset", stateT[:], 0.0, writes=["stateT"])
    for blk in range(NBLK):
        m.frontend(blk)
        t0 = blk * TBM
        for i in range(3):
            p = pfm[i % 2]; pk = ("pfm", i % 2)
            m.proj_fm(p[:], pk, i * 128)
            kb.act(raw[:, i, 3:515], p[:], AF.Copy, reads=[pk], writes=["raw"])
            kb.v("dve", "tensor_scalar", acc[:], raw[:, i, 0:512], cw[:, 4 * i:4 * i + 1], cb[:, i:i + 1], ALU.mult, ALU.add,
                 reads=["raw", "convw", "convb"], writes=["acc"])
            for k in range(1, 4):
                kb.v("dve", "scalar_tensor_tensor", acc[:], raw[:, i, k:k + 512], cw[:, 4 * i + k:4 * i + k + 1], acc[:],
                     ALU.mult, ALU.add, reads=["raw", "convw", "acc"], writes=["acc"])
            kb.act(xbc[:, i, :], acc[:], AF.Exp, scale=-1.0, reads=["acc"], writes=["xbc"])
            kb.v("dve", "tensor_scalar", xbc[:, i, :], xbc[:, i, :], 1.0, None, ALU.add, reads=["xbc"], writes=["xbc"])
            kb.v("dve", "reciprocal", xbc[:, i, :], xbc[:, i, :], reads=["xbc"], writes=["xbc"])
            kb.v("dve", "tensor_tensor", xbc[:, i, :], xbc[:, i, :], acc[:], ALU.mult, reads=["xbc", "acc"], writes=["xbc"])
            kb.v("dve", "tensor_copy", raw[:, i, 0:3], raw[:, i, 512:515], reads=["raw"], writes=["raw"])
        for j in range(4):
            m.proj_tm(ptm[:, 0:130], ("pfm", 1), j, 384, 130)
            kb.act(zs[:, j, :], ptm[:, 0:128], AF.Exp, scale=-1.0, reads=[("pfm", 1)], writes=["zs"])
            kb.v("dve", "tensor_scalar", zs[:, j, :], zs[:, j, :], 1.0, None, ALU.add, reads=["zs"], writes=["zs"])
            kb.v("dve", "reciprocal", zs[:, j, :], zs[:, j, :], reads=["zs"], writes=["zs"])
            kb.v("dve", "tensor_tensor", zs[:, j, :], zs[:, j, :], ptm[:, 0:128], ALU.mult, reads=["zs", ("pfm", 1)], writes=["zs"])
            kb.v("dve", "tensor_tensor", dt[:, j, :], ptm[:, 128:130], dtb[:], ALU.add, reads=[("pfm", 1), "dtb"], writes=["dt"])
        kb.act(dt[:], dt[:], AF.Exp, reads=["dt"], writes=["dt"])
        kb.act(dt[:], dt[:], AF.Ln, bias=1.0, reads=["dt"], writes=["dt"])
        for j in range(4):
            sl = slice(j * 128, (j + 1) * 128)
            kb.op("pe", lambda e, sl=sl: e.transpose(pg[0][:, 0:128], xbc[:, 0, sl], m.idf[:]), reads=["xbc", "idf"], writes=[("pg", 0)])
            kb.op("pe", lambda e, sl=sl: e.transpose(pg[0][:, 128:256], xbc[:, 1, sl], m.idf[:]), reads=["xbc", "idf"], writes=[("pg", 0)])
            kb.v("dve", "tensor_copy", xtok[:], pg[0][:, 0:128], reads=[("pg", 0)], writes=["xtok"])
            kb.act(btok[:], pg[0][:, 128:256], AF.Copy, reads=[("pg", 0)], writes=["btok"])
            kb.v("dve", "tensor_tensor", da[:], dt[:, j, :], na[:], ALU.mult, reads=["dt", "na"], writes=["da"])
            kb.mm(pg[2][:, 0:128], xbc[:, 1, sl], xbc[:, 2, sl], reads=["xbc"], writes=[("pg", 2)])
            for h in range(2):
                hh = slice(h * 64, (h + 1) * 64)
                kb.v("dve", "tensor_scalar", dar[:], onesf[:], da[:, h:h + 1], None, ALU.mult, reads=["onesf", "da"], writes=["dar"])
                kb.mm(pg[1][:, 0:128], dar[:], triblk[:], reads=["dar", "triblk"], writes=[("pg", 1)])
                kb.mm(pg[1][:, 128:256], dar[:], blkones[:], reads=["dar", "blkones"], writes=[("pg", 1)])
                kb.mm(pg[1][:, 256:384], triblk[:], dar[:], reads=["dar", "triblk"], writes=[("pg", 1)])
                kb.mm(pg[1][:, 384:512], blkones[:], dar[:], reads=["dar", "blkones"], writes=[("pg", 1)])
                kb.v("dve", "tensor_copy", dc[:, 0:1], pg[1][:, 256:257], reads=[("pg", 1)], writes=["dc"])
                kb.v("dve", "tensor_copy", dc[:, 1:2], pg[1][:, 384:385], reads=[("pg", 1)], writes=["dc"])
                kb.v("dve", "tensor_scalar", tmp[:], pg[1][:, 0:128], dc[:, 0:1], 0.0, ALU.subtract, ALU.min,
                     reads=[("pg", 1), "dc"], writes=["tmp"])
                kb.act(LT[:], tmp[:], AF.Exp, reads=["tmp"], writes=["LT"])
                kb.v("dve", "scalar_tensor_tensor", LT[:], LT[:], dt[:, j, h:h + 1], triblk[:], ALU.mult, ALU.mult,
                     reads=["LT", "dt", "triblk"], writes=["LT"])
                kb.v("dve", "tensor_tensor", wT[:], LT[:], pg[2][:, 0:128], ALU.mult, reads=["LT", ("pg", 2)], writes=["wT"])
                kb.mm(pg[2][:, 128 + h * 64:128 + (h + 1) * 64], wT[:], xtok[:, hh], reads=["wT", "xtok"], writes=[("pg", 2)])
                kb.v("dve", "tensor_tensor", bw[:, h:h + 1], dc[:, 1:2], dc[:, 0:1], ALU.subtract, reads=["dc"], writes=["bw"])
                kb.act(bw[:, h:h + 1], bw[:, h:h + 1], AF.Exp, reads=["bw"], writes=["bw"])
                kb.v("dve", "tensor_tensor", bw[:, h:h + 1], bw[:, h:h + 1], dt[:, j, h:h + 1], ALU.mult, reads=["bw", "dt"], writes=["bw"])
                kb.v("dve", "tensor_scalar", Bw[h][:], btok[:], bw[:, h:h + 1], None, ALU.mult, reads=["btok", "bw"], writes=[("Bw", h)])
                kb.act(edacs[:, h:h + 1], dc[:, 0:1], AF.Exp, reads=["dc"], writes=["edacs"])
                kb.act(cdec[h][:], pg[1][:, 128:256], AF.Exp, reads=[("pg", 1)], writes=[("cdec", h)])
            kb.act(ysb[:], pg[2][:, 128:256], AF.Copy, reads=[("pg", 2)], writes=["ysb"])
            for c in range(2):
                cs = slice(c * 64, (c + 1) * 64)
                kb.mm(pg[3][:, 0:128], xbc[:, 2, sl], stateT[:], reads=["xbc", "stateT"], writes=[("pg", 3)])
                for h in range(2):
                    hh = slice(h * 64, (h + 1) * 64)
                    kb.v("dve", "scalar_tensor_tensor", ysb[cs, hh], pg[3][cs, hh], edacs[cs, h:h + 1], ysb[cs, hh], ALU.mult, ALU.add,
                         reads=[("pg", 3), "edacs", "ysb"], writes=["ysb"])
                for h in range(2):
                    hh = slice(h * 64, (h + 1) * 64)
                    kb.mm(pg[3][:, 128:256], Bw[h][cs, :], xtok[cs, :], reads=[("Bw", h), "xtok"], writes=[("pg", 3)])
                    kb.v("dve", "scalar_tensor_tensor", stateT[:, hh], stateT[:, hh], cdec[h][:, c * 64:c * 64 + 1], pg[3][:, 128 + h * 64:128 + (h + 1) * 64],
                         ALU.mult, ALU.add, reads=["stateT", ("cdec", h), ("pg", 3)], writes=["stateT"])
            for h in range(2):
                hh = slice(h * 64, (h + 1) * 64)
                kb.v("dve", "scalar_tensor_tensor", ysb[:, hh], xtok[:, hh], dsk[:, h:h + 1], ysb[:, hh], ALU.mult, ALU.add,
                     reads=["xtok", "dskip", "ysb"], writes=["ysb"])
            kb.v("dve", "tensor_tensor", ysb[:], ysb[:], zs[:, j, :], ALU.mult, reads=["ysb", "zs"], writes=["ysb"])
            kb.op("pe", lambda e: e.transpose(pg[0][:, 256:384], ysb[:], m.idf[:]), reads=["ysb", "idf"], writes=[("pg", 0)])
            kb.v("dve", "tensor_copy", oT[:, sl], pg[0][:, 256:384], reads=[("pg", 0)], writes=["oT"])
        kb.dma(o_d[:, t0:t0 + TBM], oT[:], reads=["oT"])
    return m.finish()


def chunk_consts_ssd_unused():
    j = np.arange(128)
    same = (j[:, None] // 64) == (j[None, :] // 64)
    return {"triblk": (same & (j[:, None] <= j[None, :])).astype(np.float32), "blkones": same.astype(np.float32)}


def build_gdn():
    m = Mix(514)
    kb, nc = m.kb, m.nc
    o_d = m.dram("oT", [128, T], BF16, "ExternalOutput")
    cw = m.const("convw", [128, 12])
    dtb = m.const("dtb", [128, 1])
    alog = m.const("alog", [128, 1])
    nrmw = m.const("nrmw", [128, 128])
    triblk = m.const("triblk", [128, 128])
    tristr = m.const("tristrict", [128, 128])
    blkones = m.const("blkones", [128, 128])
    idf = m.idf
    onesf = m.sb("onesf", [128, 128])
    kb.v("dve", "memset", onesf[:], 1.0, writes=["onesf"])
    na = m.sb("na", [128, 1])
    kb.act(na[:], alog[:], AF.Exp, reads=["alog"], writes=["na"])
    kb.v("dve", "tensor_scalar", na[:], na[:], -1.0, None, ALU.mult, reads=["na"], writes=["na"])
    T_ = lambda name, shape=[128, 128]: m.sb(name, shape)
    raw = T_("raw", [128, 3, 515]); acc = T_("acc", [128, 512]); qkv = T_("qkv", [128, 3, 512]); sq = T_("sq", [128, 512])
    zs = T_("zs", [128, 4, 128]); beta = T_("beta", [128, 4]); gg = T_("gg", [128, 4])
    ktok = T_("ktok"); vtok = T_("vtok"); gr = T_("gr"); br = T_("br"); dc = T_("dc", [128, 2]); tmp = T_("tmp")
    ET = T_("ET"); egc = T_("egc", [128, 1]); egcbc = T_("egcbc"); eglbc = T_("eglbc"); ksc = T_("ksc", [128, 1]); bcol = T_("bcol", [128, 1])
    NTt = T_("NTt"); AT = T_("AT"); A = T_("A")
    Pa = [T_("Pa0"), T_("Pa1")]; PTa = [T_("PTa0"), T_("PTa1")]; RT = T_("RT")
    vb = T_("vb"); kbt = T_("kbt"); u = T_("u"); wT = T_("wT"); attnT = T_("attnT"); qdT = T_("qdT"); kst = T_("kst")
    vnew = T_("vnew"); S = T_("S"); osb = T_("osb"); junk = T_("junk"); ssq = T_("ssq", [128, 1]); rstd = T_("rstd", [128, 1])
    otk = m.sb("otk", [128, 4, 128], BF16)
    oT = m.sb("oT_s", [128, 4, 128], BF16)
    pfm = [m.ps("pfm%d" % i, [128, 512]) for i in range(2)]
    pg = [m.ps("pg%d" % i, [128, 512]) for i in range(4)]
    ptm = pfm[1]

    def R(b, r):
        return pg[b][:, r * 128:(r + 1) * 128], ("pg", b)

    for t_, k_ in ((raw[:].rearrange("p a b -> p (a b)"), "raw"), (S[:], "S"), (vnew[:], "vnew")):
        kb.v("dve", "memset", t_, 0.0, writes=[k_])

    def silu_from(out, out_key, src, src_key):
        kb.act(out, src, AF.Exp, scale=-1.0, reads=[src_key], writes=[out_key])
        kb.v("dve", "tensor_scalar", out, out, 1.0, None, ALU.add, reads=[out_key], writes=[out_key])
        kb.v("dve", "reciprocal", out, out, reads=[out_key], writes=[out_key])
        kb.v("dve", "tensor_tensor", out, out, src, ALU.mult, reads=[out_key, src_key], writes=[out_key])

    def mm1(dst, lhsT, rhs, reads):
        ap, key = dst
        kb.mm(ap, lhsT, rhs, reads=reads, writes=[key])

    def trp(dst, src, reads):
        ap, key = dst
        kb.op("pe", lambda e: e.transpose(ap, src, idf[:]), reads=reads + ["idf"], writes=[key])
    for blk in range(NBLK):
        m.frontend(blk)
        t0 = blk * TBM
        for i in range(3):
            p = pfm[i % 2]; pk = ("pfm", i % 2)
            m.proj_fm(p[:], pk, i * 128)
            kb.act(raw[:, i, 3:515], p[:], AF.Copy, reads=[pk], writes=["raw"])
            kb.v("dve", "tensor_scalar", acc[:], raw[:, i, 0:512], cw[:, 4 * i:4 * i + 1], None, ALU.mult,
                 reads=["raw", "convw"], writes=["acc"])
            for k in range(1, 4):
                kb.v("dve", "scalar_tensor_tensor", acc[:], raw[:, i, k:k + 512], cw[:, 4 * i + k:4 * i + k + 1], acc[:],
                     ALU.mult, ALU.add, reads=["raw", "convw", "acc"], writes=["acc"])
            silu_from(qkv[:, i, :], ("qkv", i), acc[:], "acc")
            kb.v("dve", "tensor_copy", raw[:, i, 0:3], raw[:, i, 512:515], reads=["raw"], writes=["raw"])
        for i in range(2):
            kb.v("dve", "tensor_tensor", sq[:], qkv[:, i, :], qkv[:, i, :], ALU.mult, reads=[("qkv", i)], writes=["sq"])
            kb.mm(pfm[0][:], onesf[:], sq[:], reads=["onesf", "sq"], writes=[("pfm", 0)])
            kb.v("dve", "tensor_scalar", sq[:], pfm[0][:], EPS, None, ALU.add, reads=[("pfm", 0)], writes=["sq"])
            kb.act(sq[:], sq[:], AF.Sqrt, reads=["sq"], writes=["sq"])
            kb.v("dve", "reciprocal", sq[:], sq[:], reads=["sq"], writes=["sq"])
            kb.v("dve", "scalar_tensor_tensor", qkv[:, i, :], qkv[:, i, :], (128 ** -0.5) if i == 0 else 1.0, sq[:], ALU.mult, ALU.mult,
                 reads=[("qkv", i), "sq"], writes=[("qkv", i)])
        for j in range(4):
            m.proj_tm(ptm[:, 0:130], ("pfm", 1), j, 384, 130)
            silu_from(zs[:, j, :], "zs", ptm[:, 0:128], ("pfm", 1))
            kb.v("dve", "tensor_copy", beta[:, j:j + 1], ptm[:, 128:129], reads=[("pfm", 1)], writes=["beta"])
            kb.v("dve", "tensor_scalar", gg[:, j:j + 1], ptm[:, 129:130], dtb[:, 0:1], None, ALU.add, reads=[("pfm", 1), "dtb"], writes=["gg"])
        kb.act(beta[:], beta[:], AF.Exp, scale=-1.0, reads=["beta"], writes=["beta"])
        kb.v("dve", "tensor_scalar", beta[:], beta[:], 1.0, None, ALU.add, reads=["beta"], writes=["beta"])
        kb.v("dve", "reciprocal", beta[:], beta[:], reads=["beta"], writes=["beta"])
        kb.act(gg[:], gg[:], AF.Exp, reads=["gg"], writes=["gg"])
        kb.act(gg[:], gg[:], AF.Ln, bias=1.0, reads=["gg"], writes=["gg"])
        kb.v("dve", "tensor_scalar", gg[:], gg[:], na[:, 0:1], None, ALU.mult, reads=["gg", "na"], writes=["gg"])
        for j in range(4):
            sl = slice(j * 128, (j + 1) * 128)
            qn, kn, vf = qkv[:, 0, sl], qkv[:, 1, sl], qkv[:, 2, sl]
            trp(R(0, 0), kn, [("qkv", 1)]); trp(R(0, 1), vf, [("qkv", 2)])
            kb.v("dve", "tensor_copy", ktok[:], R(0, 0)[0], reads=[R(0, 0)[1]], writes=["ktok"])
            kb.act(vtok[:], R(0, 1)[0], AF.Copy, reads=[R(0, 1)[1]], writes=["vtok"])
            kb.v("dve", "tensor_scalar", gr[:], onesf[:], gg[:, j:j + 1], None, ALU.mult, reads=["onesf", "gg"], writes=["gr"])
            kb.v("dve", "tensor_scalar", br[:], onesf[:], beta[:, j:j + 1], None, ALU.mult, reads=["onesf", "beta"], writes=["br"])
            mm1(R(1, 0), gr[:], triblk[:], ["gr", "triblk"])
            mm1(R(1, 1), gr[:], blkones[:], ["gr", "blkones"])
            mm1(R(1, 2), triblk[:], gr[:], ["gr", "triblk"])
            mm1(R(1, 3), blkones[:], gr[:], ["gr", "blkones"])
            mm1(R(2, 0), br[:], idf[:], ["br", "idf"])
            kb.v("dve", "tensor_copy", dc[:, 0:1], pg[1][:, 256:257], reads=[("pg", 1)], writes=["dc"])
            kb.v("dve", "tensor_copy", dc[:, 1:2], pg[1][:, 384:385], reads=[("pg", 1)], writes=["dc"])
            kb.v("dve", "tensor_scalar", tmp[:], R(1, 0)[0], dc[:, 0:1], 0.0, ALU.subtract, ALU.min, reads=[("pg", 1), "dc"], writes=["tmp"])
            kb.act(ET[:], tmp[:], AF.Exp, reads=["tmp"], writes=["ET"])
            kb.v("dve", "tensor_tensor", ET[:], ET[:], triblk[:], ALU.mult, reads=["ET", "triblk"], writes=["ET"])
            kb.act(egcbc[:], R(1, 0)[0], AF.Exp, reads=[("pg", 1)], writes=["egcbc"])
            kb.act(eglbc[:], R(1, 1)[0], AF.Exp, reads=[("pg", 1)], writes=["eglbc"])
            kb.act(egc[:], dc[:, 0:1], AF.Exp, reads=["dc"], writes=["egc"])
            kb.v("dve", "tensor_tensor", ksc[:], dc[:, 1:2], dc[:, 0:1], ALU.subtract, reads=["dc"], writes=["ksc"])
            kb.act(ksc[:], ksc[:], AF.Exp, reads=["ksc"], writes=["ksc"])
            kb.v("dve", "tensor_tensor", bcol[:], egc[:], beta[:, j:j + 1], ALU.mult, reads=["egc", "beta"], writes=["bcol"])
            mm1(R(2, 1), kn, kn, [("qkv", 1)])
            kb.v("dve", "tensor_tensor", NTt[:], ET[:], R(2, 1)[0], ALU.mult, reads=["ET", ("pg", 2)], writes=["NTt"])
            kb.v("dve", "tensor_tensor", NTt[:], NTt[:], tristr[:], ALU.mult, reads=["NTt", "tristrict"], writes=["NTt"])
            kb.v("dve", "tensor_tensor", AT[:], NTt[:], R(2, 0)[0], ALU.mult, reads=["NTt", ("pg", 2)], writes=["AT"])
            trp(R(0, 2), AT[:], ["AT"])
            kb.v("dve", "tensor_copy", A[:], R(0, 2)[0], reads=[("pg", 0)], writes=["A"])
            kb.v("dve", "tensor_tensor", RT[:], idf[:], AT[:], ALU.subtract, reads=["idf", "AT"], writes=["RT"])
            P, PT, Pk, PTk = A, AT, "A", "AT"
            for lv in range(1, 6):
                Pn, PTn = Pa[lv % 2], PTa[lv % 2]
                Pnk, PTnk = ("Pa", lv % 2), ("PTa", lv % 2)
                mm1(R(3, 0), PT[:], P[:], [Pk, PTk])
                kb.v("dve", "tensor_copy", Pn[:], R(3, 0)[0], reads=[("pg", 3)], writes=[Pnk])
                if lv < 5:
                    mm1(R(3, 1), P[:], PT[:], [Pk, PTk])
                    kb.act(PTn[:], R(3, 1)[0], AF.Copy, reads=[("pg", 3)], writes=[PTnk])
                mm1(R(3, 2), Pn[:], RT[:], [Pnk, "RT"])
                kb.v("dve", "tensor_tensor", RT[:], RT[:], R(3, 2)[0], ALU.add, reads=["RT", ("pg", 3)], writes=["RT"])
                P, PT, Pk, PTk = Pn, PTn, Pnk, PTnk
            kb.v("dve", "tensor_scalar", vb[:], vtok[:], beta[:, j:j + 1], None, ALU.mult, reads=["vtok", "beta"], writes=["vb"])
            kb.v("dve", "tensor_scalar", kbt[:], ktok[:], bcol[:, 0:1], None, ALU.mult, reads=["ktok", "bcol"], writes=["kbt"])
            mm1(R(2, 3), RT[:], vb[:], ["RT", "vb"])
            kb.v("dve", "tensor_copy", u[:], R(2, 3)[0], reads=[("pg", 2)], writes=["u"])
            mm1(R(3, 3), kbt[:], RT[:], ["kbt", "RT"])
            kb.act(wT[:], R(3, 3)[0], AF.Copy, reads=[("pg", 3)], writes=["wT"])
            mm1(R(2, 2), kn, qn, [("qkv", 0), ("qkv", 1)])
            kb.v("dve", "tensor_tensor", attnT[:], ET[:], R(2, 2)[0], ALU.mult, reads=["ET", ("pg", 2)], writes=["attnT"])
            kb.v("dve", "tensor_tensor", qdT[:], qn, egcbc[:], ALU.mult, reads=[("qkv", 0), "egcbc"], writes=["qdT"])
            kb.v("dve", "tensor_scalar", kst[:], ktok[:], ksc[:, 0:1], None, ALU.mult, reads=["ktok", "ksc"], writes=["kst"])
            for c in range(2):
                cs = slice(c * 64, (c + 1) * 64)
                mm1(R(1, 0), wT[:], S[:], ["wT", "S"])
                kb.v("dve", "tensor_tensor", vnew[cs, :], u[cs, :], pg[1][cs, 0:128], ALU.subtract, reads=["u", ("pg", 1)], writes=["vnew"])
                kb.mm(R(1, 1)[0], qdT[:], S[:], start=True, stop=False, reads=["qdT", "S"], writes=[("pg", 1)])
                kb.mm(R(1, 1)[0], attnT[:], vnew[:], start=False, stop=True, reads=["attnT", "vnew"], writes=[("pg", 1)])
                kb.v("dve", "tensor_copy", osb[cs, :], pg[1][cs, 128:256], reads=[("pg", 1)], writes=["osb"])
                mm1(R(1, 2), kst[cs, :], vnew[cs, :], ["kst", "vnew"])
                kb.v("dve", "scalar_tensor_tensor", S[:], S[:], eglbc[:, c * 64:c * 64 + 1], R(1, 2)[0], ALU.mult, ALU.add,
                     reads=["S", "eglbc", ("pg", 1)], writes=["S"])
            kb.act(junk[:], osb[:], AF.Square, accum_out=ssq[:], reads=["osb"], writes=["junk", "ssq"])
            kb.v("dve", "tensor_scalar", rstd[:], ssq[:], 1.0 / 128, EPS, ALU.mult, ALU.add, reads=["ssq"], writes=["rstd"])
            kb.act(rstd[:], rstd[:], AF.Sqrt, reads=["rstd"], writes=["rstd"])
            kb.v("dve", "reciprocal", rstd[:], rstd[:], reads=["rstd"], writes=["rstd"])
            kb.v("dve", "scalar_tensor_tensor", osb[:], osb[:], rstd[:, 0:1], nrmw[:], ALU.mult, ALU.mult, reads=["osb", "rstd", "nrmw"], writes=["osb"])
            kb.v("dve", "tensor_tensor", osb[:], osb[:], zs[:, j, :], ALU.mult, reads=["osb", "zs"], writes=["osb"])
            kb.v("dve", "tensor_copy", otk[:, j, :], osb[:], reads=["osb"], writes=["otk"])
        for j in range(4):
            kb.tr(m.ptr[0][:, j, :], otk[:, j, :], m.idb[:], reads=["otk", "idb"], writes=[("ptr", 0)])
        kb.v("dve", "tensor_copy", oT[:], m.ptr[0][:], reads=[("ptr", 0)], writes=["oT"])
        kb.dma(o_d[:, t0:t0 + TBM], oT[:].rearrange("p j t -> p (j t)"), reads=["oT"])
    return m.finish()


def chunk_consts():
    j = np.arange(128)
    same = (j[:, None] // 64) == (j[None, :] // 64)
    return {"triblk": (same & (j[:, None] <= j[None, :])).astype(np.float32), "blkones": same.astype(np.float32),
            "tristrict": (same & (j[:, None] < j[None, :])).astype(np.float32)}


def gdn_maps(inputs, xs, ident):
    f32 = np.float32
    W = np.asarray(inputs["ab_w_in"][0], f32)
    cwf = np.asarray(inputs["ab_conv_w"][0], f32)
    maps = []
    for c in range(8):
        b, g = c // 4, c % 4
        ar = np.arange(g * 128, (g + 1) * 128)
        cols = np.concatenate([1536 + ar, 2048 + ar, 2560 + ar, 3080 + ar, [3072 + g], [3076 + g]])
        convw = np.stack([cwf[:, ch].T for ch in (ar, 512 + ar, 1024 + ar)], 1).reshape(128, 12)
        one = lambda v: np.full((128, 1), v, f32)
        mm = {"x": np.ascontiguousarray(xs[b]), "w_in": np.ascontiguousarray(W[:, cols]),
              "norm_w": np.asarray(inputs["norm_mix"][0], f32)[None, :], "ident": ident,
              "convw": np.ascontiguousarray(convw), "dtb": one(inputs["ab_dt_bias"][0][g]), "alog": one(inputs["ab_a_log"][0][g]),
              "nrmw": np.ascontiguousarray(np.broadcast_to(np.asarray(inputs["ab_norm_w"][0], f32)[None, :], (128, 128)))}
        mm.update(chunk_consts())
        maps.append(mm)
    return maps


def build_program():
    nc = bass.Bass("TRN2", target_bir_lowering=False)
    kb = KB(nc)
    _CTX.clear()
    _CTX.update(nc=nc, kb=kb)
    x_d = nc.dram_tensor("x", [T, D], F32, kind="ExternalInput").ap()
    y_d = nc.dram_tensor("y", [T, D], F32, kind="ExternalOutput").ap()
    omT0 = nc.dram_tensor("omT0", [1024, T], BF16).ap()
    omT1 = nc.dram_tensor("omT1", [1024, T], BF16).ap()
    x1 = nc.dram_tensor("x1", [T, D], F32).ap()

    def run(tag, fn, **ctx):
        _CTX.update(tag=tag, **ctx)
        fn()

    for g in range(4):
        run("band%d" % g, build_band, x=x_d, out=omT0[g * 128:(g + 1) * 128, :])
    for g in range(4):
        run("gdn%d" % g, build_gdn, x=x_d, out=omT0[512 + g * 128:512 + (g + 1) * 128, :])
    run("tok0", lambda: build_tok(False), x=x_d, omT=omT0, out=x1)
    for g in range(4):
        run("fox%d" % g, build_fox, x=x1, out=omT1[512 + g * 128:512 + (g + 1) * 128, :])
    for g in range(4):
        run("ssd%d" % g, build_ssd, x=x1, out=omT1[g * 128:(g + 1) * 128, :])
    run("tok1", lambda: build_tok(True), x=x1, omT=omT1, out=y_d)
    kb.close()
    return nc


def host_maps(inputs, b):
    f32 = np.float32
    A = lambda k: np.asarray(inputs[k], f32)
    ident = np.eye(128, dtype=f32)
    bc = lambda v: np.ascontiguousarray(np.broadcast_to(np.asarray(v, f32)[None, :], (128, len(v))))
    m = {"x": np.ascontiguousarray(A("x")[b])}

    def put(tag, d):
        for k, v in d.items():
            m["%s_%s" % (k, tag)] = v

    W0 = A("ab_w_in")[0]; rb = A("ab_rel_bias")[0]
    W1 = A("cd_w_in")[0]
    nm0 = A("norm_mix")[0][None, :]; nm1 = A("norm_mix")[1][None, :]
    gm = gdn_maps(inputs, [None, None], ident)
    cwf = A("cd_conv_w")[0]; cbf = A("cd_conv_b")[0]
    cc = chunk_consts(); fc = fox_consts()
    for g in range(4):
        ar = np.arange(g * 128, (g + 1) * 128)
        put("band%d" % g, {"w_in": np.ascontiguousarray(W0[:, np.concatenate([ar, 512 + ar, 1024 + ar])]), "norm_w": nm0,
                           "ident": ident, "biasT": band_bias(rb[2 * g:2 * g + 2])})
        d = dict(gm[g]); d.pop("x")
        put("gdn%d" % g, d)
        cols = np.concatenate([1544 + ar, 2056 + ar, 2568 + ar, 3080 + np.arange(2 * g, 2 * g + 2)])
        d = {"w_in": np.ascontiguousarray(W1[:, cols]), "norm_w": nm1, "f_bias": bc(A("cd_f_bias")[0][2 * g:2 * g + 2])}
        d.update(fc)
        put("fox%d" % g, d)
        G = g // 2
        Bch = 512 + G * 128 + np.arange(128); Cch = 768 + G * 128 + np.arange(128)
        cols = np.concatenate([512 + ar, 512 + Bch, 512 + Cch, ar, 1536 + np.arange(2 * g, 2 * g + 2)])
        convw = np.stack([cwf[:, ch].T for ch in (ar, Bch, Cch)], 1).reshape(128, 12)
        convb = np.stack([cbf[ch] for ch in (ar, Bch, Cch)], 1)
        put("ssd%d" % g, {"w_in": np.ascontiguousarray(W1[:, cols]), "norm_w": nm1, "ident": ident,
                          "convw": np.ascontiguousarray(convw), "convb": np.ascontiguousarray(convb),
                          "dtb": bc(A("cd_dt_bias")[0][2 * g:2 * g + 2]), "alog": bc(A("cd_a_log")[0][2 * g:2 * g + 2]),
                          "dskip": bc(A("cd_d_skip")[0][2 * g:2 * g + 2]), "triblk": cc["triblk"], "blkones": cc["blkones"]})
    for L, wout in ((0, A("ab_w_out")[0]), (1, A("cd_w_out")[0])):
        d = {"w_out": wout, "w_gate": A("ffn_w_gate")[L], "w_up": A("ffn_w_up")[L], "w_down": A("ffn_w_down")[L],
             "norm_ffn": A("norm_ffn")[L][None, :], "ident": ident}
        if L == 1:
            d["norm_final"] = A("norm_final")[None, :]
            d["gnw"] = np.ascontiguousarray(A("cd_norm_w")[0].reshape(4, 128).T)
        put("tok%d" % L, d)
    return m


def kernel(**inputs):
    nc = build_program()
    maps = [host_maps(inputs, b) for b in range(2)]
    res = run_bass_kernel_spmd(nc, maps, core_ids=[0, 1])
    out = np.stack([np.asarray(res.results[b]["y"]) for b in range(2)], 0)
    return out.astype(np.float32)
```

```python
import contextlib
import os
import numpy as np
import ml_dtypes
import concourse.bass as bass
import concourse.mybir as mybir
from concourse.bass_utils import run_bass_kernel_spmd


F32 = mybir.dt.float32
BF16 = mybir.dt.bfloat16
AF = mybir.ActivationFunctionType
ALU = mybir.AluOpType
AX = mybir.AxisListType

ENGS = ["pe", "dve", "act", "pool", "sp"]
EPOCH = 30000
NDMA = 24


class KB:
    def __init__(self, nc):
        self.nc = nc
        self.ops = {e: [] for e in ENGS}
        self.cnt = {e: 0 for e in ENGS}
        self.lastw = {}
        self.readers = {}
        self.seen = {e: {} for e in ENGS}
        self.dma_i = 0
        self.semnames = set()
        self.dma_tokens = {}

    def _deps(self, eng, reads, writes):
        toks = {}
        def add(t):
            if t is None:
                return
            s, v = t
            if toks.get(s, 0) < v:
                toks[s] = v
        for k in reads:
            add(self.lastw.get(k))
        for k in writes:
            add(self.lastw.get(k))
            for s, v in self.readers.get(k, {}).items():
                add((s, v))
        waits = []
        for s, v in toks.items():
            if eng == "pe" and s.startswith("pe"):
                continue
            if self.seen[eng].get(s, 0) >= v:
                continue
            self.seen[eng][s] = v
            waits.append((s, v))
        return waits

    def _commit(self, tok, reads, writes):
        for k in reads:
            d = self.readers.setdefault(k, {})
            if d.get(tok[0], 0) < tok[1]:
                d[tok[0]] = tok[1]
        for k in writes:
            self.lastw[k] = tok
            self.readers[k] = {}

    def op(self, eng, fn, reads=(), writes=()):
        waits = self._deps(eng, reads, writes)
        n = self.cnt[eng]
        self.cnt[eng] = n + 1
        s = "%s%d" % (eng, n // EPOCH)
        self.semnames.add(s)
        tok = (s, n % EPOCH + 1)
        self.ops[eng].append((waits, fn, (s, 1)))
        self._commit(tok, reads, writes)
        return tok

    def dma(self, out, in_, reads=(), writes=(), eng="sp", **kw):
        i = self.dma_i
        self.dma_i += 1
        s = "dma%d" % (i % NDMA)
        self.semnames.add(s)
        waits = self._deps(eng, reads, writes)
        prev = 16 * (i // NDMA)
        if prev > 0 and self.seen[eng].get(s, 0) < prev:
            self.seen[eng][s] = prev
            waits.append((s, prev))
        tok = (s, prev + 16)
        self.ops[eng].append((waits, lambda e: e.dma_start(out=out, in_=in_, **kw), (s, 16)))
        self._commit(tok, reads, writes)
        self.dma_tokens[s] = tok[1]
        return tok

    def mm(self, out, lhsT, rhs, start=True, stop=True, reads=(), writes=(), **kw):
        return self.op("pe", lambda e: e.matmul(out, lhsT, rhs, start=start, stop=stop, **kw), reads, writes)

    def tr(self, out, in_, ident, reads=(), writes=()):
        return self.op("pe", lambda e: e.transpose(out, in_, ident), reads, writes)

    def act(self, out, in_, func, reads=(), writes=(), **kw):
        return self.op("act", lambda e: e.activation(out, in_, func, **kw), reads, writes)

    def v(self, eng, name, *args, reads=(), writes=(), **kw):
        return self.op(eng, lambda e: getattr(e, name)(*args, **kw), reads, writes)

    def emit(self):
        nc = self.nc
        with contextlib.ExitStack() as st:
            sems = {}
            for s in sorted(self.semnames):
                sems[s] = st.enter_context(nc.semaphore(s))
            finals = []
            for e in ENGS:
                if e == "sp" or self.cnt[e] == 0:
                    continue
                n = self.cnt[e] - 1
                finals.append(("%s%d" % (e, n // EPOCH), n % EPOCH + 1))
            for s, v in self.dma_tokens.items():
                finals.append((s, v))
            block = st.enter_context(nc.Block())

            def replay(eng, lst, extra=()):
                for waits, fn, inc in lst:
                    for s, v in waits:
                        eng.wait_ge(sems[s], v)
                    fn(eng).then_inc(sems[inc[0]], inc[1])
                for s, v in extra:
                    eng.wait_ge(sems[s], v)

            @block.tensor
            def _(eng):
                replay(eng, self.ops["pe"])

            @block.vector
            def _(eng):
                replay(eng, self.ops["dve"])

            @block.scalar
            def _(eng):
                replay(eng, self.ops["act"])

            @block.gpsimd
            def _(eng):
                replay(eng, self.ops["pool"])

            @block.sync
            def _(eng):
                replay(eng, self.ops["sp"], finals)


D = 1024
T = 8192
TBM = 512
NBLK = T // TBM
EPS = 1e-6


class Mix:
    def __init__(self, ncols, xdtype_norm=True):
        self.nc = nc = bass.Bass("TRN2", target_bir_lowering=False)
        self.kb = KB(nc)
        self.st = contextlib.ExitStack()
        self.ncols = ncols
        self.x_d = self.dram("x", [T, D])
        self.w_d = self.dram("w_in", [D, ncols])
        self.nw_d = self.dram("norm_w", [1, D])
        self.id_d = self.dram("ident", [128, 128])
        kb = self.kb
        self.w = self.sb("w", [128, 8, ncols], BF16)
        self.nw = self.sb("nw", [128, D])
        self.idf = self.sb("idf", [128, 128])
        self.idb = self.sb("idb", [128, 128], BF16)
        self.xt = self.sb("xt", [128, 4, D])
        self.hb = self.sb("hb", [128, 4, D], BF16)
        self.hT = self.sb("hT", [128, 8, TBM], BF16)
        self.ss = self.sb("ss", [128, 4])
        self.rs = self.sb("rs", [128, 4])
        self.ptr = [self.ps("ptr%d" % i, [128, 4, 128], BF16) for i in range(2)]
        kb.dma(self.idf[:], self.id_d[:, :], writes=["idf"])
        kb.v("dve", "tensor_copy", self.idb[:], self.idf[:], reads=["idf"], writes=["idb"])
        kb.dma(self.nw[:], self.nw_d[0:1, :].to_broadcast([128, D]), writes=["nw"])
        stg = self.xt
        i = 0
        for kc in range(8):
            for n0 in range(0, ncols, 1024):
                n1 = min(ncols, n0 + 1024)
                s = i % 4
                i += 1
                kb.dma(stg[:, s, 0:n1 - n0], self.w_d[kc * 128:(kc + 1) * 128, n0:n1], writes=[("xt", s)])
                eng = ["dve", "pool"][s % 2]
                kb.v(eng, "tensor_copy", self.w[:, kc, n0:n1], stg[:, s, 0:n1 - n0], reads=[("xt", s)], writes=["w"])

    def dram(self, name, shape, dt=F32, kind="ExternalInput"):
        return self.nc.dram_tensor(name, shape, dt, kind=kind).ap()

    def sb(self, name, shape, dt=F32):
        return self.st.enter_context(self.nc.sbuf_tensor(name, shape, dt))

    def ps(self, name, shape, dt=F32):
        return self.st.enter_context(self.nc.psum_tensor(name, shape, dt))

    def const(self, name, shape, bf=False):
        d = self.dram(name, shape)
        t = self.sb(name + "_s", shape)
        self.kb.dma(t[:], d, writes=[name])
        if bf:
            tb = self.sb(name + "_b", shape, BF16)
            self.kb.v("dve", "tensor_copy", tb[:], t[:], reads=[name], writes=[name + "_b"])
            return t, tb
        return t

    def frontend(self, blk):
        kb = self.kb
        xt, hb, hT, ss, rs = self.xt, self.hb, self.hT, self.ss, self.rs
        t0 = blk * TBM
        for j in range(4):
            kb.dma(xt[:, j, :], self.x_d[t0 + j * 128:t0 + (j + 1) * 128, :], writes=[("xt", j)])
        for j in range(4):
            kb.act(hb[:, j, :], xt[:, j, :], AF.Square, accum_out=ss[:, j:j + 1], reads=[("xt", j)], writes=[("hb", j), ("ss", j)])
            kb.v("dve", "tensor_scalar", rs[:, j:j + 1], ss[:, j:j + 1], 1.0 / D, EPS, ALU.mult, ALU.add,
                 reads=[("ss", j)], writes=[("rs", j)])
            kb.act(rs[:, j:j + 1], rs[:, j:j + 1], AF.Sqrt, reads=[("rs", j)], writes=[("rs", j)])
            kb.v("dve", "reciprocal", rs[:, j:j + 1], rs[:, j:j + 1], reads=[("rs", j)], writes=[("rs", j)])
            kb.v("dve", "scalar_tensor_tensor", hb[:, j, :], xt[:, j, :], rs[:, j:j + 1], self.nw[:], ALU.mult, ALU.mult,
                 reads=[("xt", j), ("rs", j), "nw"], writes=[("hb", j)])
            for q4 in range(2):
                pt = self.ptr[q4]
                for i in range(4):
                    kc = q4 * 4 + i
                    kb.tr(pt[:, i, :], hb[:, j, kc * 128:(kc + 1) * 128], self.idb[:], reads=[("hb", j), "idb"], writes=[("ptr", q4)])
                if q4 == 0:
                    kb.v("dve", "tensor_copy", hT[:, 0:4, j * 128:(j + 1) * 128], pt[:], reads=[("ptr", q4)], writes=["hT"])
                else:
                    kb.act(hT[:, 4:8, j * 128:(j + 1) * 128], pt[:], AF.Copy, reads=[("ptr", q4)], writes=["hT"])

    def proj_fm(self, psum, pkey, c0, ncol=128):
        for kc in range(8):
            self.kb.mm(psum, self.w[:, kc, c0:c0 + ncol], self.hT[:, kc, :], start=(kc == 0), stop=(kc == 7),
                       reads=["w", "hT"], writes=[pkey])

    def proj_tm(self, psum, pkey, j, c0, ncol):
        for kc in range(8):
            self.kb.mm(psum, self.hT[:, kc, j * 128:(j + 1) * 128], self.w[:, kc, c0:c0 + ncol], start=(kc == 0), stop=(kc == 7),
                       reads=["w", "hT"], writes=[pkey])

    def finish(self):
        self.kb.emit()
        self.st.close()
        return self.nc


D = 1024
DFF = 2816
NF = DFF // 128
TOK = 2048
TB = 256
EPS = 1e-6


def build_tok(final):
    nc = bass.Bass("TRN2", target_bir_lowering=False)
    dr = lambda name, shape, dt=F32, kind="ExternalInput": nc.dram_tensor(name, shape, dt, kind=kind).ap()
    x_d = dr("x", [TOK, D])
    om_d = dr("omT", [D, TOK], BF16)
    wo_d = dr("w_out", [D, D])
    wg_d = dr("w_gate", [D, DFF])
    wu_d = dr("w_up", [D, DFF])
    wd_d = dr("w_down", [DFF, D])
    nf_d = dr("norm_ffn", [1, D])
    id_d = dr("ident", [128, 128])
    if final:
        nfin_d = dr("norm_final", [1, D])
        gnw_d = dr("gnw", [128, 4])
    y_d = dr("y", [TOK, D], F32, "ExternalOutput")
    kb = KB(nc)
    with contextlib.ExitStack() as st:
        sb = lambda name, shape, dt=F32: st.enter_context(nc.sbuf_tensor(name, shape, dt))
        ps = lambda name, shape, dt=F32: st.enter_context(nc.psum_tensor(name, shape, dt))
        wo = sb("wo", [128, 8, D], BF16)
        wg = sb("wg", [128, 8, DFF], BF16)
        wu = sb("wu", [128, 8, DFF], BF16)
        wd = sb("wd", [128, NF, D], BF16)
        stg = [sb("stg%d" % i, [128, 1024]) for i in range(3)]
        nfw = sb("nfw", [128, D])
        idf = sb("idf", [128, 128]); idb = sb("idb", [128, 128], BF16)
        ones = sb("ones", [128, 128], BF16)
        xt = sb("xt", [128, 2, D])
        om = sb("om", [128, 8, TB], BF16)
        hb = sb("hb", [128, 2, D], BF16)
        hT = sb("hT", [128, 8, TB], BF16)
        actT = sb("actT", [128, NF, TB], BF16)
        sg = [sb("sg%d" % i, [128, TB]) for i in range(2)]
        ss = sb("ss", [128, 4]); rs = sb("rs", [128, 4])
        if final:
            nfin = sb("nfin", [128, D]); gnw = sb("gnw_s", [128, 4])
            sq = sb("sq", [128, 4, TB], BF16); rg = sb("rg", [128, TB])
        pmm = [ps("pmm%d" % i, [128, 512]) for i in range(2)]
        pg = [ps("pg%d" % i, [128, 512]) for i in range(2)]
        pu = [ps("pu%d" % i, [128, 512]) for i in range(2)]
        ptr = [ps("ptr%d" % i, [128, 4, 128], BF16) for i in range(2)]

        kb.dma(idf[:], id_d[:, :], writes=["idf"])
        kb.v("dve", "tensor_copy", idb[:], idf[:], reads=["idf"], writes=["idb"])
        kb.v("dve", "memset", ones[:], 1.0, writes=["ones"])
        kb.dma(nfw[:], nf_d[0:1, :].to_broadcast([128, D]), writes=["nfw"])
        if final:
            kb.dma(nfin[:], nfin_d[0:1, :].to_broadcast([128, D]), writes=["nfin"])
            kb.dma(gnw[:], gnw_d[:, :], writes=["gnw"])
        cvt_i = [0]
        def load_w(dst, src, K, N, name):
            for kc in range(K // 128):
                for n0 in range(0, N, 1024):
                    n1 = min(N, n0 + 1024)
                    i = cvt_i[0] % 3
                    cvt_i[0] += 1
                    kb.dma(stg[i][:, 0:n1 - n0], src[kc * 128:(kc + 1) * 128, n0:n1], writes=[("stg", i)])
                    eng = ["dve", "pool", "act"][i]
                    if eng == "act":
                        kb.act(dst[:, kc, n0:n1], stg[i][:, 0:n1 - n0], AF.Copy, reads=[("stg", i)], writes=[name])
                    else:
                        kb.v(eng, "tensor_copy", dst[:, kc, n0:n1], stg[i][:, 0:n1 - n0], reads=[("stg", i)], writes=[name])
        load_w(wo, wo_d, D, D, "wo")
        load_w(wg, wg_d, D, DFF, "wg")
        load_w(wu, wu_d, D, DFF, "wu")
        load_w(wd, wd_d, DFF, D, "wd")

        def rmsnorm_rows(src, j, col, wtile, dst, dst_key):
            kb.act(hb[:, j, :], src, AF.Square, accum_out=ss[:, col:col + 1], reads=["xt"], writes=["hb", ("ss", col)])
            kb.v("dve", "tensor_scalar", rs[:, col:col + 1], ss[:, col:col + 1], 1.0 / D, EPS, ALU.mult, ALU.add,
                 reads=[("ss", col)], writes=[("rs", col)])
            kb.act(rs[:, col:col + 1], rs[:, col:col + 1], AF.Sqrt, reads=[("rs", col)], writes=[("rs", col)])
            kb.v("dve", "reciprocal", rs[:, col:col + 1], rs[:, col:col + 1], reads=[("rs", col)], writes=[("rs", col)])
            kb.v("dve", "scalar_tensor_tensor", dst, src, rs[:, col:col + 1], wtile[:], ALU.mult, ALU.mult,
                 reads=["xt", ("rs", col), "nfw", "nfin"], writes=[dst_key])

        omv = om_d.rearrange("(kc p) t -> p kc t", p=128)
        for blk in range(TOK // TB):
            t0 = blk * TB
            kb.dma(xt[:], x_d[t0:t0 + TB, :].rearrange("(j p) d -> p j d", p=128), writes=["xt"])
            kb.dma(om[:], omv[:, :, t0:t0 + TB], writes=["om"])
            if final:
                kb.v("pool", "tensor_tensor", sq[:], om[:, 0:4, :], om[:, 0:4, :], ALU.mult, reads=["om"], writes=["sq"])
                for G in range(2):
                    p = pmm[G]
                    for i in range(2):
                        kb.mm(p[:, 0:TB], ones[:], sq[:, 2 * G + i, :], start=(i == 0), stop=(i == 1),
                              reads=["ones", "sq"], writes=[("pmm", G)])
                    kb.v("dve", "tensor_scalar", rg[:], p[:, 0:TB], 1.0 / 256, EPS, ALU.mult, ALU.add,
                         reads=[("pmm", G)], writes=["rg"])
                    kb.act(rg[:], rg[:], AF.Sqrt, reads=["rg"], writes=["rg"])
                    kb.v("dve", "reciprocal", rg[:], rg[:], reads=["rg"], writes=["rg"])
                    for i in range(2):
                        kc = 2 * G + i
                        kb.v("dve", "scalar_tensor_tensor", om[:, kc, :], om[:, kc, :], gnw[:, kc:kc + 1], rg[:],
                             ALU.mult, ALU.mult, reads=["om", "rg", "gnw"], writes=["om"])
            for j in range(2):
                for hf in range(2):
                    p = pmm[(2 * j + hf) % 2]
                    key = ("pmm", (2 * j + hf) % 2)
                    for kc in range(8):
                        kb.mm(p[:], om[:, kc, j * 128:(j + 1) * 128], wo[:, kc, hf * 512:(hf + 1) * 512],
                              start=(kc == 0), stop=(kc == 7), reads=["om", "wo"], writes=[key])
                    kb.v("dve", "tensor_tensor", xt[:, j, hf * 512:(hf + 1) * 512], xt[:, j, hf * 512:(hf + 1) * 512], p[:],
                         ALU.add, reads=[key, "xt"], writes=["xt"])
            for j in range(2):
                rmsnorm_rows(xt[:, j, :], j, j, nfw, hb[:, j, :], "hb")
                for q4 in range(2):
                    pt = ptr[q4]
                    for i in range(4):
                        kc = q4 * 4 + i
                        kb.tr(pt[:, i, :], hb[:, j, kc * 128:(kc + 1) * 128], idb[:], reads=["hb", "idb"], writes=[("ptr", q4)])
                    if q4 == 0:
                        kb.v("dve", "tensor_copy", hT[:, 0:4, j * 128:(j + 1) * 128], pt[:], reads=[("ptr", q4)], writes=["hT"])
                    else:
                        kb.act(hT[:, 4:8, j * 128:(j + 1) * 128], pt[:], AF.Copy, reads=[("ptr", q4)], writes=["hT"])
            for f in range(NF):
                b = f % 2
                for kc in range(8):
                    kb.mm(pg[b][:, 0:TB], wg[:, kc, f * 128:(f + 1) * 128], hT[:, kc, :], start=(kc == 0), stop=(kc == 7),
                          reads=["wg", "hT"], writes=[("pg", b)])
                for kc in range(8):
                    kb.mm(pu[b][:, 0:TB], wu[:, kc, f * 128:(f + 1) * 128], hT[:, kc, :], start=(kc == 0), stop=(kc == 7),
                          reads=["wu", "hT"], writes=[("pu", b)])
                kb.act(sg[b][:], pg[b][:, 0:TB], AF.Silu, reads=[("pg", b)], writes=[("sg", b)])
                kb.v("dve", "tensor_tensor", actT[:, f, :], sg[b][:], pu[b][:, 0:TB], ALU.mult,
                     reads=[("sg", b), ("pu", b)], writes=["actT"])
            for j in range(2):
                for hf in range(2):
                    p = pmm[(2 * j + hf) % 2]
                    key = ("pmm", (2 * j + hf) % 2)
                    for f in range(NF):
                        kb.mm(p[:], actT[:, f, j * 128:(j + 1) * 128], wd[:, f, hf * 512:(hf + 1) * 512],
                              start=(f == 0), stop=(f == NF - 1), reads=["actT", "wd"], writes=[key])
                    kb.v("dve", "tensor_tensor", xt[:, j, hf * 512:(hf + 1) * 512], xt[:, j, hf * 512:(hf + 1) * 512], p[:],
                         ALU.add, reads=[key, "xt"], writes=["xt"])
            if final:
                for j in range(2):
                    rmsnorm_rows(xt[:, j, :], j, 2 + j, nfin, xt[:, j, :], "xt")
            kb.dma(y_d[t0:t0 + TB, :].rearrange("(j p) d -> p j d", p=128), xt[:], reads=["xt"])
        kb.emit()
    return nc


NT = T // 128


def build_band():
    m = Mix(384)
    kb, nc = m.kb, m.nc
    o_d = m.dram("oT", [128, T], BF16, "ExternalOutput")
    bias_d = m.dram("biasT", [128, 10, 128])
    biasf = m.sb("biasf", [128, 10, 128])
    biasb = m.sb("biasb", [128, 10, 128], BF16)
    kb.dma(biasf[:], bias_d, writes=["biasf"])
    kb.v("dve", "tensor_copy", biasb[:], biasf[:], reads=["biasf"], writes=["biasb"])
    qT = m.sb("qT", [128, T], BF16)
    kT = m.sb("kT", [128, T], BF16)
    v = m.sb("v", [128, NT, 2, 65], BF16)
    pT = [m.sb("pT%d" % i, [128, 128], BF16) for i in range(3)]
    osb = m.sb("osb", [128, 4, 128])
    rden = m.sb("rden", [128, 2])
    oT = m.sb("oT_s", [128, 512], BF16)
    pfm = [m.ps("pfm%d" % i, [128, 512]) for i in range(2)]
    ptm = m.ps("ptm", [128, 512])
    pO = [m.ps("pO%d" % i, [128, 2, 128]) for i in range(2)]
    ptf = m.ps("ptf", [128, 4, 128])
    kb.v("dve", "memset", v[:].rearrange("p a b c -> p (a b c)"), 1.0, writes=["v"])
    for blk in range(NBLK):
        m.frontend(blk)
        t0 = blk * TBM
        m.proj_fm(pfm[0][:], ("pfm", 0), 0)
        kb.v("dve", "tensor_scalar", qT[:, t0:t0 + TBM], pfm[0][:], 0.125, None, ALU.mult, reads=[("pfm", 0)], writes=["qT"])
        m.proj_fm(pfm[1][:], ("pfm", 1), 128)
        kb.v("dve", "tensor_copy", kT[:, t0:t0 + TBM], pfm[1][:], reads=[("pfm", 1)], writes=["kT"])
        for j in range(4):
            tile = blk * 4 + j
            m.proj_tm(ptm[:, 0:128], "ptm", j, 256, 128)
            kb.v("dve", "tensor_copy", v[:, tile, :, 0:64], ptm[:, 0:128].rearrange("p (h d) -> p h d", h=2),
                 reads=["ptm"], writes=["v"])
        items = []
        for j in range(4):
            qt = blk * 4 + j
            for h in range(2):
                deltas = [dl for dl in range(5) if qt - dl >= 0]
                for di, dl in enumerate(deltas):
                    items.append((j, qt, h, di, dl, len(deltas), len(items)))

        def emit_scores(itm):
            j, qt, h, di, dl, nd, it = itm
            hs = slice(h * 64, (h + 1) * 64)
            kt = qt - dl
            ps_ = pfm[it % 2]; psk = ("pfm", it % 2)
            kb.mm(ps_[:, 0:128], kT[hs, kt * 128:(kt + 1) * 128], qT[hs, qt * 128:(qt + 1) * 128], start=True, stop=False,
                  reads=["kT", "qT"], writes=[psk])
            kb.mm(ps_[:, 0:128], m.idb[:], biasb[:, h * 5 + dl, :], start=False, stop=True, reads=["idb", "biasb"], writes=[psk])

        def emit_rest(itm):
            j, qt, h, di, dl, nd, it = itm
            hs = slice(h * 64, (h + 1) * 64)
            kt = qt - dl
            ps_ = pfm[it % 2]; psk = ("pfm", it % 2)
            pt = pT[it % 3]; ptk = ("pT", it % 3)
            po = pO[qt % 2]; pok = ("pO", qt % 2)
            kb.act(pt[:], ps_[:, 0:128], AF.Exp, reads=[psk], writes=[ptk])
            kb.mm(po[:, h, 0:65], pt[:], v[:, kt, h, :], start=(di == 0), stop=(di == nd - 1), reads=[ptk, "v"], writes=[pok])
            if di == nd - 1:
                kb.v("dve", "reciprocal", rden[:, h:h + 1], po[:, h, 64:65], reads=[pok], writes=["rden"])
                kb.v("dve", "tensor_scalar", osb[:, j, hs], po[:, h, 0:64], rden[:, h:h + 1], None, ALU.mult,
                     reads=[pok, "rden"], writes=["osb"])
                if h == 1:
                    kb.op("pe", lambda e, j=j: e.transpose(ptf[:, j, :], osb[:, j, :], m.idf[:]), reads=["osb", "idf"], writes=["ptf"])

        emit_scores(items[0])
        for i, itm in enumerate(items):
            if i + 1 < len(items):
                emit_scores(items[i + 1])
            emit_rest(itm)
        kb.v("dve", "tensor_copy", oT[:], ptf[:].rearrange("p s q -> p (s q)"), reads=["ptf"], writes=["oT"])
        kb.dma(o_d[:, t0:t0 + TBM], oT[:], reads=["oT"])
    return m.finish()


def band_bias(rb2):
    kk = np.arange(128)[:, None]; qq = np.arange(128)[None, :]
    out = np.empty((128, 10, 128), np.float32)
    for h in range(2):
        for dl in range(5):
            dist = 128 * dl + qq - kk
            idx = np.clip(dist, -256, 256) + 256
            cd = 2 * dl + qq // 64 - kk // 64
            valid = (cd >= 0) & (cd <= 8)
            out[:, h * 5 + dl, :] = np.where(valid, rb2[h][idx], np.float32(-30000.0))
    return out


NT = T // 128


def build_fox():
    m = Mix(386)
    kb, nc = m.kb, m.nc
    fb_d = m.dram("f_bias", [128, 2])
    o_d = m.dram("oT", [128, T], BF16, "ExternalOutput")
    trif = m.const("trif", [128, 128])
    umat = m.const("umat", [128, 128])
    sel = m.const("sel127", [128, 128])
    onesf = m.sb("onesf", [128, 128])
    kb.v("dve", "memset", onesf[:], 1.0, writes=["onesf"])
    trib = m.sb("trib", [128, 128], BF16)
    kb.v("dve", "tensor_copy", trib[:], trif[:], reads=["trif"], writes=["trib"])
    nfb = m.sb("nfb", [128, 2])
    kb.dma(nfb[:], fb_d[:, :], writes=["nfb"])
    kb.v("dve", "tensor_scalar", nfb[:], nfb[:], -1.0, None, ALU.mult, reads=["nfb"], writes=["nfb"])

    qT = m.sb("qT", [128, T], BF16)
    kT = m.sb("kT", [128, T], BF16)
    v = m.sb("v", [128, NT, 2, 128], BF16)
    lf = m.sb("lf", [128, 2, NT])
    F = m.sb("F", [128, 2, NT])
    totT = m.sb("totT", [128, 128])
    flast = m.sb("flast", [128, 2, NT])
    nbias = m.sb("nbias", [128, NT])
    dd = m.sb("dd", [128, 2, NT]); dhi = m.sb("dhi", [128, 2, NT], BF16); dhf = m.sb("dhf", [128, 2, NT]); dlo = m.sb("dlo", [128, 2, NT], BF16)
    frow = m.sb("frow", [128, 2, T], BF16)
    ones2 = m.sb("ones2", [128, 128], BF16); onesr = m.sb("onesr", [128, 128], BF16)
    pT = [m.sb("pT%d" % i, [128, 512], BF16) for i in range(3)]
    osb = m.sb("osb", [128, 512])
    rden = m.sb("rden", [64, 512])
    oT = [m.sb("oT_s%d" % i, [64, 512], BF16) for i in range(2)]
    sel65 = m.sb("sel65", [128, 64])
    kb.v("dve", "memset", sel65[:], 0.0, writes=["sel65"])
    kb.v("dve", "memset", sel65[64:65, :], 1.0, writes=["sel65"])
    pfm = [m.ps("pfm%d" % i, [128, 512]) for i in range(2)]
    ptm = m.ps("ptm", [128, 512])
    pO = [m.ps("pO%d" % i, [128, 512]) for i in range(2)]
    ptf = m.ps("ptf", [128, 512])

    kb.v("dve", "memset", v[:].rearrange("p a b c -> p (a b c)"), 0.0, writes=["v"])
    kb.v("dve", "memset", v[:, :, :, 64:65], 1.0, writes=["v"])
    kb.v("dve", "memset", lf[:].rearrange("p h t -> p (h t)"), 0.0, writes=["lf"])
    for blk in range(NBLK):
        m.frontend(blk)
        t0 = blk * TBM
        m.proj_fm(pfm[0][:], ("pfm", 0), 0)
        kb.v("dve", "tensor_scalar", qT[:, t0:t0 + TBM], pfm[0][:], 0.125, None, ALU.mult, reads=[("pfm", 0)], writes=["qT"])
        m.proj_fm(pfm[1][:], ("pfm", 1), 128)
        kb.v("dve", "tensor_copy", kT[:, t0:t0 + TBM], pfm[1][:], reads=[("pfm", 1)], writes=["kT"])
        for j in range(4):
            tile = blk * 4 + j
            m.proj_tm(ptm[:, 0:130], "ptm", j, 256, 130)
            kb.v("dve", "tensor_copy", v[:, tile, :, 0:64], ptm[:, 0:128].rearrange("p (h d) -> p h d", h=2),
                 reads=["ptm"], writes=["v"])
            kb.v("dve", "tensor_copy", lf[:, :, tile], ptm[:, 128:130], reads=["ptm"], writes=["lf"])
    for h in range(2):
        kb.act(lf[:, h, :], lf[:, h, :], AF.Exp, scale=-1.0, bias=nfb[:, h:h + 1], reads=["lf", "nfb"], writes=["lf"])
    kb.act(lf[:], lf[:], AF.Ln, bias=1.0, reads=["lf"], writes=["lf"])
    kb.v("dve", "tensor_scalar", lf[:], lf[:], -1.0, None, ALU.mult, reads=["lf"], writes=["lf"])
    lf2 = lf[:].rearrange("p h t -> p (h t)")
    kb.mm(ptm[:, 0:128], lf2, onesf[:], reads=["lf", "onesf"], writes=["ptm"])
    kb.v("dve", "tensor_copy", totT[:], ptm[:, 0:128], reads=["ptm"], writes=["totT"])
    kb.mm(ptm[:, 0:128], trif[:], lf2, start=True, stop=False, reads=["trif", "lf"], writes=["ptm"])
    kb.mm(ptm[:, 0:128], totT[:], umat[:], start=False, stop=True, reads=["totT", "umat"], writes=["ptm"])
    kb.v("dve", "tensor_copy", F[:].rearrange("p h t -> p (h t)"), ptm[:, 0:128], reads=["ptm"], writes=["F"])
    kb.mm(ptm[:, 0:128], sel[:], F[:].rearrange("p h t -> p (h t)"), reads=["sel127", "F"], writes=["ptm"])
    kb.v("dve", "tensor_copy", flast[:].rearrange("p h t -> p (h t)"), ptm[:, 0:128], reads=["ptm"], writes=["flast"])

    fl4 = flast[:].rearrange("p h (b s) -> p h b s", s=4)
    dd4 = dd[:].rearrange("p h (b s) -> p h b s", s=4)
    for s_ in range(4):
        kb.v("dve", "tensor_tensor", dd4[:, :, :, s_], fl4[:, :, :, s_], fl4[:, :, :, 3], ALU.subtract, reads=["flast"], writes=["dd"])
    kb.v("dve", "tensor_copy", dhi[:], dd[:], reads=["dd"], writes=["dhi"])
    kb.v("dve", "tensor_copy", dhf[:], dhi[:], reads=["dhi"], writes=["dhf"])
    kb.v("dve", "tensor_tensor", dd[:], dd[:], dhf[:], ALU.subtract, reads=["dd", "dhf"], writes=["dd"])
    kb.v("dve", "tensor_copy", dlo[:], dd[:], reads=["dd"], writes=["dlo"])
    kb.v("dve", "memset", frow[:].rearrange("p h t -> p (h t)"), 0.0, writes=["frow"])
    kb.v("dve", "memset", ones2[:], 0.0, writes=["ones2"])
    kb.v("dve", "memset", ones2[0:1, :], 1.0, writes=["ones2"])
    kb.v("dve", "memset", ones2[32:33, :], 1.0, writes=["ones2"])
    kb.v("dve", "memset", onesr[:], 1.0, writes=["onesr"])
    for h in range(2):
        for t_ in range(NT):
            kb.v("dve", "tensor_scalar", frow[0:1, h, t_ * 128:(t_ + 1) * 128], onesr[0:1, :], dhi[0:1, h, t_:t_ + 1], None, ALU.mult,
                 reads=["onesr", "dhi"], writes=["frow"])
            kb.v("pool", "tensor_scalar", frow[32:33, h, t_ * 128:(t_ + 1) * 128], onesr[32:33, :], dlo[32:33, h, t_:t_ + 1], None, ALU.mult,
                 reads=["onesr", "dlo"], writes=["frow"])
    pairs = []
    for qb in range(NBLK):
        for h in range(2):
            nkt = 4 * qb + 4
            for kt in range(nkt):
                pairs.append((qb, h, kt, nkt, len(pairs)))

    def emit_scores(p):
        qb, h, kt, nkt, it = p
        hs = slice(h * 64, (h + 1) * 64)
        q0 = qb * TBM
        j = max(0, kt - 4 * qb)
        n = TBM - 128 * j
        ps_ = pfm[it % 2]; psk = ("pfm", it % 2)
        kb.mm(ps_[:, 0:n], kT[hs, kt * 128:(kt + 1) * 128], qT[hs, q0 + 128 * j:q0 + TBM], start=True, stop=False,
              reads=["kT", "qT"], writes=[psk])
        kb.mm(ps_[:, 0:n], ones2[:], frow[:, h, q0 + 128 * j:q0 + TBM], start=False, stop=True,
              reads=["ones2", "frow"], writes=[psk])

    def emit_rest(p):
        qb, h, kt, nkt, it = p
        q0 = qb * TBM
        j = max(0, kt - 4 * qb)
        n = TBM - 128 * j
        ps_ = pfm[it % 2]; psk = ("pfm", it % 2)
        pt = pT[it % 3]; ptk = ("pT", it % 3)
        po = pO[(2 * qb + h) % 2]
        pok = ("pO", (2 * qb + h) % 2)
        if kt == 0:
            kb.v("dve", "tensor_scalar", nbias[:, 0:nkt], F[:, h, 0:nkt], flast[:, h, 4 * qb + 3:4 * qb + 4], -1.0,
                 ALU.subtract, ALU.mult, reads=["F", "flast"], writes=["nbias"])
        kb.act(pt[:, 0:n], ps_[:, 0:n], AF.Exp, bias=nbias[:, kt:kt + 1], reads=[psk, "nbias"], writes=[ptk])
        if kt >= 4 * qb:
            kb.v("pool", "tensor_tensor", pt[:, 0:128], pt[:, 0:128], trib[:], ALU.mult, reads=[ptk, "trib"], writes=[ptk])
        kb.mm(po[:, 128 * j:TBM], v[:, kt, h, :], pt[:, 0:n], start=(kt == 0), stop=(kt == nkt - 1),
              reads=[ptk, "v"], writes=[pok])
        if kt == nkt - 1:
            kb.act(osb[:], po[:, :], AF.Copy, reads=[pok], writes=["osb"])
            kb.mm(ptf[0:64, :], sel65[:], osb[:], reads=["sel65", "osb"], writes=["ptf"])
            kb.v("dve", "reciprocal", rden[:], ptf[0:64, :], reads=["ptf"], writes=["rden"])
            kb.v("dve", "tensor_tensor", oT[h][:], osb[0:64, :], rden[:], ALU.mult, reads=["osb", "rden"], writes=[("oT", h)])
            kb.dma(o_d[h * 64:(h + 1) * 64, q0:q0 + TBM], oT[h][:], reads=[("oT", h)])

    if pairs:
        emit_scores(pairs[0])
    for i, p in enumerate(pairs):
        if i + 1 < len(pairs):
            emit_scores(pairs[i + 1])
        emit_rest(p)
    return m.finish()


def fox_consts():
    j = np.arange(128)
    trif = (j[:, None] <= j[None, :]).astype(np.float32)
    hh = j // 64; tt = j % 64
    umat = ((hh[:, None] == hh[None, :]) & (tt[:, None] < tt[None, :])).astype(np.float32)
    sel = np.zeros((128, 128), np.float32); sel[127, :] = 1.0
    return {"trif": trif, "umat": umat, "sel127": sel, "ident": np.eye(128, dtype=np.float32)}


def build_ssd():
    m = Mix(514)
    kb, nc = m.kb, m.nc
    o_d = m.dram("oT", [128, T], BF16, "ExternalOutput")
    cw = m.const("convw", [128, 12])
    cb = m.const("convb", [128, 3])
    dtb = m.const("dtb", [128, 2])
    alog = m.const("alog", [128, 2])
    dsk = m.const("dskip", [128, 2])
    triblk = m.const("triblk", [128, 128])
    blkones = m.const("blkones", [128, 128])
    onesf = m.sb("onesf", [128, 128])
    kb.v("dve", "memset", onesf[:], 1.0, writes=["onesf"])
    na = m.sb("na", [128, 2])
    kb.act(na[:], alog[:], AF.Exp, reads=["alog"], writes=["na"])
    kb.v("dve", "tensor_scalar", na[:], na[:], -1.0, None, ALU.mult, reads=["na"], writes=["na"])
    raw = m.sb("raw", [128, 3, 515])
    acc = m.sb("acc", [128, 512])
    xbc = m.sb("xbc", [128, 3, 512])
    zs = m.sb("zs", [128, 4, 128])
    dt = m.sb("dt", [128, 4, 2])
    da = m.sb("da", [128, 2])
    dar = m.sb("dar", [128, 128])
    dc = m.sb("dc", [128, 2])
    tmp = m.sb("tmp", [128, 128])
    LT = m.sb("LT", [128, 128])
    wT = m.sb("wT", [128, 128])
    xtok = m.sb("xtok", [128, 128]); btok = m.sb("btok", [128, 128])
    bw = m.sb("bw", [128, 2]); Bw = [m.sb("Bw%d" % h, [128, 128]) for h in range(2)]
    edacs = m.sb("edacs", [128, 2])
    cdec = [m.sb("cdec%d" % h, [128, 128]) for h in range(2)]
    stateT = m.sb("stateT", [128, 128])
    ysb = m.sb("ysb", [128, 128])
    oT = m.sb("oT_s", [128, 512], BF16)
    pfm = [m.ps("pfm%d" % i, [128, 512]) for i in range(2)]
    ptm = pfm[1]
    pg = [m.ps("pg%d" % i, [128, 512]) for i in range(4)]
    kb.v("dve", "memset", raw[:].rearrange("p a b -> p (a b)"), 0.0, writes=["raw"])
    kb.v("dve", "memset", stateT[:], 0.0, writes=["stateT"])
    for blk in range(NBLK):
        m.frontend(blk)
        t0 = blk * TBM
        for i in range(3):
            p = pfm[i % 2]; pk = ("pfm", i % 2)
            m.proj_fm(p[:], pk, i * 128)
            kb.act(raw[:, i, 3:515], p[:], AF.Copy, reads=[pk], writes=["raw"])
            kb.v("dve", "tensor_scalar", acc[:], raw[:, i, 0:512], cw[:, 4 * i:4 * i + 1], cb[:, i:i + 1], ALU.mult, ALU.add,
                 reads=["raw", "convw", "convb"], writes=["acc"])
            for k in range(1, 4):
                kb.v("dve", "scalar_tensor_tensor", acc[:], raw[:, i, k:k + 512], cw[:, 4 * i + k:4 * i + k + 1], acc[:],
                     ALU.mult, ALU.add, reads=["raw", "convw", "acc"], writes=["acc"])
            kb.act(xbc[:, i, :], acc[:], AF.Exp, scale=-1.0, reads=["acc"], writes=["xbc"])
            kb.v("dve", "tensor_scalar", xbc[:, i, :], xbc[:, i, :], 1.0, None, ALU.add, reads=["xbc"], writes=["xbc"])
            kb.v("dve", "reciprocal", xbc[:, i, :], xbc[:, i, :], reads=["xbc"], writes=["xbc"])
            kb.v("dve", "tensor_tensor", xbc[:, i, :], xbc[:, i, :], acc[:], ALU.mult, reads=["xbc", "acc"], writes=["xbc"])
            kb.v("dve", "tensor_copy", raw[:, i, 0:3], raw[:, i, 512:515], reads=["raw"], writes=["raw"])
        for j in range(4):
            m.proj_tm(ptm[:, 0:130], ("pfm", 1), j, 384, 130)
            kb.act(zs[:, j, :], ptm[:, 0:128], AF.Exp, scale=-1.0, reads=[("pfm", 1)], writes=["zs"])
            kb.v("dve", "tensor_scalar", zs[:, j, :], zs[:, j, :], 1.0, None, ALU.add, reads=["zs"], writes=["zs"])
            kb.v("dve", "reciprocal", zs[:, j, :], zs[:, j, :], reads=["zs"], writes=["zs"])
            kb.v("dve", "tensor_tensor", zs[:, j, :], zs[:, j, :], ptm[:, 0:128], ALU.mult, reads=["zs", ("pfm", 1)], writes=["zs"])
            kb.v("dve", "tensor_tensor", dt[:, j, :], ptm[:, 128:130], dtb[:], ALU.add, reads=[("pfm", 1), "dtb"], writes=["dt"])
        kb.act(dt[:], dt[:], AF.Exp, reads=["dt"], writes=["dt"])
        kb.act(dt[:], dt[:], AF.Ln, bias=1.0, reads=["dt"], writes=["dt"])
        for j in range(4):
            sl = slice(j * 128, (j + 1) * 128)
            kb.op("pe", lambda e, sl=sl: e.transpose(pg[0][:, 0:128], xbc[:, 0, sl], m.idf[:]), reads=["xbc", "idf"], writes=[("pg", 0)])
            kb.op("pe", lambda e, sl=sl: e.transpose(pg[0][:, 128:256], xbc[:, 1, sl], m.idf[:]), reads=["xbc", "idf"], writes=[("pg", 0)])
            kb.v("dve", "tensor_copy", xtok[:], pg[0][:, 0:128], reads=[("pg", 0)], writes=["xtok"])
            kb.act(btok[:], pg[0][:, 128:256], AF.Copy, reads=[("pg", 0)], writes=["btok"])
            kb.v("dve", "tensor_tensor", da[:], dt[:, j, :], na[:], ALU.mult, reads=["dt", "na"], writes=["da"])
            kb.mm(pg[2][:, 0:128], xbc[:, 1, sl], xbc[:, 2, sl], reads=["xbc"], writes=[("pg", 2)])
            for h in range(2):
                hh = slice(h * 64, (h + 1) * 64)
                kb.v("dve", "tensor_scalar", dar[:], onesf[:], da[:, h:h + 1], None, ALU.mult, reads=["onesf", "da"], writes=["dar"])
                kb.mm(pg[1][:, 0:128], dar[:], triblk[:], reads=["dar", "triblk"], writes=[("pg", 1)])
                kb.mm(pg[1][:, 128:256], dar[:], blkones[:], reads=["dar", "blkones"], writes=[("pg", 1)])
                kb.mm(pg[1][:, 256:384], triblk[:], dar[:], reads=["dar", "triblk"], writes=[("pg", 1)])
                kb.mm(pg[1][:, 384:512], blkones[:], dar[:], reads=["dar", "blkones"], writes=[("pg", 1)])
                kb.v("dve", "tensor_copy", dc[:, 0:1], pg[1][:, 256:257], reads=[("pg", 1)], writes=["dc"])
                kb.v("dve", "tensor_copy", dc[:, 1:2], pg[1][:, 384:385], reads=[("pg", 1)], writes=["dc"])
                kb.v("dve", "tensor_scalar", tmp[:], pg[1][:, 0:128], dc[:, 0:1], 0.0, ALU.subtract, ALU.min,
                     reads=[("pg", 1), "dc"], writes=["tmp"])
                kb.act(LT[:], tmp[:], AF.Exp, reads=["tmp"], writes=["LT"])
                kb.v("dve", "scalar_tensor_tensor", LT[:], LT[:], dt[:, j, h:h + 1], triblk[:], ALU.mult, ALU.mult,
                     reads=["LT", "dt", "triblk"], writes=["LT"])
                kb.v("dve", "tensor_tensor", wT[:], LT[:], pg[2][:, 0:128], ALU.mult, reads=["LT", ("pg", 2)], writes=["wT"])
                kb.mm(pg[2][:, 128 + h * 64:128 + (h + 1) * 64], wT[:], xtok[:, hh], reads=["wT", "xtok"], writes=[("pg", 2)])
                kb.v("dve", "tensor_tensor", bw[:, h:h + 1], dc[:, 1:2], dc[:, 0:1], ALU.subtract, reads=["dc"], writes=["bw"])
                kb.act(bw[:, h:h + 1], bw[:, h:h + 1], AF.Exp, reads=["bw"], writes=["bw"])
                kb.v("dve", "tensor_tensor", bw[:, h:h + 1], bw[:, h:h + 1], dt[:, j, h:h + 1], ALU.mult, reads=["bw", "dt"], writes=["bw"])
                kb.v("dve", "tensor_scalar", Bw[h][:], btok[:], bw[:, h:h + 1], None, ALU.mult, reads=["btok", "bw"], writes=[("Bw", h)])
                kb.act(edacs[:, h:h + 1], dc[:, 0:1], AF.Exp, reads=["dc"], writes=["edacs"])
                kb.act(cdec[h][:], pg[1][:, 128:256], AF.Exp, reads=[("pg", 1)], writes=[("cdec", h)])
            kb.act(ysb[:], pg[2][:, 128:256], AF.Copy, reads=[("pg", 2)], writes=["ysb"])
            for c in range(2):
                cs = slice(c * 64, (c + 1) * 64)
                kb.mm(pg[3][:, 0:128], xbc[:, 2, sl], stateT[:], reads=["xbc", "stateT"], writes=[("pg", 3)])
                for h in range(2):
                    hh = slice(h * 64, (h + 1) * 64)
                    kb.v("dve", "scalar_tensor_tensor", ysb[cs, hh], pg[3][cs, hh], edacs[cs, h:h + 1], ysb[cs, hh], ALU.mult, ALU.add,
                         reads=[("pg", 3), "edacs", "ysb"], writes=["ysb"])
                for h in range(2):
                    hh = slice(h * 64, (h + 1) * 64)
                    kb.mm(pg[3][:, 128:256], Bw[h][cs, :], xtok[cs, :], reads=[("Bw", h), "xtok"], writes=[("pg", 3)])
                    kb.v("dve", "scalar_tensor_tensor", stateT[:, hh], stateT[:, hh], cdec[h][:, c * 64:c * 64 + 1], pg[3][:, 128 + h * 64:128 + (h + 1) * 64],
                         ALU.mult, ALU.add, reads=["stateT", ("cdec", h), ("pg", 3)], writes=["stateT"])
            for h in range(2):
                hh = slice(h * 64, (h + 1) * 64)
                kb.v("dve", "scalar_tensor_tensor", ysb[:, hh], xtok[:, hh], dsk[:, h:h + 1], ysb[:, hh], ALU.mult, ALU.add,
                     reads=["xtok", "dskip", "ysb"], writes=["ysb"])
            kb.v("dve", "tensor_tensor", ysb[:], ysb[:], zs[:, j, :], ALU.mult, reads=["ysb", "zs"], writes=["ysb"])
            kb.op("pe", lambda e: e.transpose(pg[0][:, 256:384], ysb[:], m.idf[:]), reads=["ysb", "idf"], writes=[("pg", 0)])
            kb.v("dve", "tensor_copy", oT[:, sl], pg[0][:, 256:384], reads=[("pg", 0)], writes=["oT"])
        kb.dma(o_d[:, t0:t0 + TBM], oT[:], reads=["oT"])
    return m.finish()


def chunk_consts_ssd_unused():
    j = np.arange(128)
    same = (j[:, None] // 64) == (j[None, :] // 64)
    return {"triblk": (same & (j[:, None] <= j[None, :])).astype(np.float32), "blkones": same.astype(np.float32)}


def build_gdn():
    m = Mix(514)
    kb, nc = m.kb, m.nc
    o_d = m.dram("o_tok", [T, 128], BF16, "ExternalOutput")
    cw = m.const("convw", [128, 12])
    dtb = m.const("dtb", [128, 1])
    alog = m.const("alog", [128, 1])
    nrmw = m.const("nrmw", [128, 128])
    triblk = m.const("triblk", [128, 128])
    tristr = m.const("tristrict", [128, 128])
    blkones = m.const("blkones", [128, 128])
    idf = m.idf
    onesf = m.sb("onesf", [128, 128])
    kb.v("dve", "memset", onesf[:], 1.0, writes=["onesf"])
    na = m.sb("na", [128, 1])
    kb.act(na[:], alog[:], AF.Exp, reads=["alog"], writes=["na"])
    kb.v("dve", "tensor_scalar", na[:], na[:], -1.0, None, ALU.mult, reads=["na"], writes=["na"])
    T_ = lambda name, shape=[128, 128]: m.sb(name, shape)
    raw = T_("raw", [128, 3, 515]); acc = T_("acc", [128, 512]); qkv = T_("qkv", [128, 3, 512]); sq = T_("sq", [128, 512])
    zs = T_("zs", [128, 4, 128]); beta = T_("beta", [128, 4]); gg = T_("gg", [128, 4])
    ktok = T_("ktok"); vtok = T_("vtok"); gr = T_("gr"); br = T_("br"); dc = T_("dc", [128, 2]); tmp = T_("tmp")
    ET = T_("ET"); egc = T_("egc", [128, 1]); egcbc = T_("egcbc"); eglbc = T_("eglbc"); ksc = T_("ksc", [128, 1]); bcol = T_("bcol", [128, 1])
    NTt = T_("NTt"); AT = T_("AT"); A = T_("A")
    Pa = [T_("Pa0"), T_("Pa1")]; PTa = [T_("PTa0"), T_("PTa1")]; RT = T_("RT")
    vb = T_("vb"); kbt = T_("kbt"); u = T_("u"); wT = T_("wT"); attnT = T_("attnT"); qdT = T_("qdT"); kst = T_("kst")
    vnew = T_("vnew"); S = T_("S"); osb = T_("osb"); junk = T_("junk"); ssq = T_("ssq", [128, 1]); rstd = T_("rstd", [128, 1])
    otk = m.sb("otk", [128, 4, 128], BF16)
    pfm = [m.ps("pfm%d" % i, [128, 512]) for i in range(2)]
    pg = [m.ps("pg%d" % i, [128, 512]) for i in range(4)]
    ptm = pfm[1]

    def R(b, r):
        return pg[b][:, r * 128:(r + 1) * 128], ("pg", b)

    for t_, k_ in ((raw[:].rearrange("p a b -> p (a b)"), "raw"), (S[:], "S"), (vnew[:], "vnew")):
        kb.v("dve", "memset", t_, 0.0, writes=[k_])

    def silu_from(out, out_key, src, src_key):
        kb.act(out, src, AF.Exp, scale=-1.0, reads=[src_key], writes=[out_key])
        kb.v("dve", "tensor_scalar", out, out, 1.0, None, ALU.add, reads=[out_key], writes=[out_key])
        kb.v("dve", "reciprocal", out, out, reads=[out_key], writes=[out_key])
        kb.v("dve", "tensor_tensor", out, out, src, ALU.mult, reads=[out_key, src_key], writes=[out_key])

    def mm1(dst, lhsT, rhs, reads):
        ap, key = dst
        kb.mm(ap, lhsT, rhs, reads=reads, writes=[key])

    def trp(dst, src, reads):
        ap, key = dst
        kb.op("pe", lambda e: e.transpose(ap, src, idf[:]), reads=reads + ["idf"], writes=[key])
    for blk in range(NBLK):
        m.frontend(blk)
        t0 = blk * TBM
        for i in range(3):
            p = pfm[i % 2]; pk = ("pfm", i % 2)
            m.proj_fm(p[:], pk, i * 128)
            kb.act(raw[:, i, 3:515], p[:], AF.Copy, reads=[pk], writes=["raw"])
            kb.v("dve", "tensor_scalar", acc[:], raw[:, i, 0:512], cw[:, 4 * i:4 * i + 1], None, ALU.mult,
                 reads=["raw", "convw"], writes=["acc"])
            for k in range(1, 4):
                kb.v("dve", "scalar_tensor_tensor", acc[:], raw[:, i, k:k + 512], cw[:, 4 * i + k:4 * i + k + 1], acc[:],
                     ALU.mult, ALU.add, reads=["raw", "convw", "acc"], writes=["acc"])
            silu_from(qkv[:, i, :], ("qkv", i), acc[:], "acc")
            kb.v("dve", "tensor_copy", raw[:, i, 0:3], raw[:, i, 512:515], reads=["raw"], writes=["raw"])
        for i in range(2):
            kb.v("dve", "tensor_tensor", sq[:], qkv[:, i, :], qkv[:, i, :], ALU.mult, reads=[("qkv", i)], writes=["sq"])
            kb.mm(pfm[0][:], onesf[:], sq[:], reads=["onesf", "sq"], writes=[("pfm", 0)])
            kb.v("dve", "tensor_scalar", sq[:], pfm[0][:], EPS, None, ALU.add, reads=[("pfm", 0)], writes=["sq"])
            kb.act(sq[:], sq[:], AF.Sqrt, reads=["sq"], writes=["sq"])
            kb.v("dve", "reciprocal", sq[:], sq[:], reads=["sq"], writes=["sq"])
            kb.v("dve", "scalar_tensor_tensor", qkv[:, i, :], qkv[:, i, :], (128 ** -0.5) if i == 0 else 1.0, sq[:], ALU.mult, ALU.mult,
                 reads=[("qkv", i), "sq"], writes=[("qkv", i)])
        for j in range(4):
            m.proj_tm(ptm[:, 0:130], ("pfm", 1), j, 384, 130)
            silu_from(zs[:, j, :], "zs", ptm[:, 0:128], ("pfm", 1))
            kb.v("dve", "tensor_copy", beta[:, j:j + 1], ptm[:, 128:129], reads=[("pfm", 1)], writes=["beta"])
            kb.v("dve", "tensor_scalar", gg[:, j:j + 1], ptm[:, 129:130], dtb[:, 0:1], None, ALU.add, reads=[("pfm", 1), "dtb"], writes=["gg"])
        kb.act(beta[:], beta[:], AF.Exp, scale=-1.0, reads=["beta"], writes=["beta"])
        kb.v("dve", "tensor_scalar", beta[:], beta[:], 1.0, None, ALU.add, reads=["beta"], writes=["beta"])
        kb.v("dve", "reciprocal", beta[:], beta[:], reads=["beta"], writes=["beta"])
        kb.act(gg[:], gg[:], AF.Exp, reads=["gg"], writes=["gg"])
        kb.act(gg[:], gg[:], AF.Ln, bias=1.0, reads=["gg"], writes=["gg"])
        kb.v("dve", "tensor_scalar", gg[:], gg[:], na[:, 0:1], None, ALU.mult, reads=["gg", "na"], writes=["gg"])
        for j in range(4):
            sl = slice(j * 128, (j + 1) * 128)
            qn, kn, vf = qkv[:, 0, sl], qkv[:, 1, sl], qkv[:, 2, sl]
            trp(R(0, 0), kn, [("qkv", 1)]); trp(R(0, 1), vf, [("qkv", 2)])
            kb.v("dve", "tensor_copy", ktok[:], R(0, 0)[0], reads=[R(0, 0)[1]], writes=["ktok"])
            kb.act(vtok[:], R(0, 1)[0], AF.Copy, reads=[R(0, 1)[1]], writes=["vtok"])
            kb.v("dve", "tensor_scalar", gr[:], onesf[:], gg[:, j:j + 1], None, ALU.mult, reads=["onesf", "gg"], writes=["gr"])
            kb.v("dve", "tensor_scalar", br[:], onesf[:], beta[:, j:j + 1], None, ALU.mult, reads=["onesf", "beta"], writes=["br"])
            mm1(R(1, 0), gr[:], triblk[:], ["gr", "triblk"])
            mm1(R(1, 1), gr[:], blkones[:], ["gr", "blkones"])
            mm1(R(1, 2), triblk[:], gr[:], ["gr", "triblk"])
            mm1(R(1, 3), blkones[:], gr[:], ["gr", "blkones"])
            mm1(R(2, 0), br[:], idf[:], ["br", "idf"])
            kb.v("dve", "tensor_copy", dc[:, 0:1], pg[1][:, 256:257], reads=[("pg", 1)], writes=["dc"])
            kb.v("dve", "tensor_copy", dc[:, 1:2], pg[1][:, 384:385], reads=[("pg", 1)], writes=["dc"])
            kb.v("dve", "tensor_scalar", tmp[:], R(1, 0)[0], dc[:, 0:1], 0.0, ALU.subtract, ALU.min, reads=[("pg", 1), "dc"], writes=["tmp"])
            kb.act(ET[:], tmp[:], AF.Exp, reads=["tmp"], writes=["ET"])
            kb.v("dve", "tensor_tensor", ET[:], ET[:], triblk[:], ALU.mult, reads=["ET", "triblk"], writes=["ET"])
            kb.act(egcbc[:], R(1, 0)[0], AF.Exp, reads=[("pg", 1)], writes=["egcbc"])
            kb.act(eglbc[:], R(1, 1)[0], AF.Exp, reads=[("pg", 1)], writes=["eglbc"])
            kb.act(egc[:], dc[:, 0:1], AF.Exp, reads=["dc"], writes=["egc"])
            kb.v("dve", "tensor_tensor", ksc[:], dc[:, 1:2], dc[:, 0:1], ALU.subtract, reads=["dc"], writes=["ksc"])
            kb.act(ksc[:], ksc[:], AF.Exp, reads=["ksc"], writes=["ksc"])
            kb.v("dve", "tensor_tensor", bcol[:], egc[:], beta[:, j:j + 1], ALU.mult, reads=["egc", "beta"], writes=["bcol"])
            mm1(R(2, 1), kn, kn, [("qkv", 1)])
            kb.v("dve", "tensor_tensor", NTt[:], ET[:], R(2, 1)[0], ALU.mult, reads=["ET", ("pg", 2)], writes=["NTt"])
            kb.v("dve", "tensor_tensor", NTt[:], NTt[:], tristr[:], ALU.mult, reads=["NTt", "tristrict"], writes=["NTt"])
            kb.v("dve", "tensor_tensor", AT[:], NTt[:], R(2, 0)[0], ALU.mult, reads=["NTt", ("pg", 2)], writes=["AT"])
            trp(R(0, 2), AT[:], ["AT"])
            kb.v("dve", "tensor_copy", A[:], R(0, 2)[0], reads=[("pg", 0)], writes=["A"])
            kb.v("dve", "tensor_tensor", RT[:], idf[:], AT[:], ALU.subtract, reads=["idf", "AT"], writes=["RT"])
            P, PT, Pk, PTk = A, AT, "A", "AT"
            for lv in range(1, 6):
                Pn, PTn = Pa[lv % 2], PTa[lv % 2]
                Pnk, PTnk = ("Pa", lv % 2), ("PTa", lv % 2)
                mm1(R(3, 0), PT[:], P[:], [Pk, PTk])
                kb.v("dve", "tensor_copy", Pn[:], R(3, 0)[0], reads=[("pg", 3)], writes=[Pnk])
                if lv < 5:
                    mm1(R(3, 1), P[:], PT[:], [Pk, PTk])
                    kb.act(PTn[:], R(3, 1)[0], AF.Copy, reads=[("pg", 3)], writes=[PTnk])
                mm1(R(3, 2), Pn[:], RT[:], [Pnk, "RT"])
                kb.v("dve", "tensor_tensor", RT[:], RT[:], R(3, 2)[0], ALU.add, reads=["RT", ("pg", 3)], writes=["RT"])
                P, PT, Pk, PTk = Pn, PTn, Pnk, PTnk
            kb.v("dve", "tensor_scalar", vb[:], vtok[:], beta[:, j:j + 1], None, ALU.mult, reads=["vtok", "beta"], writes=["vb"])
            kb.v("dve", "tensor_scalar", kbt[:], ktok[:], bcol[:, 0:1], None, ALU.mult, reads=["ktok", "bcol"], writes=["kbt"])
            mm1(R(2, 3), RT[:], vb[:], ["RT", "vb"])
            kb.v("dve", "tensor_copy", u[:], R(2, 3)[0], reads=[("pg", 2)], writes=["u"])
            mm1(R(3, 3), kbt[:], RT[:], ["kbt", "RT"])
            kb.act(wT[:], R(3, 3)[0], AF.Copy, reads=[("pg", 3)], writes=["wT"])
            mm1(R(2, 2), kn, qn, [("qkv", 0), ("qkv", 1)])
            kb.v("dve", "tensor_tensor", attnT[:], ET[:], R(2, 2)[0], ALU.mult, reads=["ET", ("pg", 2)], writes=["attnT"])
            kb.v("dve", "tensor_tensor", qdT[:], qn, egcbc[:], ALU.mult, reads=[("qkv", 0), "egcbc"], writes=["qdT"])
            kb.v("dve", "tensor_scalar", kst[:], ktok[:], ksc[:, 0:1], None, ALU.mult, reads=["ktok", "ksc"], writes=["kst"])
            for c in range(2):
                cs = slice(c * 64, (c + 1) * 64)
                mm1(R(1, 0), wT[:], S[:], ["wT", "S"])
                kb.v("dve", "tensor_tensor", vnew[cs, :], u[cs, :], pg[1][cs, 0:128], ALU.subtract, reads=["u", ("pg", 1)], writes=["vnew"])
                kb.mm(R(1, 1)[0], qdT[:], S[:], start=True, stop=False, reads=["qdT", "S"], writes=[("pg", 1)])
                kb.mm(R(1, 1)[0], attnT[:], vnew[:], start=False, stop=True, reads=["attnT", "vnew"], writes=[("pg", 1)])
                kb.v("dve", "tensor_copy", osb[cs, :], pg[1][cs, 128:256], reads=[("pg", 1)], writes=["osb"])
                mm1(R(1, 2), kst[cs, :], vnew[cs, :], ["kst", "vnew"])
                kb.v("dve", "scalar_tensor_tensor", S[:], S[:], eglbc[:, c * 64:c * 64 + 1], R(1, 2)[0], ALU.mult, ALU.add,
                     reads=["S", "eglbc", ("pg", 1)], writes=["S"])
            kb.act(junk[:], osb[:], AF.Square, accum_out=ssq[:], reads=["osb"], writes=["junk", "ssq"])
            kb.v("dve", "tensor_scalar", rstd[:], ssq[:], 1.0 / 128, EPS, ALU.mult, ALU.add, reads=["ssq"], writes=["rstd"])
            kb.act(rstd[:], rstd[:], AF.Sqrt, reads=["rstd"], writes=["rstd"])
            kb.v("dve", "reciprocal", rstd[:], rstd[:], reads=["rstd"], writes=["rstd"])
            kb.v("dve", "scalar_tensor_tensor", osb[:], osb[:], rstd[:, 0:1], nrmw[:], ALU.mult, ALU.mult, reads=["osb", "rstd", "nrmw"], writes=["osb"])
            kb.v("dve", "tensor_tensor", osb[:], osb[:], zs[:, j, :], ALU.mult, reads=["osb", "zs"], writes=["osb"])
            kb.v("dve", "tensor_copy", otk[:, j, :], osb[:], reads=["osb"], writes=["otk"])
        kb.dma(o_d[t0:t0 + TBM, :].rearrange("(j p) d -> p j d", p=128), otk[:], reads=["otk"])
    return m.finish()


def chunk_consts():
    j = np.arange(128)
    same = (j[:, None] // 64) == (j[None, :] // 64)
    return {"triblk": (same & (j[:, None] <= j[None, :])).astype(np.float32), "blkones": same.astype(np.float32),
            "tristrict": (same & (j[:, None] < j[None, :])).astype(np.float32)}


def gdn_maps(inputs, xs, ident):
    f32 = np.float32
    W = np.asarray(inputs["ab_w_in"][0], f32)
    cwf = np.asarray(inputs["ab_conv_w"][0], f32)
    maps = []
    for c in range(8):
        b, g = c // 4, c % 4
        ar = np.arange(g * 128, (g + 1) * 128)
        cols = np.concatenate([1536 + ar, 2048 + ar, 2560 + ar, 3080 + ar, [3072 + g], [3076 + g]])
        convw = np.stack([cwf[:, ch].T for ch in (ar, 512 + ar, 1024 + ar)], 1).reshape(128, 12)
        one = lambda v: np.full((128, 1), v, f32)
        mm = {"x": np.ascontiguousarray(xs[b]), "w_in": np.ascontiguousarray(W[:, cols]),
              "norm_w": np.asarray(inputs["norm_mix"][0], f32)[None, :], "ident": ident,
              "convw": np.ascontiguousarray(convw), "dtb": one(inputs["ab_dt_bias"][0][g]), "alog": one(inputs["ab_a_log"][0][g]),
              "nrmw": np.ascontiguousarray(np.broadcast_to(np.asarray(inputs["ab_norm_w"][0], f32)[None, :], (128, 128)))}
        mm.update(chunk_consts())
        maps.append(mm)
    return maps


def kernel(**inputs):
    bf16 = ml_dtypes.bfloat16
    f32 = np.float32
    A = lambda k: np.asarray(inputs[k], f32)
    x = np.ascontiguousarray(A("x"))
    ident = np.eye(128, dtype=f32)
    cores = list(range(8))
    bc = lambda v: np.ascontiguousarray(np.broadcast_to(np.asarray(v, f32)[None, :], (128, len(v))))
    W0 = A("ab_w_in")[0]
    rb = A("ab_rel_bias")[0]
    maps = []
    for c in cores:
        b, g = c // 4, c % 4
        ar = np.arange(g * 128, (g + 1) * 128)
        cols = np.concatenate([ar, 512 + ar, 1024 + ar])
        maps.append({"x": x[b], "w_in": np.ascontiguousarray(W0[:, cols]), "norm_w": A("norm_mix")[0][None, :],
                     "ident": ident, "biasT": band_bias(rb[2 * g:2 * g + 2])})
    res = run_bass_kernel_spmd(build_band(), maps, core_ids=cores)
    omT = np.zeros((2, 1024, T), bf16)
    for c in cores:
        b, g = c // 4, c % 4
        omT[b, g * 128:(g + 1) * 128] = res.results[c]["oT"]
    res = run_bass_kernel_spmd(build_gdn(), gdn_maps(inputs, x, ident), core_ids=cores)
    for c in cores:
        b, g = c // 4, c % 4
        omT[b, 512 + g * 128:512 + (g + 1) * 128] = np.asarray(res.results[c]["o_tok"]).T
    def tok_maps(xin, omT, L, wout, final):
        xin = xin.reshape(16384, 1024)
        maps = []
        for c in cores:
            b, j = c // 4, c % 4
            mm = {"x": np.ascontiguousarray(xin[c * 2048:(c + 1) * 2048]),
                  "omT": np.ascontiguousarray(omT[b][:, j * 2048:(j + 1) * 2048]),
                  "w_out": np.asarray(wout, f32), "w_gate": A("ffn_w_gate")[L],
                  "w_up": A("ffn_w_up")[L], "w_down": A("ffn_w_down")[L],
                  "norm_ffn": A("norm_ffn")[L][None, :], "ident": ident}
            if final:
                mm["norm_final"] = A("norm_final")[None, :]
                mm["gnw"] = np.ascontiguousarray(A("cd_norm_w")[0].reshape(4, 128).T)
            maps.append(mm)
        return maps
    res = run_bass_kernel_spmd(build_tok(False), tok_maps(x, omT, 0, inputs["ab_w_out"][0], False), core_ids=cores)
    x1 = np.concatenate([res.results[c]["y"] for c in cores], 0).reshape(2, T, 1024)
    W1 = A("cd_w_in")[0]
    consts = fox_consts()
    maps = []
    for c in cores:
        b, g = c // 4, c % 4
        ar = np.arange(g * 128, (g + 1) * 128)
        cols = np.concatenate([1544 + ar, 2056 + ar, 2568 + ar, 3080 + np.arange(2 * g, 2 * g + 2)])
        mm = {"x": np.ascontiguousarray(x1[b]), "w_in": np.ascontiguousarray(W1[:, cols]),
              "norm_w": A("norm_mix")[1][None, :], "f_bias": bc(A("cd_f_bias")[0][2 * g:2 * g + 2])}
        mm.update(consts)
        maps.append(mm)
    res = run_bass_kernel_spmd(build_fox(), maps, core_ids=cores)
    omT = np.zeros((2, 1024, T), bf16)
    for c in cores:
        b, g = c // 4, c % 4
        omT[b, 512 + g * 128:512 + (g + 1) * 128] = res.results[c]["oT"]
    cwf = A("cd_conv_w")[0]; cbf = A("cd_conv_b")[0]
    cc = chunk_consts()
    maps = []
    for c in cores:
        b, g = c // 4, c % 4
        G = g // 2
        xch = np.arange(g * 128, (g + 1) * 128); Bch = 512 + G * 128 + np.arange(128); Cch = 768 + G * 128 + np.arange(128)
        cols = np.concatenate([512 + xch, 512 + Bch, 512 + Cch, xch, 1536 + np.arange(2 * g, 2 * g + 2)])
        convw = np.stack([cwf[:, ch].T for ch in (xch, Bch, Cch)], 1).reshape(128, 12)
        convb = np.stack([cbf[ch] for ch in (xch, Bch, Cch)], 1)
        mm = {"x": np.ascontiguousarray(x1[b]), "w_in": np.ascontiguousarray(W1[:, cols]), "norm_w": A("norm_mix")[1][None, :],
              "ident": ident, "convw": np.ascontiguousarray(convw), "convb": np.ascontiguousarray(convb),
              "dtb": bc(A("cd_dt_bias")[0][2 * g:2 * g + 2]), "alog": bc(A("cd_a_log")[0][2 * g:2 * g + 2]),
              "dskip": bc(A("cd_d_skip")[0][2 * g:2 * g + 2]), "triblk": cc["triblk"], "blkones": cc["blkones"]}
        maps.append(mm)
    res = run_bass_kernel_spmd(build_ssd(), maps, core_ids=cores)
    for c in cores:
        b, g = c // 4, c % 4
        omT[b, g * 128:(g + 1) * 128] = res.results[c]["oT"]
    res = run_bass_kernel_spmd(build_tok(True), tok_maps(x1, omT, 1, inputs["cd_w_out"][0], True), core_ids=cores)
    out = np.concatenate([res.results[c]["y"] for c in cores], 0).reshape(2, T, 1024)
    return out.astype(np.float32)
```
